# Optimizing a Trainium2 kernel written in Bass

```python
import jax
import jax.numpy as jnp
from jax import lax
import numpy as np

D_MODEL = 1024
BATCH = 4
SEQ = 8192
DEPTH = 2

GRID_W = 64
CTX_LEN = 256
HEAD_DIM = 64
ROPE_BASE = 10000.0
EPS = 1e-6
NEG_INF = -1e30
Q_BLOCK = 128

MLA_HEADS = 8
MLA_NOPE = 64
MLA_ROPE = 32
MLA_V = 64
MLA_Q_RANK = 256
MLA_KV_RANK = 128
MLA_IN = MLA_Q_RANK + MLA_KV_RANK + MLA_ROPE

NA_HEADS = 8
NA_KR_MAX = 8
NA_KC = 16
NA_CB = 16
NA_KB = 2 * NA_KC

SWA_HEADS = 16
SWA_KV_HEADS = 2
SWA_WINDOW = 128
SWA_BLOCK = 128

D_FF = (-(-8 * D_MODEL // 3) + 255) // 256 * 256

EVEN_IN = MLA_IN + 3 * NA_HEADS * HEAD_DIM
EVEN_OUT = MLA_HEADS * MLA_V + NA_HEADS * HEAD_DIM
ODD_IN = (SWA_HEADS + 2 * SWA_KV_HEADS) * HEAD_DIM
ODD_OUT = SWA_HEADS * HEAD_DIM
N_EVEN = (DEPTH + 1) // 2
N_ODD = DEPTH // 2

kernel_name = 'hybrid_mla_natten_swa_dit'


def rms_norm(x, gain=None):
    xf = x.astype(jnp.float32)
    y = xf * lax.rsqrt(jnp.mean(xf * xf, axis=-1, keepdims=True) + EPS)
    if gain is not None:
        y = y * gain.astype(jnp.float32)
    return y.astype(x.dtype)


def modulate(x, shift, scale):
    return rms_norm(x) * (1 + scale) + shift


def axial_rope(x, rows, cols):
    half = x.shape[-1] // 2
    inv = ROPE_BASE ** (-jnp.arange(0, half, 2, dtype=jnp.float32) / half)

    def rot(xa, pos):
        ang = pos.astype(jnp.float32)[:, None] * inv
        cos = jnp.cos(ang)[None, :, None, :]
        sin = jnp.sin(ang)[None, :, None, :]
        x1, x2 = jnp.split(xa.astype(jnp.float32), 2, axis=-1)
        return jnp.concatenate([x1 * cos - x2 * sin, x1 * sin + x2 * cos], axis=-1)

    xr, xc = jnp.split(x, 2, axis=-1)
    return jnp.concatenate([rot(xr, rows), rot(xc, cols)], axis=-1).astype(x.dtype)


def swiglu(h, w_gate_up, w_down):
    g, u = jnp.split(h @ w_gate_up, 2, axis=-1)
    return (jax.nn.silu(g) * u) @ w_down


def mla_heads(p_lat, p_ctx, rows, cols, q_norm, kv_norm, w_q_up, w_uk, w_uv, ctx_out):
    b, s = p_lat.shape[:2]
    n_ctx = p_ctx.shape[1]
    scale = (MLA_NOPE + MLA_ROPE) ** -0.5

    def queries(p, rotate):
        q = rms_norm(p[..., :MLA_Q_RANK], q_norm) @ w_q_up
        q = q.reshape(p.shape[0], p.shape[1], MLA_HEADS, MLA_NOPE + MLA_ROPE)
        q_nope, q_rope = q[..., :MLA_NOPE], q[..., MLA_NOPE:]
        if rotate:
            q_rope = axial_rope(q_rope, rows, cols)
        return jnp.einsum('bshn,hcn->bshc', q_nope, w_uk), q_rope

    def keys(p, rotate):
        c_kv = rms_norm(p[..., MLA_Q_RANK:MLA_Q_RANK + MLA_KV_RANK], kv_norm)
        k_rope = p[..., MLA_Q_RANK + MLA_KV_RANK:][:, :, None, :]
        if rotate:
            k_rope = axial_rope(k_rope, rows, cols)
        return c_kv, k_rope[:, :, 0]

    def attend(q_lat, q_rope, c_kv, k_rope):
        sc = jnp.einsum('bqhc,bkc->bhqk', q_lat, c_kv) + jnp.einsum('bqhr,bkr->bhqk', q_rope, k_rope)
        p = jax.nn.softmax(sc.astype(jnp.float32) * scale, axis=-1).astype(c_kv.dtype)
        return jnp.einsum('bhqk,bkc->bqhc', p, c_kv)

    ckv_c, kr_c = keys(p_ctx, False)
    ckv_l, kr_l = keys(p_lat, True)
    ckv_all = jnp.concatenate([ckv_c, ckv_l], axis=1)
    kr_all = jnp.concatenate([kr_c, kr_l], axis=1)
    ql, qr = queries(p_lat, True)
    nb = s // Q_BLOCK

    def to_blocks(t):
        return t.reshape(b, nb, Q_BLOCK, *t.shape[2:]).swapaxes(0, 1)

    o = lax.map(lambda a: attend(a[0], a[1], ckv_all, kr_all), (to_blocks(ql), to_blocks(qr)))
    o = o.swapaxes(0, 1).reshape(b, s, MLA_HEADS, MLA_KV_RANK)
    o_lat = jnp.einsum('bshc,hcv->bshv', o, w_uv).reshape(b, s, MLA_HEADS * MLA_V)
    o_ctx = None
    if ctx_out:
        qlc, qrc = queries(p_ctx, False)
        oc = attend(qlc, qrc, ckv_c, kr_c)
        o_ctx = jnp.einsum('bshc,hcv->bshv', oc, w_uv).reshape(b, n_ctx, MLA_HEADS * MLA_V)
    return o_lat, o_ctx


def na_heads(q, k, v, qc, kc, vc, rel_bias, ctx_out):
    b, s, h, d = q.shape
    n_ctx = kc.shape[1]
    n_rows = s // GRID_W
    kr = min(NA_KR_MAX, n_rows)
    n_cb = GRID_W // NA_CB
    scale = d ** -0.5
    qcol = np.arange(GRID_W).reshape(n_cb, NA_CB)
    q_start = np.clip(qcol - NA_KC // 2, 0, GRID_W - NA_KC)
    kb_start = np.clip(np.arange(n_cb) * NA_CB - NA_KC // 2, 0, GRID_W - NA_KB)
    kcol = kb_start[:, None] + np.arange(NA_KB)
    col_ok = (kcol[:, None, :] >= q_start[..., None]) & (kcol[:, None, :] < q_start[..., None] + NA_KC)
    dc_idx = np.clip(kcol[:, None, :] - qcol[..., None] + NA_KC - 1, 0, 2 * NA_KC - 2)
    col_bias = rel_bias[:, :, dc_idx].astype(jnp.float32)
    kg = k.reshape(b, n_rows, GRID_W, h, d)
    vg = v.reshape(b, n_rows, GRID_W, h, d)
    q_rows = q.reshape(b, n_rows, n_cb, NA_CB, h, d).swapaxes(0, 1)

    def row(args):
        r, qr = args
        rs = jnp.clip(r - kr // 2, 0, n_rows - kr)
        k_blk = lax.dynamic_slice_in_dim(kg, rs, kr, axis=1)[:, :, kcol]
        v_blk = lax.dynamic_slice_in_dim(vg, rs, kr, axis=1)[:, :, kcol]
        s_lat = jnp.einsum('bnqhd,bknmhd->bhnqkm', qr, k_blk).astype(jnp.float32) * scale
        dr_idx = rs + jnp.arange(kr) - r + NA_KR_MAX - 1
        s_lat = s_lat + col_bias[:, dr_idx].transpose(0, 2, 3, 1, 4)[None]
        s_lat = jnp.where(col_ok[:, :, None, :], s_lat, NEG_INF).reshape(b, h, n_cb, NA_CB, kr * NA_KB)
        s_ctx = jnp.einsum('bnqhd,bchd->bhnqc', qr, kc).astype(jnp.float32) * scale
        p = jax.nn.softmax(jnp.concatenate([s_ctx, s_lat], axis=-1), axis=-1).astype(v.dtype)
        p_lat = p[..., n_ctx:].reshape(b, h, n_cb, NA_CB, kr, NA_KB)
        return (jnp.einsum('bhnqc,bchd->bnqhd', p[..., :n_ctx], vc)
                + jnp.einsum('bhnqkm,bknmhd->bnqhd', p_lat, v_blk))

    o = lax.map(row, (jnp.arange(n_rows), q_rows))
    o_lat = o.swapaxes(0, 1).reshape(b, s, h * d)
    o_ctx = None
    if ctx_out:
        sc = jnp.einsum('bqhd,bkhd->bhqk', qc, kc).astype(jnp.float32) * scale
        pc = jax.nn.softmax(sc, axis=-1).astype(vc.dtype)
        o_ctx = jnp.einsum('bhqk,bkhd->bqhd', pc, vc).reshape(b, n_ctx, h * d)
    return o_lat, o_ctx


def swa_heads(q, k, v, qc, kc, vc, sinks, ctx_out):
    b, s, h, d = q.shape
    kvh = k.shape[2]
    g = h // kvh
    n_ctx = kc.shape[1]
    scale = d ** -0.5
    nb = s // SWA_BLOCK
    n_side = -(-SWA_WINDOW // SWA_BLOCK)
    pad = n_side * SWA_BLOCK
    kw = (2 * n_side + 1) * SWA_BLOCK

    def windows(t):
        tp = jnp.pad(t, ((0, 0), (pad, pad), (0, 0), (0, 0))).reshape(b, nb + 2 * n_side, SWA_BLOCK, kvh, d)
        return jnp.concatenate([tp[:, j:j + nb] for j in range(2 * n_side + 1)], axis=2).swapaxes(0, 1)

    q_pos = jnp.arange(s).reshape(nb, SWA_BLOCK)
    k_pos = (jnp.arange(nb)[:, None] - n_side) * SWA_BLOCK + jnp.arange(kw)[None, :]
    sink_col = sinks.astype(jnp.float32).reshape(kvh, g, 1, 1)

    def attend(qb, kb, vb, band_ok):
        tq = qb.shape[1]
        qg = qb.reshape(b, tq, kvh, g, d)
        s_ctx = jnp.einsum('bqkgd,bckd->bkgqc', qg, kc).astype(jnp.float32) * scale
        parts = [jnp.broadcast_to(sink_col, (b, kvh, g, tq, 1)), s_ctx]
        if kb is not None:
            s_lat = jnp.einsum('bqkgd,bmkd->bkgqm', qg, kb).astype(jnp.float32) * scale
            parts.append(jnp.where(band_ok, s_lat, NEG_INF))
        p = jax.nn.softmax(jnp.concatenate(parts, axis=-1), axis=-1).astype(vc.dtype)
        o = jnp.einsum('bkgqc,bckd->bqkgd', p[..., 1:1 + n_ctx], vc)
        if kb is not None:
            o = o + jnp.einsum('bkgqm,bmkd->bqkgd', p[..., 1 + n_ctx:], vb)
        return o.reshape(b, tq, h * d)

    def block(args):
        qb, kb, vb, qp, kp = args
        ok = (jnp.abs(kp[None, :] - qp[:, None]) <= SWA_WINDOW) & (kp >= 0)[None, :] & (kp < s)[None, :]
        return attend(qb, kb, vb, ok)

    q_blocks = q.reshape(b, nb, SWA_BLOCK, h, d).swapaxes(0, 1)
    o = lax.map(block, (q_blocks, windows(k), windows(v), q_pos, k_pos))
    o_lat = o.swapaxes(0, 1).reshape(b, s, h * d)
    o_ctx = attend(qc, None, None, None) if ctx_out else None
    return o_lat, o_ctx


def even_mixer(a_lat, a_ctx, rows, cols, w_in, q_norm, kv_norm, w_q_up, w_uk, w_uv, rel_bias, w_out, ctx_out):
    b, s, _ = a_lat.shape
    n_ctx = a_ctx.shape[1]
    p_lat = a_lat @ w_in
    p_ctx = a_ctx @ w_in
    o_mla_lat, o_mla_ctx = mla_heads(p_lat[..., :MLA_IN], p_ctx[..., :MLA_IN], rows, cols,
                                     q_norm, kv_norm, w_q_up, w_uk, w_uv, ctx_out)
    na_l = p_lat[..., MLA_IN:].reshape(b, s, 3, NA_HEADS, HEAD_DIM)
    na_c = p_ctx[..., MLA_IN:].reshape(b, n_ctx, 3, NA_HEADS, HEAD_DIM)
    o_na_lat, o_na_ctx = na_heads(na_l[:, :, 0], na_l[:, :, 1], na_l[:, :, 2],
                                  na_c[:, :, 0], na_c[:, :, 1], na_c[:, :, 2], rel_bias, ctx_out)
    o_lat = jnp.concatenate([o_mla_lat, o_na_lat], axis=-1) @ w_out
    o_ctx = jnp.concatenate([o_mla_ctx, o_na_ctx], axis=-1) @ w_out if ctx_out else None
    return o_lat, o_ctx


def odd_mixer(a_lat, a_ctx, rows, cols, w_in, sinks, w_out, ctx_out):
    b, s, _ = a_lat.shape
    n_ctx = a_ctx.shape[1]

    def split(p, t):
        q = p[..., :SWA_HEADS * HEAD_DIM].reshape(b, t, SWA_HEADS, HEAD_DIM)
        kv = p[..., SWA_HEADS * HEAD_DIM:].reshape(b, t, 2, SWA_KV_HEADS, HEAD_DIM)
        return q, kv[:, :, 0], kv[:, :, 1]

    q, k, v = split(a_lat @ w_in, s)
    q = axial_rope(q, rows, cols)
    k = axial_rope(k, rows, cols)
    qc, kc, vc = split(a_ctx @ w_in, n_ctx)
    o_lat, o_ctx = swa_heads(q, k, v, qc, kc, vc, sinks, ctx_out)
    return o_lat @ w_out, (o_ctx @ w_out if ctx_out else None)


def setup_inputs(seed: int = 0) -> dict:
    key = jax.random.key(seed)
    ks = jax.random.split(key, 20)
    nrm = jax.random.normal
    f32 = jnp.float32
    return {
        'x': nrm(ks[0], (BATCH, SEQ, D_MODEL), f32),
        'c': nrm(ks[1], (BATCH, D_MODEL), f32),
        'ctx': nrm(ks[2], (BATCH, CTX_LEN, D_MODEL), f32),
        'c_ctx': nrm(ks[3], (D_MODEL,), f32),
        'mod_w': nrm(ks[4], (DEPTH, D_MODEL, 6 * D_MODEL), f32) * (0.5 * D_MODEL ** -0.5),
        'mod_b': nrm(ks[5], (DEPTH, 6 * D_MODEL), f32) * 0.02,
        'even_w_in': nrm(ks[6], (N_EVEN, D_MODEL, EVEN_IN), f32) * D_MODEL ** -0.5,
        'mla_q_norm': 1.0 + 0.02 * nrm(ks[7], (N_EVEN, MLA_Q_RANK), f32),
        'mla_kv_norm': 1.0 + 0.02 * nrm(ks[8], (N_EVEN, MLA_KV_RANK), f32),
        'mla_w_q_up': nrm(ks[9], (N_EVEN, MLA_Q_RANK, MLA_HEADS * (MLA_NOPE + MLA_ROPE)), f32) * MLA_Q_RANK ** -0.5,
        'mla_w_uk': nrm(ks[10], (N_EVEN, MLA_HEADS, MLA_KV_RANK, MLA_NOPE), f32) * MLA_NOPE ** -0.5,
        'mla_w_uv': nrm(ks[11], (N_EVEN, MLA_HEADS, MLA_KV_RANK, MLA_V), f32) * MLA_KV_RANK ** -0.5,
        'na_rel_bias': 0.2 * nrm(ks[12], (N_EVEN, NA_HEADS, 2 * NA_KR_MAX - 1, 2 * NA_KC - 1), f32),
        'even_w_out': nrm(ks[13], (N_EVEN, EVEN_OUT, D_MODEL), f32) * EVEN_OUT ** -0.5,
        'odd_w_in': nrm(ks[14], (N_ODD, D_MODEL, ODD_IN), f32) * D_MODEL ** -0.5,
        'swa_sinks': nrm(ks[15], (N_ODD, SWA_HEADS), f32),
        'odd_w_out': nrm(ks[16], (N_ODD, ODD_OUT, D_MODEL), f32) * ODD_OUT ** -0.5,
        'ffn_w_gate_up': nrm(ks[17], (DEPTH, D_MODEL, 2 * D_FF), f32) * D_MODEL ** -0.5,
        'ffn_w_down': nrm(ks[18], (DEPTH, D_FF, D_MODEL), f32) * D_FF ** -0.5,
        'final_norm': 1.0 + 0.02 * nrm(ks[19], (D_MODEL,), f32),
    }


def reference(x, c, ctx, c_ctx, mod_w, mod_b, even_w_in, mla_q_norm, mla_kv_norm, mla_w_q_up, mla_w_uk,
              mla_w_uv, na_rel_bias, even_w_out, odd_w_in, swa_sinks, odd_w_out, ffn_w_gate_up, ffn_w_down,
              final_norm):
    s = x.shape[1]
    t = jnp.arange(s, dtype=jnp.int32)
    rows, cols = t // GRID_W, t % GRID_W
    h_lat, h_ctx = x, ctx
    silu_c = jax.nn.silu(c)[:, None, :]
    silu_cc = jax.nn.silu(c_ctx)[None, None, :]
    for l in range(DEPTH):
        ctx_out = l < DEPTH - 1
        sh_a, sc_a, g_a, sh_f, sc_f, g_f = jnp.split(silu_c @ mod_w[l] + mod_b[l], 6, axis=-1)
        csh_a, csc_a, cg_a, csh_f, csc_f, cg_f = jnp.split(silu_cc @ mod_w[l] + mod_b[l], 6, axis=-1)
        a_lat = modulate(h_lat, sh_a, sc_a)
        a_ctx = modulate(h_ctx, csh_a, csc_a)
        if l % 2 == 0:
            e = l // 2
            o_lat, o_ctx = even_mixer(a_lat, a_ctx, rows, cols, even_w_in[e], mla_q_norm[e], mla_kv_norm[e],
                                      mla_w_q_up[e], mla_w_uk[e], mla_w_uv[e], na_rel_bias[e], even_w_out[e],
                                      ctx_out)
        else:
            o = l // 2
            o_lat, o_ctx = odd_mixer(a_lat, a_ctx, rows, cols, odd_w_in[o], swa_sinks[o], odd_w_out[o], ctx_out)
        h_lat = h_lat + g_a * o_lat
        h_lat = h_lat + g_f * swiglu(modulate(h_lat, sh_f, sc_f), ffn_w_gate_up[l], ffn_w_down[l])
        if ctx_out:
            h_ctx = h_ctx + cg_a * o_ctx
            h_ctx = h_ctx + cg_f * swiglu(modulate(h_ctx, csh_f, csc_f), ffn_w_gate_up[l], ffn_w_down[l])
    return rms_norm(h_lat, final_norm)
```

```python
from contextlib import ExitStack
import numpy as np
import concourse.bass as bass
import concourse.mybir as mybir
from concourse.bass_utils import run_bass_kernel_spmd

F32 = mybir.dt.float32
BF16 = mybir.dt.bfloat16
AF = mybir.ActivationFunctionType
ALU = mybir.AluOpType

D = 1024
S_FULL = 8192
CTX = 256
T = 512
NT_MAIN = 9
NT_NAKV = 9
OWN = 4096
EPS = 1e-6
DFF = 2816
NEG = -1e30

ENGS = ("pe", "act", "dve", "pool", "sp")


class Sched:
    def __init__(self, nc):
        self.nc = nc
        self.streams = {e: [] for e in ENGS}
        self.writer = {}
        self.readers = {}
        self.known = {e: {} for e in ENGS}
        self.dma_count = {}
        self.sig = {e: set() for e in ENGS}
        self.snap = {}
        self.out_slots = set()

    def _knows(self, eng, ev):
        k = self.known[eng]
        if ev[0] == 'c':
            return k.get(ev[1], -1) >= ev[2]
        return k.get(('d', ev[1]), 0) >= ev[2]

    def _learn(self, eng, ev):
        k = self.known[eng]
        key = ev[1] if ev[0] == 'c' else ('d', ev[1])
        if k.get(key, -1) < ev[2]:
            k[key] = ev[2]
        s = self.snap.get(ev)
        if s:
            for kk, vv in s.items():
                if k.get(kk, -1) < vv:
                    k[kk] = vv

    def _deps(self, eng, reads, writes):
        evs = []
        for key in reads:
            w = self.writer.get(key)
            if w is not None:
                evs.append(w)
        for key in writes:
            w = self.writer.get(key)
            if w is not None:
                evs.append(w)
            evs.extend(self.readers.get(key, ()))
        waits = []
        for ev in evs:
            if ev[0] == 'c' and ev[1] == eng and eng == 'pe':
                continue
            if self._knows(eng, ev):
                continue
            waits.append(ev)
            if ev[0] == 'c':
                self.sig[ev[1]].add(ev[2])
            self._learn(eng, ev)
        return waits

    def _record(self, ev, reads, writes):
        for key in reads:
            self.readers.setdefault(key, []).append(ev)
        for key in writes:
            self.writer[key] = ev
            self.readers[key] = []

    def op(self, eng, fn, reads=(), writes=()):
        waits = self._deps(eng, reads, writes)
        idx = len(self.streams[eng])
        ev = ('c', eng, idx)
        self.snap[ev] = {k: v for k, v in self.known[eng].items() if not isinstance(k, tuple)}
        self.streams[eng].append(('op', fn, waits, idx))
        self._record(ev, reads, writes)
        return ev

    def dma(self, eng, fn, slot, reads=(), writes=(), out=False):
        waits = self._deps(eng, reads, writes)
        n = self.dma_count.get(slot, 0) + 1
        self.dma_count[slot] = n
        ev = ('d', slot, n)
        self.snap[ev] = {k: v for k, v in self.known[eng].items() if not isinstance(k, tuple)}
        self.streams[eng].append(('dma', fn, waits, slot))
        self._record(ev, reads, writes)
        if out:
            self.out_slots.add(slot)
        return ev

    def emit(self, stack):
        nc = self.nc
        csem = {e: stack.enter_context(nc.semaphore("c_" + e)) for e in ENGS if e != "sp"}
        dsem = {s: stack.enter_context(nc.semaphore("d_%s" % (s,))) for s in self.dma_count}
        rank = {}
        for e in ENGS:
            srt = sorted(self.sig[e])
            rank[e] = {i: r + 1 for r, i in enumerate(srt)}
        block = stack.enter_context(nc.Block())

        def run(ename, engobj):
            for kind, fn, waits, x in self.streams[ename]:
                for ev in waits:
                    if ev[0] == 'c':
                        engobj.wait_ge(csem[ev[1]], rank[ev[1]][ev[2]])
                    else:
                        engobj.wait_ge(dsem[ev[1]], 16 * ev[2])
                ins = fn(engobj)
                if kind == 'dma':
                    ins.then_inc(dsem[x], 16)
                elif x in rank[ename]:
                    ins.then_inc(csem[ename], 1)
            if ename == "sp":
                for s in sorted(self.out_slots, key=str):
                    engobj.wait_ge(dsem[s], 16 * self.dma_count[s])

        @block.tensor
        def _(e):
            run("pe", e)

        @block.scalar
        def _(e):
            run("act", e)

        @block.vector
        def _(e):
            run("dve", e)

        @block.gpsimd
        def _(e):
            run("pool", e)

        @block.sync
        def _(e):
            run("sp", e)


def _rope_tables(pos_r, pos_c, dim):
    half = dim // 2
    inv = (10000.0 ** (-np.arange(0, half, 2, dtype=np.float32) / half)).astype(np.float32)
    nf = half // 2
    C = np.zeros((dim, pos_r.shape[0]), np.float32)
    Sg = np.zeros((dim, pos_r.shape[0]), np.float32)
    for f in range(dim):
        pos = pos_r if f < half else pos_c
        i = f % half
        ang = pos.astype(np.float32) * inv[i % nf]
        C[f] = np.cos(ang)
        Sg[f] = -np.sin(ang) if i < nf else np.sin(ang)
    return C, Sg


def _swap_idx(dim):
    half = dim // 2
    nf = half // 2
    idx = np.zeros(dim, np.int64)
    for f in range(dim):
        base = 0 if f < half else half
        i = f % half
        idx[f] = base + (i + nf if i < nf else i - nf)
    return idx


def _na_tables(rel_bias, rev):
    kc = np.arange(64)[:, None]
    qc = np.arange(64)[None, :]
    if rev:
        kt, qt = 63 - kc, 63 - qc
    else:
        kt, qt = kc, qc
    q_start = np.clip(qt - 8, 0, 48)
    col_ok = (kt >= q_start) & (kt < q_start + 16)
    dc_idx = np.clip(kt - qt + 15, 0, 30)
    bias = np.zeros((8, 2, 64, 15, 64), np.float32)
    mask = np.zeros((2, 64, 15, 64), np.float32)
    for e in range(15):
        dr_l = 7 - e
        dr_t = -dr_l if rev else dr_l
        g = rel_bias[:, dr_t + 7][:, dc_idx]
        bias[:, 0, :, e, :] = g
        bias[:, 1, :, e, :] = g
        row_ok = (-4 <= dr_t <= 3)
        mask[0, :, e, :] = np.where(col_ok & row_ok, 0.0, NEG)
        mask[1, :, e, :] = np.where(col_ok, 0.0, NEG)
    return bias.reshape(8, 2, 64, 960), mask.reshape(2, 64, 960)


def _prep_shared(inp):
    f = np.float32
    w = {}
    wi = inp["even_w_in"][0]
    kr = wi[:, 384:416]
    sw32 = _swap_idx(32)
    w["w0kv"] = np.concatenate([wi[:, 256:384], kr, kr[:, sw32]], 1)
    w["w0nak"] = wi[:, 928:1440]
    w["w0nav"] = wi[:, 1440:1952]
    w["w0q"] = np.concatenate([wi[:, 0:256], wi[:, 416:928]], 1)
    qu = inp["mla_w_q_up"][0].reshape(256, 8, 96)
    nope = qu[:, :, :64].reshape(256, 512)
    rope = qu[:, :, 64:]
    rsw = rope[:, :, sw32]
    nope3 = qu[:, :, :64]
    parts = []
    for h in range(8):
        parts += [nope3[:, h], rope[:, h], nope3[:, h], rsw[:, h]]
    w["wqup"] = np.concatenate(parts, 1)
    w["wukP"] = inp["mla_w_uk"][0].transpose(1, 0, 2).reshape(128, 512)
    w["wuv"] = inp["mla_w_uv"][0].transpose(1, 0, 2).reshape(128, 512)
    w["wout0"] = inp["even_w_out"][0]
    for l in range(2):
        w["wgu%d" % l] = inp["ffn_w_gate_up"][l]
        w["wdn%d" % l] = inp["ffn_w_down"][l]
    wo = inp["odd_w_in"][0]
    sw64 = _swap_idx(64)
    q = wo[:, :1024].reshape(1024, 16, 64)
    qs = q[:, :, sw64]
    w["w1q"] = np.concatenate([np.concatenate([q[:, 2 * c:2 * c + 2].reshape(1024, 128),
                                               qs[:, 2 * c:2 * c + 2].reshape(1024, 128)], 1) for c in range(8)], 1)
    k = wo[:, 1024:1152].reshape(1024, 2, 64)
    ksw = k[:, :, sw64]
    w["w1k"] = np.concatenate([k[:, 0], k[:, 0], ksw[:, 0], ksw[:, 0], k[:, 1], k[:, 1], ksw[:, 1], ksw[:, 1]], 1)
    v = wo[:, 1152:1280].reshape(1024, 2, 64)
    w["w1v"] = np.concatenate([v[:, 0], v[:, 1]], 1)
    w["wout1"] = inp["odd_w_out"][0]
    w = {k_: np.ascontiguousarray(v_, dtype=f) for k_, v_ in w.items()}
    sh = {}
    sh["modw"] = np.ascontiguousarray(inp["mod_w"], dtype=f)
    mb = inp["mod_b"].reshape(2, 48, 128).transpose(2, 0, 1)
    sh["modb"] = np.ascontiguousarray(np.repeat(mb[:, :, :, None], 2, axis=3), dtype=f)
    sh["qnorm"] = np.ascontiguousarray(inp["mla_q_norm"][0].reshape(2, 128).T, dtype=f)
    sh["kvnorm"] = np.ascontiguousarray(inp["mla_kv_norm"][0].reshape(1, 128).T, dtype=f)
    sh["fnorm"] = np.ascontiguousarray(inp["final_norm"].reshape(8, 128).T, dtype=f)
    sh["sinks"] = np.ascontiguousarray(np.repeat(inp["swa_sinks"][0][None, :], 128, axis=0), dtype=f)
    kk = np.arange(128)[:, None]
    qq = np.arange(128)[None, :]
    mlo = np.where(qq <= kk, 0.0, NEG).astype(f)
    mhi = np.where(kk <= qq, 0.0, NEG).astype(f)
    sh["swamask"] = np.ascontiguousarray(np.stack([np.tile(mlo, (1, 4)), np.tile(mhi, (1, 4))]), dtype=f)
    return w, sh


WSHAPES = {"w0kv": (1024, 192), "w0nak": (1024, 512), "w0nav": (1024, 512), "w0q": (1024, 768),
           "wqup": (256, 1536), "wukP": (128, 512), "wuv": (128, 512), "wout0": (1024, 1024),
           "wgu0": (1024, 5632), "wdn0": (2816, 1024), "w1q": (1024, 2048), "w1k": (1024, 512),
           "w1v": (1024, 128), "wout1": (1024, 1024), "wgu1": (1024, 5632), "wdn1": (2816, 1024)}
WORDER = ["w0kv", "w0nak", "w0nav", "w0q", "wqup", "wukP", "wuv", "wout0", "wgu0", "wdn0",
          "w1k", "w1v", "w1q", "wout1", "wgu1", "wdn1"]


def _prep_core(inp, cid, sh):
    f = np.float32
    b, half = cid // 2, cid % 2
    perm = np.arange(S_FULL) if half == 0 else np.arange(S_FULL)[::-1]
    m = {}
    m["xl"] = np.ascontiguousarray(inp["x"][b][perm], dtype=f)
    m["ctxl"] = np.ascontiguousarray(inp["ctx"][b], dtype=f)
    cv = np.stack([inp["c"][b].reshape(8, 128).T, inp["c_ctx"].reshape(8, 128).T], axis=2)
    m["cvec"] = np.ascontiguousarray(cv, dtype=f)
    rows, cols = perm // 64, perm % 64
    C, Sg = _rope_tables(rows, cols, 32)
    m["ropeMc"] = np.ascontiguousarray(np.tile(C, (3, 1)), dtype=f)
    m["ropeMs"] = np.ascontiguousarray(np.tile(Sg, (3, 1)), dtype=f)
    C, Sg = _rope_tables(rows[:5120], cols[:5120], 64)
    m["ropeSc"] = np.ascontiguousarray(np.tile(C, (2, 1)), dtype=f)
    m["ropeSs"] = np.ascontiguousarray(np.tile(Sg, (2, 1)), dtype=f)
    nb, nm = _na_tables(inp["na_rel_bias"][0], half == 1)
    m["nabias"] = np.ascontiguousarray(nb, dtype=f)
    m["namask"] = np.ascontiguousarray(nm, dtype=f)
    return m


def build_program(n_main=NT_MAIN, dbg=None, stage=9):
    nc = bass.Bass("TRN2", target_bir_lowering=False)
    S = Sched(nc)
    st = ExitStack()

    def din(name, shape, dt=F32):
        return nc.dram_tensor(name, list(shape), dt, kind="ExternalInput").ap()

    xl = din("xl", [S_FULL, D])
    ctxl = din("ctxl", [CTX, D])
    cvec = din("cvec", [128, 8, 2])
    modw = din("modw", [2, D, 6 * D])
    modb = din("modb", [128, 2, 48, 2])
    qnorm_d = din("qnorm", [128, 2])
    kvnorm_d = din("kvnorm", [128, 1])
    fnorm_d = din("fnorm", [128, 8])
    sinks_d = din("sinks", [128, 16])
    swamask_d = din("swamask", [2, 128, 512])
    ropeMc = din("ropeMc", [96, S_FULL])
    ropeMs = din("ropeMs", [96, S_FULL])
    ropeSc = din("ropeSc", [128, 5120])
    ropeSs = din("ropeSs", [128, 5120])
    nabias = din("nabias", [8, 2, 64, 960])
    namask = din("namask", [2, 64, 960])
    wf = {k: din(k, WSHAPES[k]) for k in WORDER}
    y = nc.dram_tensor("y", [OWN, D], F32, kind="ExternalOutput").ap()
    dbg_out = {}
    if dbg:
        for k, shp in dbg.items():
            dbg_out[k] = nc.dram_tensor("dbg_" + k, list(shp), F32, kind="ExternalOutput").ap()

    wb = {k: nc.dram_tensor("wb_" + k, list(WSHAPES[k]), BF16).ap() for k in WORDER}
    NK = S_FULL + CTX
    khp_d = nc.dram_tensor("khp_d", [4, 128, NK], BF16).ap()
    kr_d = nc.dram_tensor("kr_d", [32, NK], BF16).ap()
    vh_d = nc.dram_tensor("vh_d", [NK, 768], BF16).ap()
    NAT = NT_NAKV * T
    nakT_d = nc.dram_tensor("nakT_d", [128, 4, NAT], BF16).ap()
    nav_d = nc.dram_tensor("nav_d", [NAT, 768], BF16).ap()
    nastr_d = nc.dram_tensor("nastr_d", [8, 2, 64, 960], BF16).ap()

    def sb(name, shape, dt):
        return st.enter_context(nc.sbuf_tensor(name, list(shape), dt))

    ident_f = sb("ident_f", [128, 128], F32)
    ident_b = sb("ident_b", [128, 128], BF16)
    ones_b = sb("ones_b", [128, 128], BF16)
    modv = sb("modv", [128, 2 * 6 * 8 * 2], F32)
    onep = sb("onep", [128, 2 * 6 * 8 * 2], F32)
    silc = sb("silc", [128, 16], F32)
    qnorm = sb("qnorm_s", [128, 2], F32)
    kvnorm = sb("kvnorm_s", [128, 1], F32)
    fnorm = sb("fnorm_s", [128, 8], F32)
    esink = sb("esink", [128, 16], F32)
    swam = sb("swam", [128, 2, 512], BF16)
    hbuf = [sb("hbuf%d" % i, [128, 8, T], F32) for i in range(2)]
    hctx = hbuf[1]
    xstage = [sb("xstage%d" % i, [128, D], F32) for i in range(2)]
    ostage = xstage
    abuf = [sb("abuf%d" % i, [128, 8, T], BF16) for i in range(2)]
    NWS = 3
    wslot = [sb("wslot%d" % i, [128, 4096], BF16) for i in range(NWS)]
    sqb = sb("sqb", [128, 2, T], BF16)
    tmpf = sb("tmpf", [128, 2, T], F32)
    rstd = sb("rstd", [128, T], F32)
    cq = tmpf
    cqn = sb("cqn", [128, 2, T], BF16)
    uni = sb("uni", [128, 8 * T], BF16)
    qnope = uni[:, 0:4 * T].rearrange("p (k t) -> p k t", k=4)
    Qh = sb("Qh", [96, 8, T], BF16)
    pswap = sb("pswap", [128, 128], F32)
    vstage = sb("vstage", [128, 768], BF16)
    recb = sb("recb", [128, 2, T], F32)
    ckvf = recb[:, 0, :]
    naq = uni[:, 4 * T:8 * T].rearrange("p (k t) -> p k t", k=4)
    nakw = sb("nakw", [128, 2, 1024], BF16)
    navw = sb("navw", [64, 2, 16 * 192], BF16)
    nastrip = sb("nastrip", [128, 2, 960], BF16)
    naedge = sb("naedge", [128, 1, 960], BF16)
    nakc = sb("nakc", [128, 4, CTX], BF16)
    navc = sb("navc", [128, 2, 768], BF16)
    actT = sb("actT", [128, 22, T], BF16)
    silt = sb("silt", [128, 2, T], F32)
    q1 = uni[:, :].rearrange("p (k t) -> p k t", k=8)
    k1 = sb("k1", [128, 2, 3 * T], BF16)
    v1 = sb("v1", [128, 12, 384], BF16)
    k1c = sb("k1c", [128, 2, CTX], BF16)
    v1c = sb("v1c", [128, 2, 384], BF16)
    pT = [sb("pT%d" % i, [128, T], BF16) for i in range(3)]
    ropeC = sb("ropeC", [128, T], F32)
    ropeS = sb("ropeS", [128, 2, T], F32)
    mck = [sb("mck%d" % i, [96, 1024], BF16) for i in range(2)]
    mv = [sb("mv%d" % i, [128, 8, 192], BF16) for i in range(2)]

    psb = [st.enter_context(nc.psum_tensor("psb%d" % i, [128, 512], F32)) for i in range(8)]
    print("sbuf bytes remaining/partition:", nc.sbuf_bytes_remaining)

    cnt = {"mm": 0, "ws": 0, "pt": 0, "uid": 0}

    def mm_bank():
        b = cnt["mm"] % 4
        cnt["mm"] += 1
        return b

    def pkey(b):
        return "ps%d" % b

    def next_pt():
        i = cnt["pt"] % 3
        cnt["pt"] += 1
        return i

    def mvi(l, w, kc, col):
        i = ((l * 6 + w) * 8 + kc) * 2 + col
        return i

    S.op("pool", lambda e: e.memset(ident_f[:], 1.0), writes=["ident_f"])
    S.op("pool", lambda e: e.affine_select(out=ident_f[:], in_=ident_f[:], pattern=[[-1, 128]],
                                           compare_op=ALU.is_equal, fill=0.0, base=0, channel_multiplier=1),
         reads=["ident_f"], writes=["ident_f"])
    S.op("dve", lambda e: e.tensor_copy(out=ident_b[:], in_=ident_f[:]), reads=["ident_f"], writes=["ident_b"])
    S.op("pool", lambda e: e.memset(ones_b[:], 1.0), writes=["ones_b"])
    S.op("pool", lambda e: e.memset(pswap[:], 1.0), writes=["pswap"])
    S.op("pool", lambda e: e.affine_select(out=pswap[:], in_=pswap[:], pattern=[[-1, 128]],
                                           compare_op=ALU.is_equal, fill=0.0, base=64, channel_multiplier=1),
         reads=["pswap"], writes=["pswap"])
    S.op("pool", lambda e: e.memset(xstage[0][:, 0:128], 1.0), writes=["xstage0"])
    S.op("pool", lambda e: e.affine_select(out=xstage[0][:, 0:128], in_=xstage[0][:, 0:128], pattern=[[-1, 128]],
                                           compare_op=ALU.is_equal, fill=0.0, base=-64, channel_multiplier=1),
         reads=["xstage0"], writes=["xstage0"])
    S.op("pool", lambda e: e.tensor_tensor(out=pswap[:], in0=pswap[:], in1=xstage[0][:, 0:128], op=ALU.add),
         reads=["pswap", "xstage0"], writes=["pswap"])
    S.op("pool", lambda e: e.memset(vstage[:, :].rearrange("p (i u) -> p i u", u=192)[:, :, 64:128], 1.0),
         writes=["vstage"])
    S.op("pool", lambda e: e.memset(navc[:, :, :].rearrange("p b (i u) -> p b i u", u=192)[:, :, :, 64:128], 1.0),
         writes=["navc"])
    S.op("pool", lambda e: e.memset(v1[:, :, :].rearrange("p b (i u) -> p b i u", u=192)[:, :, :, 64:128], 1.0),
         writes=[("v1", 0), ("v1", 1), ("v1", 2)])
    S.op("pool", lambda e: e.memset(v1c[:, :, :].rearrange("p b (i u) -> p b i u", u=192)[:, :, :, 64:128], 1.0),
         writes=["v1c"])

    def cast_weight(k):
        r, c = WSHAPES[k]
        per = r * c // 128
        src = wf[k].rearrange("r c -> (r c)").rearrange("(p n) -> p n", p=128)
        dst = wb[k].rearrange("r c -> (r c)").rearrange("(p n) -> p n", p=128)
        for c0 in range(0, per, 8192):
            c1 = min(per, c0 + 8192)
            S.dma("pool", lambda e, s_=src[:, c0:c1], d_=dst[:, c0:c1]: e.dma_start(out=d_, in_=s_),
                  "wc_" + k, writes=[("wb", k, c0)])

    for k in ("w0kv", "wukP", "wuv", "w0nak", "w0nav"):
        cast_weight(k)
    CAST_AT = {1: ["w0q", "wqup"], 2: ["wout0"], 3: ["wgu0"], 7: ["wdn0"], 10: ["w1k", "w1v"]}

    def ld(dst_ap, src_ap, key, slot):
        S.dma("sp", lambda e: e.dma_start(out=dst_ap, in_=src_ap), slot, writes=[key])

    ld(qnorm[:], qnorm_d, "qnorm", "c0_qnorm")
    ld(kvnorm[:], kvnorm_d, "kvnorm", "c0_kvnorm")
    ld(fnorm[:], fnorm_d, "fnorm", "c0_fnorm")
    ld(esink[:], sinks_d, "esink", "c0_esink")
    ld(silc[:], cvec.rearrange("p k c -> p (k c)"), "silc", "c0_silc")
    S.dma("sp", lambda e: e.dma_start(out=modv[:, 0:192], in_=modb.rearrange("p l m c -> p (l m c)")), "c0_modv",
          writes=[("modv", 0), ("modv", 1)])
    S.op("act", lambda e: e.activation(out=esink[:], in_=esink[:], func=AF.Exp), reads=["esink"], writes=["esink"])
    S.op("act", lambda e: e.activation(out=silc[:], in_=silc[:], func=AF.Silu), reads=["silc"], writes=["silc"])
    for i in range(2):
        S.dma("sp", lambda e, i=i: e.dma_start(out=tmpf[:, i, :], in_=swamask_d[i]), "c1", writes=["tmpf"])
    S.op("act", lambda e: e.activation(out=swam[:, :, :], in_=tmpf[:, :, :], func=AF.Copy),
         reads=["tmpf"], writes=["swam"])


    def _finish():
        S.emit(st)
        st.close()
        print("instructions per engine:", {e: len(S.streams[e]) for e in ENGS})
        return nc
    if stage < 2:
        return _finish()
    def mod_blocks(l, ms, bank):
        for m in ms:
            xs = xstage[m % 2]
            xk = "xstage%d" % (m % 2)
            S.dma("sp", lambda e, xs=xs, m=m: e.dma_start(
                out=xs[:, :].rearrange("p (k c) -> p k c", k=8),
                in_=modw[l].rearrange("(k p) c -> p k c", p=128)[:, :, m * 128:(m + 1) * 128]),
                "xst%d" % (m % 2), writes=[xk])
            for kc in range(8):
                S.op("pe", lambda e, xs=xs, kc=kc, m=m: e.matmul(
                    psb[bank][:, 2 * m:2 * m + 2], lhsT=xs[:, kc * 128:(kc + 1) * 128],
                    rhs=silc[:, 2 * kc:2 * kc + 2], start=(kc == 0), stop=(kc == 7)),
                    reads=[xk, "silc"], writes=[pkey(bank)])

    def mod_finish(l, bank):
        S.op("dve", lambda e: e.tensor_tensor(
            out=modv[:, 96 * l:96 * l + 96], in0=psb[bank][:, 0:96], in1=modv[:, 96 * l:96 * l + 96], op=ALU.add),
            reads=[pkey(bank), ("modv", l)], writes=[("modv", l)])
        S.op("dve", lambda e: e.tensor_scalar_add(out=onep[:, 96 * l:96 * l + 96], in0=modv[:, 96 * l:96 * l + 96],
                                                  scalar1=1.0), reads=[("modv", l)], writes=[("onep", l)])

    bank0 = mm_bank()
    for blk in range(12):
        hbk = hbuf[blk % 2]
        hkk = "hbuf%d" % (blk % 2)
        S.dma("sp", lambda e, hbk=hbk, blk=blk: e.dma_start(
            out=hbk[:, :, :], in_=modw[0].rearrange("(k p) c -> p k c", p=128)[:, :, blk * 512:(blk + 1) * 512]),
            "modld%d" % (blk % 2), writes=[(hkk, kc) for kc in range(8)])
        for mm_ in range(4):
            m = blk * 4 + mm_
            for kc in range(8):
                S.op("pe", lambda e, hbk=hbk, kc=kc, m=m, mm_=mm_: e.matmul(
                    psb[bank0][:, 2 * m:2 * m + 2], lhsT=hbk[:, kc, mm_ * 128:(mm_ + 1) * 128],
                    rhs=silc[:, 2 * kc:2 * kc + 2], start=(kc == 0), stop=(kc == 7)),
                    reads=[(hkk, kc), "silc"], writes=[pkey(bank0)])
    mod_finish(0, bank0)

    if stage < 3:
        return _finish()
    def build_strip(h, v):
        S.dma("sp", lambda e: e.dma_start(out=xstage[0][0:64, 0:960], in_=nabias[h, v]),
              "xst0", writes=["xstage0"])
        S.dma("sp", lambda e: e.dma_start(out=xstage[1][0:64, 0:960], in_=namask[v]),
              "xst1", writes=["xstage1"])
        S.op("pool", lambda e: e.tensor_scalar_mul(out=xstage[0][0:64, 0:960], in0=xstage[0][0:64, 0:960], scalar1=8.0),
             reads=["xstage0"], writes=["xstage0"])
        S.op("pool", lambda e: e.tensor_tensor(out=naedge[0:64, 0, :], in0=xstage[0][0:64, 0:960],
                                               in1=xstage[1][0:64, 0:960], op=ALU.add),
             reads=["xstage0", "xstage1"], writes=["naedge0"])
        S.dma("pool", lambda e: e.dma_start(out=nastr_d[h, v], in_=naedge[0:64, 0, :]), "nb3",
              reads=["naedge0"], writes=["nastr_d"])

    def load_T(src_rows_ap, nblk, dst, dkey):
        for tb in range(nblk):
            xs = xstage[tb % 2]
            xk = "xstage%d" % (tb % 2)
            S.dma("sp", lambda e, xs=xs, tb=tb: e.dma_start(out=xs[:, :], in_=src_rows_ap[tb * 128:(tb + 1) * 128, :]),
                  "xst%d" % (tb % 2), writes=[xk])
            for g in range(2):
                bank = mm_bank()
                for q in range(4):
                    kc = 4 * g + q
                    S.op("pe", lambda e, xs=xs, kc=kc, q=q, bank=bank: e.transpose(
                        out=psb[bank][:, q * 128:(q + 1) * 128], in_=xs[:, kc * 128:(kc + 1) * 128],
                        identity=ident_f[:]), reads=[xk, "ident_f"], writes=[pkey(bank)])
                eng = "dve" if g == 0 else "act"
                if eng == "dve":
                    S.op("dve", lambda e, g=g, tb=tb, bank=bank: e.tensor_copy(
                        out=dst[:, 4 * g:4 * g + 4, tb * 128:(tb + 1) * 128],
                        in_=psb[bank][:, :].rearrange("p (q t) -> p q t", q=4)),
                        reads=[pkey(bank)], writes=[(dkey, 4 * g + q) for q in range(4)])
                else:
                    S.op("act", lambda e, g=g, tb=tb, bank=bank: e.activation(
                        out=dst[:, 4 * g:4 * g + 4, tb * 128:(tb + 1) * 128],
                        in_=psb[bank][:, :].rearrange("p (q t) -> p q t", q=4), func=AF.Copy),
                        reads=[pkey(bank)], writes=[(dkey, 4 * g + q) for q in range(4)])

    def rms_stats(src_fn, skeys, KC, N, nfeat):
        bank = mm_bank()
        for kc in range(KC):
            S.op("act", lambda e, kc=kc: e.activation(out=sqb[:, kc % 2, :N], in_=src_fn(kc), func=AF.Square),
                 reads=[skeys[kc]], writes=[("sqb", kc % 2)])
            S.op("pe", lambda e, kc=kc, bank=bank: e.matmul(
                psb[bank][:, :N], lhsT=ones_b[:, :], rhs=sqb[:, kc % 2, :N], start=(kc == 0), stop=(kc == KC - 1)),
                reads=[("sqb", kc % 2), "ones_b"], writes=[pkey(bank)])
        S.op("dve", lambda e, bank=bank: e.tensor_scalar(
            out=rstd[:, :N], in0=psb[bank][:, :N], scalar1=1.0 / nfeat, scalar2=EPS, op0=ALU.mult, op1=ALU.add),
            reads=[pkey(bank)], writes=["rstd"])
        S.op("act", lambda e: e.activation(out=rstd[:, :N], in_=rstd[:, :N], func=AF.Sqrt),
             reads=["rstd"], writes=["rstd"])
        S.op("dve", lambda e: e.reciprocal(out=rstd[:, :N], in_=rstd[:, :N]), reads=["rstd"], writes=["rstd"])

    def norm_mod(hsrc, hkey, N, l, wsh, wsc, col, dst, dkey, eng="dve"):
        rms_stats(lambda kc: hsrc[:, kc, :N], [(hkey, kc) for kc in range(8)], 8, N, D)
        for kc in range(8):
            i_sc = mvi(l, wsc, kc, col)
            i_sh = mvi(l, wsh, kc, col)
            if eng == "pool" and kc % 2 == 1:
                S.op("pool", lambda e, kc=kc, i_sc=i_sc: e.tensor_scalar_mul(
                    out=tmpf[:, kc % 2, :N], in0=hsrc[:, kc, :N], scalar1=onep[:, i_sc:i_sc + 1]),
                    reads=[(hkey, kc), ("onep", l)], writes=[("tmpf", kc % 2)])
                S.op("pool", lambda e, kc=kc: e.tensor_tensor(
                    out=tmpf[:, kc % 2, :N], in0=tmpf[:, kc % 2, :N], in1=rstd[:, :N], op=ALU.mult),
                    reads=[("tmpf", kc % 2), "rstd"], writes=[("tmpf", kc % 2)])
            else:
                S.op("dve", lambda e, kc=kc, i_sc=i_sc: e.scalar_tensor_tensor(
                    out=tmpf[:, kc % 2, :N], in0=hsrc[:, kc, :N], scalar=onep[:, i_sc:i_sc + 1], in1=rstd[:, :N],
                    op0=ALU.mult, op1=ALU.mult),
                    reads=[(hkey, kc), "rstd", ("onep", l)], writes=[("tmpf", kc % 2)])
            S.op("act", lambda e, kc=kc, i_sh=i_sh: e.activation(
                out=dst[:, kc, :N], in_=tmpf[:, kc % 2, :N], func=AF.Identity, bias=modv[:, i_sh:i_sh + 1], scale=1.0),
                reads=[("tmpf", kc % 2), ("modv", l)], writes=[(dkey, kc)])

    def wload(wname, KC, c0, ncols, k0=0):
        s = cnt["ws"] % NWS
        cnt["ws"] += 1
        view = wslot[s][:, 0:KC * ncols].rearrange("p (k c) -> p k c", k=KC)
        P = min(128, WSHAPES[wname][0])
        src = wb[wname].rearrange("(k p) c -> p k c", p=P)[:, k0:k0 + KC, c0:c0 + ncols]
        S.dma("sp", lambda e: e.dma_start(out=view[:P], in_=src), "ws%d" % s,
              reads=[("wb", wname, q0) for q0 in range(0, WSHAPES[wname][0] * WSHAPES[wname][1] // 128, 8192)],
              writes=["wslot%d" % s])
        return view, "wslot%d" % s

    def linear(wname, KC, rhs_fn, rkeys, N, mchunks, evac, blockcols=None, kparts=None):
        if blockcols is None:
            blockcols = max(128, (4096 // KC) // 128 * 128)
        i = 0
        while i < len(mchunks):
            c0 = mchunks[i][0]
            j = i
            while j < len(mchunks) and mchunks[j][0] + mchunks[j][1] - c0 <= blockcols:
                j += 1
            ncols = mchunks[j - 1][0] + mchunks[j - 1][1] - c0
            view, wkey = wload(wname, KC, c0, ncols)
            for ii in range(i, j):
                mc0, mn = mchunks[ii]
                bank = mm_bank()
                for kc in range(KC):
                    kp = 128 if kparts is None else kparts
                    S.op("pe", lambda e, view=view, kc=kc, mc0=mc0, mn=mn, c0=c0, bank=bank, kp=kp: e.matmul(
                        psb[bank][:mn, :N], lhsT=view[:kp, kc, mc0 - c0:mc0 - c0 + mn], rhs=rhs_fn(kc),
                        start=(kc == 0), stop=(kc == KC - 1)),
                        reads=[wkey, rkeys[kc]], writes=[pkey(bank)])
                evac(ii, psb[bank][:mn, :N], pkey(bank))
            i = j

    def linear_tm(wname, KC, lhs_fn, lkeys, ntb, tbsz, ncols, evac):
        view, wkey = wload(wname, KC, 0, ncols)
        for tb in range(ntb):
            bank = mm_bank()
            for kc in range(KC):
                S.op("pe", lambda e, kc=kc, tb=tb, bank=bank: e.matmul(
                    psb[bank][:tbsz, :ncols], lhsT=lhs_fn(kc, tb), rhs=view[:, kc, :ncols],
                    start=(kc == 0), stop=(kc == KC - 1)),
                    reads=[wkey, lkeys[kc]], writes=[pkey(bank)])
            evac(tb, psb[bank][:tbsz, :ncols], pkey(bank))

    def copy_evac(dst_fn, dkey_fn, alt=True):
        def ev(i, ps, bk):
            if alt and i % 2 == 1:
                S.op("act", lambda e: e.activation(out=dst_fn(i), in_=ps, func=AF.Copy),
                     reads=[bk], writes=[dkey_fn(i)])
            else:
                S.op("dve", lambda e: e.tensor_copy(out=dst_fn(i), in_=ps), reads=[bk], writes=[dkey_fn(i)])
        return ev

    ropeT = [sb("ropeT%d" % i, [128, T], F32) for i in range(1)]

    NPT = len(pT)

    def run_attn(blocks, skew=2, defer=3):
        deferred = []
        nb = len(blocks)
        for k in range(nb + skew):
            if k < nb:
                b = blocks[k]
                if b.get("pre"):
                    b["pre"]()
                bank = mm_bank()
                np_, c0, c1 = b["np_"], b["c0"], b["c1"]
                n = len(b["qk"])
                for i, (lhsT, rhs, rk, oc0, oc1) in enumerate(b["qk"]):
                    o_ap = psb[bank][:np_, oc0:oc1]
                    if len(rhs.shape) == 3:
                        o_ap = o_ap.rearrange("p (h t) -> p h t", h=rhs.shape[1])
                    S.op("pe", lambda e, lhsT=lhsT, rhs=rhs, i=i, o_ap=o_ap, n=n: e.matmul(
                        o_ap, lhsT=lhsT, rhs=rhs, start=(i == 0), stop=(i == n - 1)),
                        reads=rk, writes=[pkey(bank)])
                pi = cnt["pt"] % NPT
                cnt["pt"] += 1
                b["pi"] = pi
                S.op("act", lambda e, bank=bank, pi=pi, np_=np_, c0=c0, c1=c1, sc=b["scale"]: e.activation(
                    out=pT[pi][:np_, c0:c1], in_=psb[bank][:np_, c0:c1], func=AF.Exp, scale=sc),
                    reads=[pkey(bank)], writes=["pT%d" % pi])
            still = []
            for (when, fn) in deferred:
                if when <= k:
                    fn()
                else:
                    still.append((when, fn))
            deferred = still
            kk = k - skew
            if kk >= 0:
                b = blocks[kk]
                pi, np_, c0, c1 = b["pi"], b["np_"], b["c0"], b["c1"]
                acc = b["acc"]
                S.op("pe", lambda e, b=b, pi=pi, np_=np_, c0=c0, c1=c1, acc=acc: e.matmul(
                    psb[acc][:, c0:c1], lhsT=b["v_lhsT"], rhs=pT[pi][:np_, c0:c1], start=b["first"], stop=b["last"]),
                    reads=["pT%d" % pi] + b["vkeys"], writes=[pkey(acc)])
                if b.get("epi"):
                    fn = b["epi"]()
                    if fn is not None:
                        deferred.append((k + defer, fn))
        for (_, fn) in deferred:
            fn()

    acc_rot = {"i": 0}

    def next_acc():
        i = acc_rot["i"] % 4
        acc_rot["i"] += 1
        return 4 + i

    def pair_epilogue(accA, accB, N, final_fn, sink_heads=None):
        for (acc, lo) in ((accA, 64), (accB, 0)):
            if sink_heads is None:
                S.op("dve", lambda e, acc=acc, lo=lo: e.reciprocal(out=recb[lo:lo + 64, 0, :N], in_=psb[acc][lo:lo + 64, :N]),
                     reads=[pkey(acc)], writes=[("recb", 0)])
            else:
                heads = sink_heads[0] if lo == 64 else sink_heads[1]
                for hh, head in enumerate(heads):
                    S.op("dve", lambda e, acc=acc, lo=lo, hh=hh, head=head: e.tensor_scalar_add(
                        out=recb[lo:lo + 64, 0, hh * 128:(hh + 1) * 128], in0=psb[acc][lo:lo + 64, hh * 128:(hh + 1) * 128],
                        scalar1=esink[lo:lo + 64, head:head + 1]),
                        reads=[pkey(acc), "esink"], writes=[("recb", 0)])
                S.op("dve", lambda e, lo=lo: e.reciprocal(out=recb[lo:lo + 64, 0, :N], in_=recb[lo:lo + 64, 0, :N]),
                     reads=[("recb", 0)], writes=[("recb", 0)])

        def pe_part():
            bank = mm_bank()
            S.op("pe", lambda e: e.matmul(psb[bank][:, :N], lhsT=pswap[:, :], rhs=recb[:, 0, :N], start=True, stop=True),
                 reads=[("recb", 0), "pswap"], writes=[pkey(bank)])
            S.op("act", lambda e: e.activation(out=recb[:, 1, :N], in_=psb[bank][:, :N], func=AF.Copy),
                 reads=[pkey(bank)], writes=[("recb", 1)])
            final_fn(recb[:, 1, :N], ("recb", 1))
        return pe_part

    KV_KEYS = []
    for t_ in range(17):
        k0_ = t_ * T
        KV_KEYS.append(("kr_d", k0_))
        KV_KEYS += [("khp_d", k0_, i_) for i_ in range(4)]
        KV_KEYS += [("vh_d", k0_, tb_) for tb_ in range(4)]

    def mla_heads_tile(N, key_chunks, cat, catkey):
        scale = 96.0 ** -0.5
        blocks = []
        nblk_tot = sum(nb for (_, nb) in key_chunks)
        accs = {}
        for h in range(8):
            i, par = h // 2, h % 2
            acc = next_acc()
            accs[h] = acc
            done = 0
            for (k0, nb) in key_chunks:
                s = cnt["uid"] % 2
                cnt["uid"] += 1

                def pre(s=s, k0=k0, nb=nb, h=h, i=i):
                    S.dma("sp", lambda e: e.dma_start(out=mck[s][0:64, :nb * 128],
                                                      in_=khp_d[i, 64 * (h % 2):64 * (h % 2) + 64, k0:k0 + nb * 128]),
                          "mck%d" % s, reads=KV_KEYS, writes=["mck%d" % s])
                    S.dma("sp", lambda e: e.dma_start(out=mck[s][64:96, :nb * 128], in_=kr_d[:, k0:k0 + nb * 128]),
                          "mck%d" % s, reads=KV_KEYS, writes=["mck%d" % s])
                    S.dma("sp", lambda e: e.dma_start(
                        out=mv[s][:, :nb, :],
                        in_=vh_d[k0:k0 + nb * 128, 192 * i:192 * i + 192].rearrange("(b p) c -> p b c", p=128)),
                        "mv%d" % s, reads=KV_KEYS, writes=["mv%d" % s])
                for kb in range(nb):
                    qk = [(mck[s][:, kb * 128:(kb + 1) * 128], Qh[:, h, :N], ["mck%d" % s, ("Qh", h)], 0, N)]
                    blk = dict(np_=128, c0=0, c1=N, qk=qk, scale=scale,
                               v_lhsT=mv[s][:, kb, 64 * par:64 * par + 128], vkeys=["mv%d" % s],
                               acc=acc, first=(done == 0), last=(done == nblk_tot - 1),
                               pre=(pre if kb == 0 else None))
                    done += 1
                    blocks.append(blk)
            if par == 1:
                def epi(i=i, accA=accs[h - 1], accB=acc):
                    def fin(rec, rkey):
                        for (p0, a_) in ((0, accA), (64, accB)):
                            S.op("dve", lambda e, p0=p0, a_=a_: e.tensor_tensor(
                                out=cat[p0:p0 + 64, i, :N], in0=psb[a_][p0:p0 + 64, :N], in1=rec[p0:p0 + 64, :],
                                op=ALU.mult), reads=[pkey(a_), rkey], writes=[(catkey, i)])
                    return pair_epilogue(accA, accB, N, fin)
                blocks[-1]["epi"] = epi
        run_attn(blocks)

    NA_KEYS = [("nakT_d", t_ * T, i_) for t_ in range(NT_NAKV) for i_ in range(4)] + \
              [("nav_d", t_ * T, tb_) for t_ in range(NT_NAKV) for tb_ in range(4)]

    def na_heads_tile(j, N, cat, catkey, latent=True):
        scale = 0.125
        blocks = []
        for i in range(4):
            pre = None
            if latent:
                nqr = N // 64
                w0 = max(0, 8 * j - 4)
                w1 = min(NT_NAKV * 8, 8 * j + nqr + 4)
                nr = w1 - w0
                s = i % 2

                def pre(s=s, i=i, w0=w0, nr=nr):
                    S.dma("sp", lambda e: e.dma_start(out=nakw[:, s, :nr * 64], in_=nakT_d[:, i, w0 * 64:(w0 + nr) * 64]),
                          "nakw%d" % s, reads=NA_KEYS, writes=["nakw%d" % s])
                    S.dma("sp", lambda e: e.dma_start(
                        out=navw[:, s, :nr * 192].rearrange("p (r f) -> p r f", f=192),
                        in_=nav_d[w0 * 64:(w0 + nr) * 64, i * 192:(i + 1) * 192].rearrange("(r t) f -> t r f", t=64)),
                        "navw%d" % s, reads=NA_KEYS, writes=["navw%d" % s])
                    for pp in range(2):
                        S.dma("sp", lambda e, pp=pp: e.dma_start(
                            out=nastrip[64 * pp:64 * pp + 64, s, :], in_=nastr_d[2 * i + pp, 0]),
                            "nastrip%d" % s, reads=["nastr_d"], writes=["nastrip%d" % s])
                        if j == 0:
                            S.dma("sp", lambda e, pp=pp: e.dma_start(
                                out=naedge[64 * pp:64 * pp + 64, 0, :], in_=nastr_d[2 * i + pp, 1]),
                                "naedge0", reads=["nastr_d"], writes=["naedge0"])
            for par in range(2):
                h = 2 * i + par
                p0 = 64 * par
                acc = next_acc()
                if par == 0:
                    accA = acc
                vo = 192 * i + 64 * par
                hb = []
                for cb in range(2):
                    qk = [(nakc[p0:p0 + 64, i, cb * 128:(cb + 1) * 128], naq[p0:p0 + 64, i, :N],
                           [("nakc", i), ("naq", i)], 0, N)]
                    hb.append(dict(np_=128, c0=0, c1=N, qk=qk, scale=scale, v_lhsT=navc[:, cb, vo:vo + 128],
                                   vkeys=["navc"], acc=acc))
                if latent:
                    for kr in range(w0, w1):
                        if j == 0:
                            r0, r1 = (0, 7) if kr <= 7 else (kr - 4, 7)
                        else:
                            r0, r1 = max(8 * j, kr - 4), min(8 * j + nqr - 1, kr + 4)
                        if r0 > r1:
                            continue
                        a, b = r0 - 8 * j, r1 - 8 * j + 1
                        c0, c1 = a * 64, b * 64
                        lk = (kr - w0) * 64
                        segs = []
                        if j == 0 and a < 4:
                            be = min(b, 4)
                            segs.append((a, be, naedge[p0:p0 + 64, 0, :], "naedge0"))
                            if b > 4:
                                segs.append((4, b, nastrip[p0:p0 + 64, s, :], "nastrip%d" % s))
                        else:
                            segs.append((a, b, nastrip[p0:p0 + 64, s, :], "nastrip%d" % s))
                        qk = [(nakw[p0:p0 + 64, s, lk:lk + 64], naq[p0:p0 + 64, i, c0:c1],
                               ["nakw%d" % s, ("naq", i)], c0, c1)]
                        for (sa, sb_, strip, skey) in segs:
                            e0 = (7 - kr + 8 * j + sa) * 64
                            qk.append((ident_b[p0:p0 + 64, p0:p0 + 64], strip[:, e0:e0 + (sb_ - sa) * 64],
                                       [skey, "ident_b"], sa * 64, sb_ * 64))
                        vw = (kr - w0) * 192 + 64 * par
                        hb.append(dict(np_=64, c0=c0, c1=c1, qk=qk, scale=scale,
                                       v_lhsT=navw[:, s, vw:vw + 128], vkeys=["navw%d" % s], acc=acc))
                for bi, blk in enumerate(hb):
                    blk["first"] = (bi == 0)
                    blk["last"] = (bi == len(hb) - 1)
                if par == 0 and pre is not None:
                    hb[0]["pre"] = pre

                if par == 1:
                    def epi(i=i, accA=accA, accB=acc):
                        def fin(rec, rkey):
                            for (q0, a_) in ((0, accA), (64, accB)):
                                S.op("dve", lambda e, q0=q0, a_=a_: e.tensor_tensor(
                                    out=cat[q0:q0 + 64, 4 + i, :N], in0=psb[a_][q0:q0 + 64, :N], in1=rec[q0:q0 + 64, :],
                                    op=ALU.mult), reads=[pkey(a_), rkey], writes=[(catkey, 4 + i)])
                        return pair_epilogue(accA, accB, N, fin)
                    hb[-1]["epi"] = epi
                blocks.extend(hb)
        run_attn(blocks)

    def gated_residual(hdst, hkey, N, l, wg, col):
        def ev(i, ps, bk):
            ig = mvi(l, wg, i, col)
            S.op("dve", lambda e: e.scalar_tensor_tensor(
                out=hdst[:, i, :N], in0=ps, scalar=modv[:, ig:ig + 1], in1=hdst[:, i, :N], op0=ALU.mult, op1=ALU.add),
                reads=[bk, (hkey, i), ("modv", l)], writes=[(hkey, i)])
        return ev

    def ffn(hdst, hkey, N, l, col, ab, abkey):
        norm_mod(hdst, hkey, N, l, 3, 4, col, ab, abkey)
        wn = "wgu%d" % l
        for f0 in range(0, 22, 2):
            nf = min(2, 22 - f0)
            gview, gkey = wload(wn, 8, f0 * 128, nf * 128)
            uview, ukey = wload(wn, 8, DFF + f0 * 128, nf * 128)
            for ff in range(nf):
                fi = f0 + ff
                bg = mm_bank()
                for kc in range(8):
                    S.op("pe", lambda e, kc=kc, ff=ff, bg=bg, gview=gview: e.matmul(
                        psb[bg][:, :N], lhsT=gview[:, kc, ff * 128:(ff + 1) * 128], rhs=ab[:, kc, :N],
                        start=(kc == 0), stop=(kc == 7)), reads=[gkey, (abkey, kc)], writes=[pkey(bg)])
                bu = mm_bank()
                for kc in range(8):
                    S.op("pe", lambda e, kc=kc, ff=ff, bu=bu, uview=uview: e.matmul(
                        psb[bu][:, :N], lhsT=uview[:, kc, ff * 128:(ff + 1) * 128], rhs=ab[:, kc, :N],
                        start=(kc == 0), stop=(kc == 7)), reads=[ukey, (abkey, kc)], writes=[pkey(bu)])
                S.op("act", lambda e, fi=fi, bg=bg: e.activation(out=silt[:, fi % 2, :N], in_=psb[bg][:, :N],
                                                                 func=AF.Silu),
                     reads=[pkey(bg)], writes=[("silt", fi % 2)])
                S.op("dve", lambda e, fi=fi, bu=bu: e.tensor_tensor(out=actT[:, fi, :N], in0=psb[bu][:, :N],
                                                                   in1=silt[:, fi % 2, :N], op=ALU.mult),
                     reads=[pkey(bu), ("silt", fi % 2)], writes=[("actT", fi)])
        linear("wdn%d" % l, 22, lambda kc: actT[:, kc, :N], [("actT", kc) for kc in range(22)], N,
               [(c * 128, 128) for c in range(8)], gated_residual(hdst, hkey, N, l, 5, col), blockcols=128)

    def rope_pair(k, ps, bk, npp, N, dst_ap, dkey):
        ii = k // 2
        if k % 2 == 0:
            S.op("dve", lambda e: e.tensor_tensor(out=ropeT[0][:npp, :N], in0=ps, in1=ropeC[:npp, :N], op=ALU.mult),
                 reads=[bk, "ropeC"], writes=[("ropeT", 0)])
        else:
            S.op("dve", lambda e: e.tensor_tensor(out=ropeS[:npp, 1, :N], in0=ps, in1=ropeS[:npp, 0, :N], op=ALU.mult),
                 reads=[bk, "ropeS0"], writes=["ropeS1"])
            S.op("dve", lambda e: e.tensor_tensor(out=dst_ap, in0=ropeT[0][:npp, :N], in1=ropeS[:npp, 1, :N],
                                                  op=ALU.add),
                 reads=[("ropeT", 0), "ropeS1"], writes=[dkey])

    def l0_q_side(ab, abkey, N, tok0, rotate):
        def ev(i, ps, bk):
            if i < 2:
                S.op("dve", lambda e: e.tensor_copy(out=cq[:, i, :N], in_=ps), reads=[bk], writes=[("tmpf", i)])
            else:
                S.op("act", lambda e: e.activation(out=naq[:, i - 2, :N], in_=ps, func=AF.Copy),
                     reads=[bk], writes=[("naq", i - 2)])
        linear("w0q", 8, lambda kc: ab[:, kc, :N], [(abkey, kc) for kc in range(8)], N,
               [(c * 128, 128) for c in range(6)], ev)
        rms_stats(lambda kc: cq[:, kc, :N], [("tmpf", kc) for kc in range(2)], 2, N, 256)
        for kc in range(2):
            S.op("dve", lambda e, kc=kc: e.scalar_tensor_tensor(
                out=cqn[:, kc, :N], in0=cq[:, kc, :N], scalar=qnorm[:, kc:kc + 1], in1=rstd[:, :N],
                op0=ALU.mult, op1=ALU.mult), reads=[("tmpf", kc), "rstd", "qnorm"], writes=[("cqn", kc)])
        if rotate:
            mch = [(h * 192 + 96 * v, 96) for h in range(8) for v in range(2)]
            S.dma("sp", lambda e: e.dma_start(out=ropeC[:96, :N], in_=ropeMc[:, tok0:tok0 + N]), "ropeC",
                  writes=["ropeC"])
            S.dma("sp", lambda e: e.dma_start(out=ropeS[:96, 0, :N], in_=ropeMs[:, tok0:tok0 + N]), "ropeS",
                  writes=["ropeS0"])
        else:
            mch = [(h * 192, 96) for h in range(8)]

        def ev2(i, ps, bk):
            if not rotate:
                h = i
                if h % 2 == 0:
                    S.op("dve", lambda e: e.tensor_copy(out=Qh[:, h, :N], in_=ps), reads=[bk], writes=[("Qh", h)])
                else:
                    S.op("act", lambda e: e.activation(out=Qh[:, h, :N], in_=ps, func=AF.Copy),
                         reads=[bk], writes=[("Qh", h)])
                return
            h, v = i // 2, i % 2
            if v == 0:
                S.op("act", lambda e: e.activation(out=Qh[0:64, h, :N], in_=ps[0:64, :], func=AF.Copy),
                     reads=[bk], writes=[("Qh", h)])
                S.op("dve", lambda e: e.tensor_tensor(out=ropeT[0][64:96, :N], in0=ps[64:96, :], in1=ropeC[64:96, :N],
                                                      op=ALU.mult),
                     reads=[bk, "ropeC"], writes=[("ropeT", 0)])
            else:
                S.op("dve", lambda e: e.tensor_tensor(out=ropeS[64:96, 1, :N], in0=ps[64:96, :], in1=ropeS[64:96, 0, :N],
                                                      op=ALU.mult),
                     reads=[bk, "ropeS0"], writes=["ropeS1"])
                S.op("dve", lambda e: e.tensor_tensor(out=Qh[64:96, h, :N], in0=ropeT[0][64:96, :N],
                                                      in1=ropeS[64:96, 1, :N], op=ALU.add),
                     reads=[("ropeT", 0), "ropeS1"], writes=[("Qh", h)])
        linear("wqup", 2, lambda kc: cqn[:, kc, :N], [("cqn", kc) for kc in range(2)], N, mch, ev2, blockcols=1536)

    def l0_kv(ab, abkey, N, key0, tok0, rotate):
        if rotate:
            S.dma("sp", lambda e: e.dma_start(out=ropeC[:32, :N], in_=ropeMc[0:32, tok0:tok0 + N]), "ropeC",
                  writes=["ropeC"])
            S.dma("sp", lambda e: e.dma_start(out=ropeS[:32, 0, :N], in_=ropeMs[0:32, tok0:tok0 + N]), "ropeS",
                  writes=["ropeS0"])

        def ev(i, ps, bk):
            if i == 0:
                S.op("dve", lambda e: e.tensor_copy(out=ckvf[:, :N], in_=ps), reads=[bk], writes=[("recb", 0)])
            elif not rotate:
                S.op("act", lambda e: e.activation(out=pT[1][:32, :N], in_=ps, func=AF.Copy),
                     reads=[bk], writes=["pT1"])
            else:
                rope_pair(i - 1, ps, bk, 32, N, pT[1][:32, :N], "pT1")
        mch = [(0, 128), (128, 32)] + ([(160, 32)] if rotate else [])
        linear("w0kv", 8, lambda kc: ab[:, kc, :N], [(abkey, kc) for kc in range(8)], N, mch, ev)
        rms_stats(lambda kc: ckvf[:, :N], [("recb", 0)], 1, N, 128)
        S.op("dve", lambda e: e.scalar_tensor_tensor(out=pT[0][:, :N], in0=ckvf[:, :N], scalar=kvnorm[:, 0:1],
                                                     in1=rstd[:, :N], op0=ALU.mult, op1=ALU.mult),
             reads=[("recb", 0), "rstd", "kvnorm"], writes=["pT0"])
        S.dma("pool", lambda e: e.dma_start(out=kr_d[:, key0:key0 + N], in_=pT[1][:32, :N]),
              "kvo1", reads=["pT1"], writes=[("kr_d", key0)])
        ukv, ukkey = wload("wukP", 1, 0, 512)
        for i in range(4):
            bank = mm_bank()
            S.op("pe", lambda e, i=i, bank=bank: e.matmul(psb[bank][:, :N], lhsT=ukv[:, 0, i * 128:(i + 1) * 128],
                                                          rhs=pT[0][:, :N], start=True, stop=True),
                 reads=[ukkey, "pT0"], writes=[pkey(bank)])
            st_ = pT[2]
            sk = "pT2"
            if i % 2 == 0:
                S.op("dve", lambda e, st_=st_, bank=bank: e.tensor_copy(out=st_[:, :N], in_=psb[bank][:, :N]),
                     reads=[pkey(bank)], writes=[sk])
            else:
                S.op("act", lambda e, st_=st_, bank=bank: e.activation(out=st_[:, :N], in_=psb[bank][:, :N],
                                                                       func=AF.Copy),
                     reads=[pkey(bank)], writes=[sk])
            S.dma("pool", lambda e, i=i, st_=st_: e.dma_start(out=khp_d[i, :, key0:key0 + N], in_=st_[:, :N]),
                  "kvo0", reads=[sk], writes=[("khp_d", key0, i)])
        uvv, uvkey = wload("wuv", 1, 0, 512)
        for tb in range(N // 128):
            bank = mm_bank()
            S.op("pe", lambda e, tb=tb, bank=bank: e.matmul(psb[bank][:, :512], lhsT=pT[0][:, tb * 128:(tb + 1) * 128],
                                                            rhs=uvv[:, 0, :], start=True, stop=True),
                 reads=[uvkey, "pT0"], writes=[pkey(bank)])
            v_evac(psb[bank][:, :512], pkey(bank), vstage[:, :], "vstage")
            S.dma("pool", lambda e, tb=tb: e.dma_start(out=vh_d[key0 + tb * 128:key0 + (tb + 1) * 128, :], in_=vstage[:, :]),
                  "kvo2", reads=["vstage"], writes=[("vh_d", key0, tb)])

    def v_evac(ps, bk, dst2d, dkey):
        src = ps.rearrange("p (i t c) -> p i t c", t=2, c=64)
        dst = dst2d.rearrange("p (i u) -> p i u", u=192)
        S.op("dve", lambda e: e.tensor_copy(out=dst[:, :, 0:64], in_=src[:, :, 0, :]), reads=[bk], writes=[dkey])
        S.op("act", lambda e: e.activation(out=dst[:, :, 128:192], in_=src[:, :, 1, :], func=AF.Copy),
             reads=[bk], writes=[dkey])

    def l0_nakv(ab, abkey, N, tok0, to_ctx):
        if to_ctx:
            linear("w0nak", 8, lambda kc: ab[:, kc, :N], [(abkey, kc) for kc in range(8)], N,
                   [(c * 128, 128) for c in range(4)],
                   copy_evac(lambda i: nakc[:, i, :N], lambda i: ("nakc", i)))
            linear_tm("w0nav", 8, lambda kc, tb: ab[:, kc, tb * 128:(tb + 1) * 128],
                      [(abkey, kc) for kc in range(8)], N // 128, 128, 512,
                      lambda tb, ps, bk: v_evac(ps, bk, navc[:, tb, :], "navc"))
            return

        def evk(i, ps, bk):
            S.op("act" if i % 2 else "dve",
                 (lambda e: e.activation(out=pT[i % 2][:, :N], in_=ps, func=AF.Copy)) if i % 2 else
                 (lambda e: e.tensor_copy(out=pT[i % 2][:, :N], in_=ps)),
                 reads=[bk], writes=["pT%d" % (i % 2)])
            S.dma("pool", lambda e: e.dma_start(out=nakT_d[:, i, tok0:tok0 + N], in_=pT[i % 2][:, :N]),
                  "nako%d" % (i % 2), reads=["pT%d" % (i % 2)], writes=[("nakT_d", tok0, i)])
        linear("w0nak", 8, lambda kc: ab[:, kc, :N], [(abkey, kc) for kc in range(8)], N,
               [(c * 128, 128) for c in range(4)], evk)

        def evv(tb, ps, bk):
            v_evac(ps, bk, vstage[:, :], "vstage")
            S.dma("pool", lambda e: e.dma_start(out=nav_d[tok0 + tb * 128:tok0 + (tb + 1) * 128, :], in_=vstage[:, :]),
                  "navo", reads=["vstage"], writes=[("nav_d", tok0, tb)])
        linear_tm("w0nav", 8, lambda kc, tb: ab[:, kc, tb * 128:(tb + 1) * 128],
                  [(abkey, kc) for kc in range(8)], N // 128, 128, 512, evv)

    def v1_evac(ps, bk, dst2d, dkey):
        src = ps.rearrange("p (g c) -> p g c", c=64)
        dst = dst2d.rearrange("p (g u) -> p g u", u=192)
        S.op("dve", lambda e: e.tensor_copy(out=dst[:, :, 0:64], in_=src), reads=[bk], writes=[dkey])
        S.op("act", lambda e: e.activation(out=dst[:, :, 128:192], in_=src, func=AF.Copy), reads=[bk], writes=[dkey])

    def l0_finish(hdst, hkey, N, col, cat, catkey, ab2, ab2key):
        linear("wout0", 8, lambda kc: cat[:, kc, :N], [(catkey, kc) for kc in range(8)], N,
               [(c * 128, 128) for c in range(8)], gated_residual(hdst, hkey, N, 0, 2, col))
        ffn(hdst, hkey, N, 0, col, ab2, ab2key)

    def dump(name, src_fn, nchunk, N):
        if name not in dbg_out:
            return
        for kc in range(nchunk):
            src, keys = src_fn(kc)
            S.op("dve", lambda e, src=src: e.tensor_copy(out=ostage[0][:, :N], in_=src), reads=keys, writes=["xstage0"])
            S.dma("pool", lambda e, kc=kc: e.dma_start(out=dbg_out[name][kc * 128:(kc + 1) * 128, :], in_=ostage[0][:, :N]),
                  "dbg", reads=["xstage0"], out=True)

    def prologue_A():
        load_T(ctxl, 2, hctx, "hbuf1")
        norm_mod(hctx, "hbuf1", CTX, 0, 0, 1, 1, abuf[0], "abuf0")
        l0_kv(abuf[0], "abuf0", CTX, S_FULL, 0, False)
        l0_nakv(abuf[0], "abuf0", CTX, 0, True)
        l0_q_side(abuf[0], "abuf0", CTX, 0, False)
        mla_heads_tile(CTX, [(S_FULL, 2)], abuf[1], "abuf1")
        na_heads_tile(0, CTX, abuf[1], "abuf1", latent=False)
        l0_finish(hctx, "hbuf1", CTX, 1, abuf[1], "abuf1", abuf[0], "abuf0")
        dump("hctx1", lambda kc: (hctx[:, kc, :CTX], [("hbuf1", kc)]), 8, CTX)
        norm_mod(hctx, "hbuf1", CTX, 1, 0, 1, 1, abuf[0], "abuf0")
        linear("w1k", 8, lambda kc: abuf[0][:, kc, :CTX], [("abuf0", kc) for kc in range(8)], CTX,
               [(0, 128), (256, 128)], copy_evac(lambda i: k1c[:, i, :], lambda i: ("k1c", i)))
        linear_tm("w1v", 8, lambda kc, tb: abuf[0][:, kc, tb * 128:(tb + 1) * 128], [("abuf0", kc) for kc in range(8)],
                  2, 128, 128, lambda tb, ps, bk: v1_evac(ps, bk, v1c[:, tb, :], "v1c"))


    if stage < 5:
        return _finish()
    NPRO = S_FULL // T

    def pro_head(i):
        for k_ in CAST_AT.get(i, ()):
            cast_weight(k_)
        build_strip(i // 2, i % 2)
        mod_blocks(1, range(3 * i, 3 * i + 3), 7)
        load_T(xl[i * T:(i + 1) * T, :], 4, hbuf[i % 2], "hbuf%d" % (i % 2))
        norm_mod(hbuf[i % 2], "hbuf%d" % (i % 2), T, 0, 0, 1, 0, abuf[i % 2], "abuf%d" % (i % 2))

    pro_head(0)
    for i in range(NPRO):
        ab = abuf[i % 2]
        ak = "abuf%d" % (i % 2)
        if i + 1 < NPRO:
            pro_head(i + 1)
        l0_kv(ab, ak, T, i * T, i * T, True)
        if i < NT_NAKV:
            l0_nakv(ab, ak, T, i * T, False)

    def l1_kv(j, hb, hk, N=T):
        ab, ak = abuf[0], "abuf0"
        norm_mod(hb, hk, N, 1, 0, 1, 0, ab, ak)
        slot = j % 3
        S.dma("sp", lambda e: e.dma_start(out=ropeC[:, :N], in_=ropeSc[:, j * T:j * T + N]), "ropeC", writes=["ropeC"])
        S.dma("sp", lambda e: e.dma_start(out=ropeS[:, 0, :N], in_=ropeSs[:, j * T:j * T + N]), "ropeS",
              writes=["ropeS0"])

        def ev(i, ps, bk):
            rope_pair(i, ps, bk, 128, N, k1[:, i // 2, slot * T:slot * T + N], ("k1", i // 2, slot))
        linear("w1k", 8, lambda kc: ab[:, kc, :N], [(ak, kc) for kc in range(8)], N,
               [(c * 128, 128) for c in range(4)], ev)
        linear_tm("w1v", 8, lambda kc, tb: ab[:, kc, tb * 128:(tb + 1) * 128], [(ak, kc) for kc in range(8)],
                  N // 128, 128, 128, lambda tb, ps, bk: v1_evac(ps, bk, v1[:, slot * 4 + tb, :], ("v1", slot)))

    def l1_main(t, hb, hk):
        ab, ak = abuf[1], "abuf1"
        norm_mod(hb, hk, T, 1, 0, 1, 0, ab, ak)
        S.dma("sp", lambda e: e.dma_start(out=ropeC[:, :], in_=ropeSc[:, t * T:(t + 1) * T]), "ropeC", writes=["ropeC"])
        S.dma("sp", lambda e: e.dma_start(out=ropeS[:, 0, :], in_=ropeSs[:, t * T:(t + 1) * T]), "ropeS",
              writes=["ropeS0"])

        def evq(i, ps, bk):
            rope_pair(i, ps, bk, 128, T, q1[:, i // 2, :], ("q1", i // 2))
        linear("w1q", 8, lambda kc: ab[:, kc, :], [(ak, kc) for kc in range(8)], T,
               [(c * 128, 128) for c in range(16)], evq)
        cat, catkey = abuf[0], "abuf0"
        blocks = []
        for qb in range(4):
            gq = 4 * t + qb
            for g in range(2):
                for par in range(2):
                    p0 = 64 * par
                    acc = next_acc()
                    if par == 0:
                        accA = acc
                    vo = 192 * g + 64 * par
                    rhs = q1[p0:p0 + 64, 4 * g:4 * g + 4, qb * 128:(qb + 1) * 128]
                    rk = [("q1", 4 * g + c) for c in range(4)]
                    hbl = []
                    for cb in range(2):
                        qk = [(k1c[p0:p0 + 64, g, cb * 128:(cb + 1) * 128], rhs, [("k1c", g)] + rk, 0, 512)]
                        hbl.append(dict(np_=128, c0=0, c1=512, qk=qk, scale=0.125,
                                       v_lhsT=v1c[:, cb, vo:vo + 128], vkeys=["v1c"], acc=acc))
                    for dk in (-1, 0, 1):
                        kb = gq + dk
                        if kb < 0:
                            continue
                        slot = (kb // 4) % 3
                        kcol = slot * T + (kb % 4) * 128
                        qk = [(k1[p0:p0 + 64, g, kcol:kcol + 128], rhs, [("k1", g, slot)] + rk, 0, 512)]
                        if dk != 0:
                            mi = 0 if dk < 0 else 1
                            qk.append((ident_b[:, :], swam[:, mi, :], ["ident_b", "swam"], 0, 512))
                        hbl.append(dict(np_=128, c0=0, c1=512, qk=qk, scale=0.125,
                                       v_lhsT=v1[:, slot * 4 + (kb % 4), vo:vo + 128], vkeys=[("v1", slot)],
                                       acc=acc))
                    for bi, blk in enumerate(hbl):
                        blk["first"] = (bi == 0)
                        blk["last"] = (bi == len(hbl) - 1)

                    if par == 1:
                        def epi(g=g, qb=qb, accA=accA, accB=acc):
                            def fin(rec, rkey):
                                for (q0, a_) in ((0, accA), (64, accB)):
                                    S.op("dve", lambda e, q0=q0, a_=a_: e.tensor_tensor(
                                        out=cat[q0:q0 + 64, 4 * g:4 * g + 4, qb * 128:(qb + 1) * 128],
                                        in0=psb[a_][q0:q0 + 64, :].rearrange("p (h t) -> p h t", h=4),
                                        in1=rec[q0:q0 + 64, :].rearrange("p (h t) -> p h t", h=4), op=ALU.mult),
                                        reads=[pkey(a_), rkey], writes=[(catkey, 4 * g + c) for c in range(4)])
                            sink = ([8 * g + 2 * hh for hh in range(4)], [8 * g + 2 * hh + 1 for hh in range(4)])
                            return pair_epilogue(accA, accB, T, fin, sink_heads=sink)
                        hbl[-1]["epi"] = epi
                    blocks.extend(hbl)
        run_attn(blocks)
        linear("wout1", 8, lambda kc: cat[:, kc, :], [(catkey, kc) for kc in range(8)], T,
               [(c * 128, 128) for c in range(8)], gated_residual(hb, hk, T, 1, 2, 0))
        ffn(hb, hk, T, 1, 0, abuf[1], "abuf1")
        rms_stats(lambda kc: hb[:, kc, :], [(hk, kc) for kc in range(8)], 8, T, D)
        for kc in range(8):
            S.op("dve", lambda e, kc=kc: e.scalar_tensor_tensor(
                out=hb[:, kc, :], in0=hb[:, kc, :], scalar=fnorm[:, kc:kc + 1], in1=rstd[:, :],
                op0=ALU.mult, op1=ALU.mult), reads=[(hk, kc), "rstd", "fnorm"], writes=[(hk, kc)])
        for tb in range(4):
            os_ = ostage[tb % 2]
            ok = "xstage%d" % (tb % 2)
            for g2 in range(2):
                bank = mm_bank()
                for q in range(4):
                    kc = 4 * g2 + q
                    S.op("pe", lambda e, kc=kc, q=q, tb=tb, bank=bank: e.transpose(
                        out=psb[bank][:, q * 128:(q + 1) * 128], in_=hb[:, kc, tb * 128:(tb + 1) * 128],
                        identity=ident_f[:]), reads=[(hk, kc), "ident_f"], writes=[pkey(bank)])
                if g2 == 0:
                    S.op("dve", lambda e, os_=os_, bank=bank: e.tensor_copy(out=os_[:, 0:512], in_=psb[bank][:, :]),
                         reads=[pkey(bank)], writes=[ok])
                else:
                    S.op("act", lambda e, os_=os_, bank=bank: e.activation(out=os_[:, 512:1024], in_=psb[bank][:, :],
                                                                           func=AF.Copy),
                         reads=[pkey(bank)], writes=[ok])
            r0 = t * T + tb * 128
            S.dma("pool", lambda e, os_=os_, r0=r0: e.dma_start(out=y[r0:r0 + 128, :], in_=os_[:, :]),
                  "yout%d" % (tb % 2), reads=[ok], out=True)

    mod_finish(1, 7)
    prologue_A()
    if stage < 6:
        return _finish()
    for j in range(n_main):
        if j == 0:
            for k_ in ("w1q", "wout1", "wgu1", "wdn1"):
                cast_weight(k_)
        hb = hbuf[j % 2]
        hk = "hbuf%d" % (j % 2)
        NJ = T if j < 8 else 128
        load_T(xl[j * T:j * T + NJ, :], NJ // 128, hb, hk)
        norm_mod(hb, hk, NJ, 0, 0, 1, 0, abuf[0], "abuf0")
        l0_q_side(abuf[0], "abuf0", NJ, j * T, True)
        if stage == 61:
            return _finish()
        chunks = [(c * 1024, 8) for c in range(8)] + [(S_FULL, 2)]
        mla_heads_tile(NJ, chunks, abuf[1], "abuf1")
        if stage == 62:
            if j == 0:
                dump("h1t0", lambda kc: (abuf[1][:, kc, :], [("abuf1", kc)]), 4, T)
            return _finish()
        na_heads_tile(j, NJ, abuf[1], "abuf1", latent=True)
        if stage in (63, 631):
            if j == 0:
                dump("h1t0", lambda kc: (abuf[1][:, kc, :], [("abuf1", kc)]), 8, T)
            return _finish()
        l0_finish(hb, hk, NJ, 0, abuf[1], "abuf1", abuf[0], "abuf0")
        if j == 0:
            dump("h1t0", lambda kc: (hb[:, kc, :], [(hk, kc)]), 8, T)
        if stage == 64:
            return _finish()
        l1_kv(j, hb, hk, NJ)
        if j >= 1:
            l1_main(j - 1, hbuf[(j - 1) % 2], "hbuf%d" % ((j - 1) % 2))

    S.emit(st)
    st.close()
    ninst = {e: len(S.streams[e]) for e in ENGS}
    print("instructions per engine:", ninst)
    return nc


_CACHE = {}


def kernel(**inputs):
    inp = {k: np.asarray(v) for k, v in inputs.items()}
    w, sh = _prep_shared(inp)
    in_maps = []
    for cid in range(8):
        m = _prep_core(inp, cid, sh)
        m.update(w)
        m.update(sh)
        in_maps.append(m)
    if "nc" not in _CACHE:
        _CACHE["nc"] = build_program()
    res = run_bass_kernel_spmd(_CACHE["nc"], in_maps, core_ids=list(range(8)))
    out = np.zeros((4, S_FULL, D), np.float32)
    for cid in range(8):
        b, half = cid // 2, cid % 2
        yl = np.asarray(res.results[cid]["y"], dtype=np.float32)
        if half == 0:
            out[b, :OWN] = yl
        else:
            out[b, OWN:] = yl[::-1]
    return out
```

```python
from contextlib import ExitStack
import numpy as np
import concourse.bass as bass
import concourse.mybir as mybir
from concourse.bass_utils import run_bass_kernel_spmd

F32 = mybir.dt.float32
BF16 = mybir.dt.bfloat16
AF = mybir.ActivationFunctionType
ALU = mybir.AluOpType

D = 1024
S_FULL = 8192
CTX = 256
T = 512
NT_MAIN = 9
NT_NAKV = 9
OWN = 4096
EPS = 1e-6
DFF = 2816
NEG = -1e30

ENGS = ("pe", "act", "dve", "pool", "sp")


class Sched:
    def __init__(self, nc):
        self.nc = nc
        self.streams = {e: [] for e in ENGS}
        self.writer = {}
        self.readers = {}
        self.known = {e: {} for e in ENGS}
        self.dma_count = {}
        self.sig = {e: set() for e in ENGS}
        self.snap = {}
        self.out_slots = set()

    def _knows(self, eng, ev):
        k = self.known[eng]
        if ev[0] == 'c':
            return k.get(ev[1], -1) >= ev[2]
        return k.get(('d', ev[1]), 0) >= ev[2]

    def _learn(self, eng, ev):
        k = self.known[eng]
        key = ev[1] if ev[0] == 'c' else ('d', ev[1])
        if k.get(key, -1) < ev[2]:
            k[key] = ev[2]
        s = self.snap.get(ev)
        if s:
            for kk, vv in s.items():
                if k.get(kk, -1) < vv:
                    k[kk] = vv

    def _deps(self, eng, reads, writes):
        evs = []
        for key in reads:
            w = self.writer.get(key)
            if w is not None:
                evs.append(w)
        for key in writes:
            w = self.writer.get(key)
            if w is not None:
                evs.append(w)
            evs.extend(self.readers.get(key, ()))
        waits = []
        for ev in evs:
            if ev[0] == 'c' and ev[1] == eng and eng == 'pe':
                continue
            if self._knows(eng, ev):
                continue
            waits.append(ev)
            if ev[0] == 'c':
                self.sig[ev[1]].add(ev[2])
            self._learn(eng, ev)
        return waits

    def _record(self, ev, reads, writes):
        for key in reads:
            self.readers.setdefault(key, []).append(ev)
        for key in writes:
            self.writer[key] = ev
            self.readers[key] = []

    def op(self, eng, fn, reads=(), writes=()):
        waits = self._deps(eng, reads, writes)
        idx = len(self.streams[eng])
        ev = ('c', eng, idx)
        self.snap[ev] = {k: v for k, v in self.known[eng].items() if not isinstance(k, tuple)}
        self.streams[eng].append(('op', fn, waits, idx))
        self._record(ev, reads, writes)
        return ev

    def dma(self, eng, fn, slot, reads=(), writes=(), out=False):
        waits = self._deps(eng, reads, writes)
        n = self.dma_count.get(slot, 0) + 1
        self.dma_count[slot] = n
        ev = ('d', slot, n)
        self.snap[ev] = {k: v for k, v in self.known[eng].items() if not isinstance(k, tuple)}
        self.streams[eng].append(('dma', fn, waits, slot))
        self._record(ev, reads, writes)
        if out:
            self.out_slots.add(slot)
        return ev

    def emit(self, stack):
        nc = self.nc
        csem = {e: stack.enter_context(nc.semaphore("c_" + e)) for e in ENGS if e != "sp"}
        dsem = {s: stack.enter_context(nc.semaphore("d_%s" % (s,))) for s in self.dma_count}
        rank = {}
        for e in ENGS:
            srt = sorted(self.sig[e])
            rank[e] = {i: r + 1 for r, i in enumerate(srt)}
        block = stack.enter_context(nc.Block())

        def run(ename, engobj):
            for kind, fn, waits, x in self.streams[ename]:
                for ev in waits:
                    if ev[0] == 'c':
                        engobj.wait_ge(csem[ev[1]], rank[ev[1]][ev[2]])
                    else:
                        engobj.wait_ge(dsem[ev[1]], 16 * ev[2])
                ins = fn(engobj)
                if kind == 'dma':
                    ins.then_inc(dsem[x], 16)
                elif x in rank[ename]:
                    ins.then_inc(csem[ename], 1)
            if ename == "sp":
                for s in sorted(self.out_slots, key=str):
                    engobj.wait_ge(dsem[s], 16 * self.dma_count[s])

        @block.tensor
        def _(e):
            run("pe", e)

        @block.scalar
        def _(e):
            run("act", e)

        @block.vector
        def _(e):
            run("dve", e)

        @block.gpsimd
        def _(e):
            run("pool", e)

        @block.sync
        def _(e):
            run("sp", e)


def _rope_tables(pos_r, pos_c, dim):
    half = dim // 2
    inv = (10000.0 ** (-np.arange(0, half, 2, dtype=np.float32) / half)).astype(np.float32)
    nf = half // 2
    C = np.zeros((dim, pos_r.shape[0]), np.float32)
    Sg = np.zeros((dim, pos_r.shape[0]), np.float32)
    for f in range(dim):
        pos = pos_r if f < half else pos_c
        i = f % half
        ang = pos.astype(np.float32) * inv[i % nf]
        C[f] = np.cos(ang)
        Sg[f] = -np.sin(ang) if i < nf else np.sin(ang)
    return C, Sg


def _swap_idx(dim):
    half = dim // 2
    nf = half // 2
    idx = np.zeros(dim, np.int64)
    for f in range(dim):
        base = 0 if f < half else half
        i = f % half
        idx[f] = base + (i + nf if i < nf else i - nf)
    return idx


def _na_tables(rel_bias, rev):
    kc = np.arange(64)[:, None]
    qc = np.arange(64)[None, :]
    if rev:
        kt, qt = 63 - kc, 63 - qc
    else:
        kt, qt = kc, qc
    q_start = np.clip(qt - 8, 0, 48)
    col_ok = (kt >= q_start) & (kt < q_start + 16)
    dc_idx = np.clip(kt - qt + 15, 0, 30)
    bias = np.zeros((8, 2, 64, 15, 64), np.float32)
    mask = np.zeros((2, 64, 15, 64), np.float32)
    for e in range(15):
        dr_l = 7 - e
        dr_t = -dr_l if rev else dr_l
        g = rel_bias[:, dr_t + 7][:, dc_idx]
        bias[:, 0, :, e, :] = g
        bias[:, 1, :, e, :] = g
        row_ok = (-4 <= dr_t <= 3)
        mask[0, :, e, :] = np.where(col_ok & row_ok, 0.0, NEG)
        mask[1, :, e, :] = np.where(col_ok, 0.0, NEG)
    return bias.reshape(8, 2, 64, 960), mask.reshape(2, 64, 960)


def _prep_shared(inp):
    f = np.float32
    w = {}
    wi = inp["even_w_in"][0]
    kr = wi[:, 384:416]
    sw32 = _swap_idx(32)
    w["w0kv"] = np.concatenate([wi[:, 256:384], kr, kr[:, sw32]], 1)
    w["w0nak"] = wi[:, 928:1440]
    w["w0nav"] = wi[:, 1440:1952]
    w["w0q"] = np.concatenate([wi[:, 0:256], wi[:, 416:928]], 1)
    qu = inp["mla_w_q_up"][0].reshape(256, 8, 96)
    nope = qu[:, :, :64].reshape(256, 512)
    rope = qu[:, :, 64:]
    rsw = rope[:, :, sw32]
    nope3 = qu[:, :, :64]
    parts = []
    for h in range(8):
        parts += [nope3[:, h], rope[:, h], nope3[:, h], rsw[:, h]]
    w["wqup"] = np.concatenate(parts, 1)
    w["wukP"] = inp["mla_w_uk"][0].transpose(1, 0, 2).reshape(128, 512)
    w["wuv"] = inp["mla_w_uv"][0].transpose(1, 0, 2).reshape(128, 512)
    w["wout0"] = inp["even_w_out"][0]
    for l in range(2):
        w["wgu%d" % l] = inp["ffn_w_gate_up"][l]
        w["wdn%d" % l] = inp["ffn_w_down"][l]
    wo = inp["odd_w_in"][0]
    sw64 = _swap_idx(64)
    q = wo[:, :1024].reshape(1024, 16, 64)
    qs = q[:, :, sw64]
    w["w1q"] = np.concatenate([np.concatenate([q[:, 2 * c:2 * c + 2].reshape(1024, 128),
                                               qs[:, 2 * c:2 * c + 2].reshape(1024, 128)], 1) for c in range(8)], 1)
    k = wo[:, 1024:1152].reshape(1024, 2, 64)
    ksw = k[:, :, sw64]
    w["w1k"] = np.concatenate([k[:, 0], k[:, 0], ksw[:, 0], ksw[:, 0], k[:, 1], k[:, 1], ksw[:, 1], ksw[:, 1]], 1)
    v = wo[:, 1152:1280].reshape(1024, 2, 64)
    w["w1v"] = np.concatenate([v[:, 0], v[:, 1]], 1)
    w["wout1"] = inp["odd_w_out"][0]
    w = {k_: np.ascontiguousarray(v_, dtype=f) for k_, v_ in w.items()}
    sh = {}
    sh["modw"] = np.ascontiguousarray(inp["mod_w"], dtype=f)
    mb = inp["mod_b"].reshape(2, 48, 128).transpose(2, 0, 1)
    sh["modb"] = np.ascontiguousarray(np.repeat(mb[:, :, :, None], 2, axis=3), dtype=f)
    sh["qnorm"] = np.ascontiguousarray(inp["mla_q_norm"][0].reshape(2, 128).T, dtype=f)
    sh["kvnorm"] = np.ascontiguousarray(inp["mla_kv_norm"][0].reshape(1, 128).T, dtype=f)
    sh["fnorm"] = np.ascontiguousarray(inp["final_norm"].reshape(8, 128).T, dtype=f)
    sh["sinks"] = np.ascontiguousarray(np.repeat(inp["swa_sinks"][0][None, :], 128, axis=0), dtype=f)
    kk = np.arange(128)[:, None]
    qq = np.arange(128)[None, :]
    mlo = np.where(qq <= kk, 0.0, NEG).astype(f)
    mhi = np.where(kk <= qq, 0.0, NEG).astype(f)
    sh["swamask"] = np.ascontiguousarray(np.stack([np.tile(mlo, (1, 4)), np.tile(mhi, (1, 4))]), dtype=f)
    return w, sh


WSHAPES = {"w0kv": (1024, 192), "w0nak": (1024, 512), "w0nav": (1024, 512), "w0q": (1024, 768),
           "wqup": (256, 1536), "wukP": (128, 512), "wuv": (128, 512), "wout0": (1024, 1024),
           "wgu0": (1024, 5632), "wdn0": (2816, 1024), "w1q": (1024, 2048), "w1k": (1024, 512),
           "w1v": (1024, 128), "wout1": (1024, 1024), "wgu1": (1024, 5632), "wdn1": (2816, 1024)}
WORDER = ["w0kv", "w0nak", "w0nav", "w0q", "wqup", "wukP", "wuv", "wout0", "wgu0", "wdn0",
          "w1k", "w1v", "w1q", "wout1", "wgu1", "wdn1"]


def _prep_core(inp, cid, sh):
    f = np.float32
    b, half = cid // 2, cid % 2
    perm = np.arange(S_FULL) if half == 0 else np.arange(S_FULL)[::-1]
    m = {}
    m["xl"] = np.ascontiguousarray(inp["x"][b][perm], dtype=f)
    m["ctxl"] = np.ascontiguousarray(inp["ctx"][b], dtype=f)
    cv = np.stack([inp["c"][b].reshape(8, 128).T, inp["c_ctx"].reshape(8, 128).T], axis=2)
    m["cvec"] = np.ascontiguousarray(cv, dtype=f)
    rows, cols = perm // 64, perm % 64
    C, Sg = _rope_tables(rows, cols, 32)
    m["ropeMc"] = np.ascontiguousarray(np.tile(C, (3, 1)), dtype=f)
    m["ropeMs"] = np.ascontiguousarray(np.tile(Sg, (3, 1)), dtype=f)
    C, Sg = _rope_tables(rows[:5120], cols[:5120], 64)
    m["ropeSc"] = np.ascontiguousarray(np.tile(C, (2, 1)), dtype=f)
    m["ropeSs"] = np.ascontiguousarray(np.tile(Sg, (2, 1)), dtype=f)
    nb, nm = _na_tables(inp["na_rel_bias"][0], half == 1)
    m["nabias"] = np.ascontiguousarray(nb, dtype=f)
    m["namask"] = np.ascontiguousarray(nm, dtype=f)
    return m


def build_program(n_main=NT_MAIN, dbg=None, stage=9):
    nc = bass.Bass("TRN2", target_bir_lowering=False)
    S = Sched(nc)
    st = ExitStack()

    def din(name, shape, dt=F32):
        return nc.dram_tensor(name, list(shape), dt, kind="ExternalInput").ap()

    xl = din("xl", [S_FULL, D])
    ctxl = din("ctxl", [CTX, D])
    cvec = din("cvec", [128, 8, 2])
    modw = din("modw", [2, D, 6 * D])
    modb = din("modb", [128, 2, 48, 2])
    qnorm_d = din("qnorm", [128, 2])
    kvnorm_d = din("kvnorm", [128, 1])
    fnorm_d = din("fnorm", [128, 8])
    sinks_d = din("sinks", [128, 16])
    swamask_d = din("swamask", [2, 128, 512])
    ropeMc = din("ropeMc", [96, S_FULL])
    ropeMs = din("ropeMs", [96, S_FULL])
    ropeSc = din("ropeSc", [128, 5120])
    ropeSs = din("ropeSs", [128, 5120])
    nabias = din("nabias", [8, 2, 64, 960])
    namask = din("namask", [2, 64, 960])
    wf = {k: din(k, WSHAPES[k]) for k in WORDER}
    y = nc.dram_tensor("y", [OWN, D], F32, kind="ExternalOutput").ap()
    dbg_out = {}
    if dbg:
        for k, shp in dbg.items():
            dbg_out[k] = nc.dram_tensor("dbg_" + k, list(shp), F32, kind="ExternalOutput").ap()

    wb = {k: nc.dram_tensor("wb_" + k, list(WSHAPES[k]), BF16).ap() for k in WORDER}
    NK = S_FULL + CTX
    khp_d = nc.dram_tensor("khp_d", [4, 128, NK], BF16).ap()
    kr_d = nc.dram_tensor("kr_d", [32, NK], BF16).ap()
    vh_d = nc.dram_tensor("vh_d", [NK, 768], BF16).ap()
    NAT = NT_NAKV * T
    nakT_d = nc.dram_tensor("nakT_d", [128, 4, NAT], BF16).ap()
    nav_d = nc.dram_tensor("nav_d", [NAT, 768], BF16).ap()
    nastr_d = nc.dram_tensor("nastr_d", [8, 2, 64, 960], BF16).ap()

    def sb(name, shape, dt):
        return st.enter_context(nc.sbuf_tensor(name, list(shape), dt))

    ident_f = sb("ident_f", [128, 128], F32)
    ident_b = sb("ident_b", [128, 128], BF16)
    ones_b = sb("ones_b", [128, 128], BF16)
    modv = sb("modv", [128, 2 * 6 * 8 * 2], F32)
    onep = sb("onep", [128, 2 * 6 * 8 * 2], F32)
    silc = sb("silc", [128, 16], F32)
    qnorm = sb("qnorm_s", [128, 2], F32)
    kvnorm = sb("kvnorm_s", [128, 1], F32)
    fnorm = sb("fnorm_s", [128, 8], F32)
    esink = sb("esink", [128, 16], F32)
    swam = sb("swam", [128, 2, 512], BF16)
    hbuf = [sb("hbuf%d" % i, [128, 8, T], F32) for i in range(2)]
    hctx = hbuf[1]
    xstage = [sb("xstage%d" % i, [128, D], F32) for i in range(2)]
    ostage = xstage
    abuf = [sb("abuf%d" % i, [128, 8, T], BF16) for i in range(2)]
    NWS = 3
    wslot = [sb("wslot%d" % i, [128, 4096], BF16) for i in range(NWS)]
    sqb = sb("sqb", [128, 2, T], BF16)
    tmpf = sb("tmpf", [128, 2, T], F32)
    rstd = sb("rstd", [128, T], F32)
    cq = tmpf
    cqn = sb("cqn", [128, 2, T], BF16)
    uni = sb("uni", [128, 8 * T], BF16)
    qnope = uni[:, 0:4 * T].rearrange("p (k t) -> p k t", k=4)
    Qh = sb("Qh", [96, 8, T], BF16)
    pswap = sb("pswap", [128, 128], F32)
    vstage = sb("vstage", [128, 768], BF16)
    recb = sb("recb", [128, 2, T], F32)
    ckvf = recb[:, 0, :]
    naq = uni[:, 4 * T:8 * T].rearrange("p (k t) -> p k t", k=4)
    nakw = sb("nakw", [128, 2, 1024], BF16)
    navw = sb("navw", [64, 2, 16 * 192], BF16)
    nastrip = sb("nastrip", [128, 2, 960], BF16)
    naedge = sb("naedge", [128, 1, 960], BF16)
    nakc = sb("nakc", [128, 4, CTX], BF16)
    navc = sb("navc", [128, 2, 768], BF16)
    actT = sb("actT", [128, 22, T], BF16)
    silt = sb("silt", [128, 2, T], F32)
    q1 = uni[:, :].rearrange("p (k t) -> p k t", k=8)
    k1 = sb("k1", [128, 2, 3 * T], BF16)
    v1 = sb("v1", [128, 12, 384], BF16)
    k1c = sb("k1c", [128, 2, CTX], BF16)
    v1c = sb("v1c", [128, 2, 384], BF16)
    pT = [sb("pT%d" % i, [128, T], BF16) for i in range(3)]
    ropeC = sb("ropeC", [128, T], F32)
    ropeS = sb("ropeS", [128, 2, T], F32)
    mck = [sb("mck%d" % i, [96, 1024], BF16) for i in range(2)]
    mv = [sb("mv%d" % i, [128, 8, 192], BF16) for i in range(2)]

    psb = [st.enter_context(nc.psum_tensor("psb%d" % i, [128, 512], F32)) for i in range(8)]
    print("sbuf bytes remaining/partition:", nc.sbuf_bytes_remaining)

    cnt = {"mm": 0, "ws": 0, "pt": 0, "uid": 0}

    def mm_bank():
        b = cnt["mm"] % 4
        cnt["mm"] += 1
        return b

    def pkey(b):
        return "ps%d" % b

    def next_pt():
        i = cnt["pt"] % 3
        cnt["pt"] += 1
        return i

    def mvi(l, w, kc, col):
        i = ((l * 6 + w) * 8 + kc) * 2 + col
        return i

    S.op("pool", lambda e: e.memset(ident_f[:], 1.0), writes=["ident_f"])
    S.op("pool", lambda e: e.affine_select(out=ident_f[:], in_=ident_f[:], pattern=[[-1, 128]],
                                           compare_op=ALU.is_equal, fill=0.0, base=0, channel_multiplier=1),
         reads=["ident_f"], writes=["ident_f"])
    S.op("dve", lambda e: e.tensor_copy(out=ident_b[:], in_=ident_f[:]), reads=["ident_f"], writes=["ident_b"])
    S.op("pool", lambda e: e.memset(ones_b[:], 1.0), writes=["ones_b"])
    S.op("pool", lambda e: e.memset(pswap[:], 1.0), writes=["pswap"])
    S.op("pool", lambda e: e.affine_select(out=pswap[:], in_=pswap[:], pattern=[[-1, 128]],
                                           compare_op=ALU.is_equal, fill=0.0, base=64, channel_multiplier=1),
         reads=["pswap"], writes=["pswap"])
    S.op("pool", lambda e: e.memset(xstage[0][:, 0:128], 1.0), writes=["xstage0"])
    S.op("pool", lambda e: e.affine_select(out=xstage[0][:, 0:128], in_=xstage[0][:, 0:128], pattern=[[-1, 128]],
                                           compare_op=ALU.is_equal, fill=0.0, base=-64, channel_multiplier=1),
         reads=["xstage0"], writes=["xstage0"])
    S.op("pool", lambda e: e.tensor_tensor(out=pswap[:], in0=pswap[:], in1=xstage[0][:, 0:128], op=ALU.add),
         reads=["pswap", "xstage0"], writes=["pswap"])
    S.op("pool", lambda e: e.memset(vstage[:, :].rearrange("p (i u) -> p i u", u=192)[:, :, 64:128], 1.0),
         writes=["vstage"])
    S.op("pool", lambda e: e.memset(navc[:, :, :].rearrange("p b (i u) -> p b i u", u=192)[:, :, :, 64:128], 1.0),
         writes=["navc"])
    S.op("pool", lambda e: e.memset(v1[:, :, :].rearrange("p b (i u) -> p b i u", u=192)[:, :, :, 64:128], 1.0),
         writes=[("v1", 0), ("v1", 1), ("v1", 2)])
    S.op("pool", lambda e: e.memset(v1c[:, :, :].rearrange("p b (i u) -> p b i u", u=192)[:, :, :, 64:128], 1.0),
         writes=["v1c"])

    def cast_weight(k):
        r, c = WSHAPES[k]
        per = r * c // 128
        src = wf[k].rearrange("r c -> (r c)").rearrange("(p n) -> p n", p=128)
        dst = wb[k].rearrange("r c -> (r c)").rearrange("(p n) -> p n", p=128)
        for c0 in range(0, per, 8192):
            c1 = min(per, c0 + 8192)
            S.dma("pool", lambda e, s_=src[:, c0:c1], d_=dst[:, c0:c1]: e.dma_start(out=d_, in_=s_),
                  "wc_" + k, writes=[("wb", k, c0)])

    for k in ("w0kv", "wukP", "wuv", "w0nak", "w0nav"):
        cast_weight(k)
    CAST_AT = {1: ["w0q", "wqup"], 2: ["wout0"], 3: ["wgu0"], 7: ["wdn0"], 10: ["w1k", "w1v"]}

    def ld(dst_ap, src_ap, key, slot):
        S.dma("sp", lambda e: e.dma_start(out=dst_ap, in_=src_ap), slot, writes=[key])

    ld(qnorm[:], qnorm_d, "qnorm", "c0_qnorm")
    ld(kvnorm[:], kvnorm_d, "kvnorm", "c0_kvnorm")
    ld(fnorm[:], fnorm_d, "fnorm", "c0_fnorm")
    ld(esink[:], sinks_d, "esink", "c0_esink")
    ld(silc[:], cvec.rearrange("p k c -> p (k c)"), "silc", "c0_silc")
    S.dma("sp", lambda e: e.dma_start(out=modv[:, 0:192], in_=modb.rearrange("p l m c -> p (l m c)")), "c0_modv",
          writes=[("modv", 0), ("modv", 1)])
    S.op("act", lambda e: e.activation(out=esink[:], in_=esink[:], func=AF.Exp), reads=["esink"], writes=["esink"])
    S.op("act", lambda e: e.activation(out=silc[:], in_=silc[:], func=AF.Silu), reads=["silc"], writes=["silc"])
    for i in range(2):
        S.dma("sp", lambda e, i=i: e.dma_start(out=tmpf[:, i, :], in_=swamask_d[i]), "c1", writes=["tmpf"])
    S.op("act", lambda e: e.activation(out=swam[:, :, :], in_=tmpf[:, :, :], func=AF.Copy),
         reads=["tmpf"], writes=["swam"])


    def _finish():
        S.emit(st)
        st.close()
        print("instructions per engine:", {e: len(S.streams[e]) for e in ENGS})
        return nc
    if stage < 2:
        return _finish()
    def mod_blocks(l, ms, bank):
        for m in ms:
            xs = xstage[m % 2]
            xk = "xstage%d" % (m % 2)
            S.dma("sp", lambda e, xs=xs, m=m: e.dma_start(
                out=xs[:, :].rearrange("p (k c) -> p k c", k=8),
                in_=modw[l].rearrange("(k p) c -> p k c", p=128)[:, :, m * 128:(m + 1) * 128]),
                "xst%d" % (m % 2), writes=[xk])
            for kc in range(8):
                S.op("pe", lambda e, xs=xs, kc=kc, m=m: e.matmul(
                    psb[bank][:, 2 * m:2 * m + 2], lhsT=xs[:, kc * 128:(kc + 1) * 128],
                    rhs=silc[:, 2 * kc:2 * kc + 2], start=(kc == 0), stop=(kc == 7)),
                    reads=[xk, "silc"], writes=[pkey(bank)])

    def mod_finish(l, bank):
        S.op("dve", lambda e: e.tensor_tensor(
            out=modv[:, 96 * l:96 * l + 96], in0=psb[bank][:, 0:96], in1=modv[:, 96 * l:96 * l + 96], op=ALU.add),
            reads=[pkey(bank), ("modv", l)], writes=[("modv", l)])
        S.op("dve", lambda e: e.tensor_scalar_add(out=onep[:, 96 * l:96 * l + 96], in0=modv[:, 96 * l:96 * l + 96],
                                                  scalar1=1.0), reads=[("modv", l)], writes=[("onep", l)])

    bank0 = mm_bank()
    for blk in range(12):
        hbk = hbuf[blk % 2]
        hkk = "hbuf%d" % (blk % 2)
        S.dma("sp", lambda e, hbk=hbk, blk=blk: e.dma_start(
            out=hbk[:, :, :], in_=modw[0].rearrange("(k p) c -> p k c", p=128)[:, :, blk * 512:(blk + 1) * 512]),
            "modld%d" % (blk % 2), writes=[(hkk, kc) for kc in range(8)])
        for mm_ in range(4):
            m = blk * 4 + mm_
            for kc in range(8):
                S.op("pe", lambda e, hbk=hbk, kc=kc, m=m, mm_=mm_: e.matmul(
                    psb[bank0][:, 2 * m:2 * m + 2], lhsT=hbk[:, kc, mm_ * 128:(mm_ + 1) * 128],
                    rhs=silc[:, 2 * kc:2 * kc + 2], start=(kc == 0), stop=(kc == 7)),
                    reads=[(hkk, kc), "silc"], writes=[pkey(bank0)])
    mod_finish(0, bank0)

    if stage < 3:
        return _finish()
    def build_strip(h, v):
        S.dma("sp", lambda e: e.dma_start(out=xstage[0][0:64, 0:960], in_=nabias[h, v]),
              "xst0", writes=["xstage0"])
        S.dma("sp", lambda e: e.dma_start(out=xstage[1][0:64, 0:960], in_=namask[v]),
              "xst1", writes=["xstage1"])
        S.op("pool", lambda e: e.tensor_scalar_mul(out=xstage[0][0:64, 0:960], in0=xstage[0][0:64, 0:960], scalar1=8.0),
             reads=["xstage0"], writes=["xstage0"])
        S.op("pool", lambda e: e.tensor_tensor(out=naedge[0:64, 0, :], in0=xstage[0][0:64, 0:960],
                                               in1=xstage[1][0:64, 0:960], op=ALU.add),
             reads=["xstage0", "xstage1"], writes=["naedge0"])
        S.dma("pool", lambda e: e.dma_start(out=nastr_d[h, v], in_=naedge[0:64, 0, :]), "nb3",
              reads=["naedge0"], writes=["nastr_d"])

    def load_T(src_rows_ap, nblk, dst, dkey):
        for tb in range(nblk):
            xs = xstage[tb % 2]
            xk = "xstage%d" % (tb % 2)
            S.dma("sp", lambda e, xs=xs, tb=tb: e.dma_start(out=xs[:, :], in_=src_rows_ap[tb * 128:(tb + 1) * 128, :]),
                  "xst%d" % (tb % 2), writes=[xk])
            for g in range(2):
                bank = mm_bank()
                for q in range(4):
                    kc = 4 * g + q
                    S.op("pe", lambda e, xs=xs, kc=kc, q=q, bank=bank: e.transpose(
                        out=psb[bank][:, q * 128:(q + 1) * 128], in_=xs[:, kc * 128:(kc + 1) * 128],
                        identity=ident_f[:]), reads=[xk, "ident_f"], writes=[pkey(bank)])
                eng = "dve" if g == 0 else "act"
                if eng == "dve":
                    S.op("dve", lambda e, g=g, tb=tb, bank=bank: e.tensor_copy(
                        out=dst[:, 4 * g:4 * g + 4, tb * 128:(tb + 1) * 128],
                        in_=psb[bank][:, :].rearrange("p (q t) -> p q t", q=4)),
                        reads=[pkey(bank)], writes=[(dkey, 4 * g + q) for q in range(4)])
                else:
                    S.op("act", lambda e, g=g, tb=tb, bank=bank: e.activation(
                        out=dst[:, 4 * g:4 * g + 4, tb * 128:(tb + 1) * 128],
                        in_=psb[bank][:, :].rearrange("p (q t) -> p q t", q=4), func=AF.Copy),
                        reads=[pkey(bank)], writes=[(dkey, 4 * g + q) for q in range(4)])

    def load_T_big(src_rows_ap, dst, dkey, stg, stgkey):
        stgv = stg[:, :, :].rearrange("p k t -> p (k t)").rearrange("p (b d) -> p b d", b=4)
        skeys = [(stgkey, kc) for kc in range(8)]
        S.dma("sp", lambda e: e.dma_start(out=stgv, in_=src_rows_ap.rearrange("(b p) d -> p b d", p=128)),
              "xbig", writes=skeys)
        for tb in range(4):
            for g in range(2):
                bank = mm_bank()
                for q in range(4):
                    kc = 4 * g + q
                    S.op("pe", lambda e, kc=kc, q=q, bank=bank, tb=tb: e.transpose(
                        out=psb[bank][:, q * 128:(q + 1) * 128], in_=stgv[:, tb, kc * 128:(kc + 1) * 128],
                        identity=ident_f[:]), reads=skeys + ["ident_f"], writes=[pkey(bank)])
                if g == 0:
                    S.op("dve", lambda e, g=g, tb=tb, bank=bank: e.tensor_copy(
                        out=dst[:, 4 * g:4 * g + 4, tb * 128:(tb + 1) * 128],
                        in_=psb[bank][:, :].rearrange("p (q t) -> p q t", q=4)),
                        reads=[pkey(bank)], writes=[(dkey, 4 * g + q) for q in range(4)])
                else:
                    S.op("act", lambda e, g=g, tb=tb, bank=bank: e.activation(
                        out=dst[:, 4 * g:4 * g + 4, tb * 128:(tb + 1) * 128],
                        in_=psb[bank][:, :].rearrange("p (q t) -> p q t", q=4), func=AF.Copy),
                        reads=[pkey(bank)], writes=[(dkey, 4 * g + q) for q in range(4)])

    def rms_stats(src_fn, skeys, KC, N, nfeat):
        bank = mm_bank()
        for kc in range(KC):
            S.op("act", lambda e, kc=kc: e.activation(out=sqb[:, kc % 2, :N], in_=src_fn(kc), func=AF.Square),
                 reads=[skeys[kc]], writes=[("sqb", kc % 2)])
            S.op("pe", lambda e, kc=kc, bank=bank: e.matmul(
                psb[bank][:, :N], lhsT=ones_b[:, :], rhs=sqb[:, kc % 2, :N], start=(kc == 0), stop=(kc == KC - 1)),
                reads=[("sqb", kc % 2), "ones_b"], writes=[pkey(bank)])
        S.op("dve", lambda e, bank=bank: e.tensor_scalar(
            out=rstd[:, :N], in0=psb[bank][:, :N], scalar1=1.0 / nfeat, scalar2=EPS, op0=ALU.mult, op1=ALU.add),
            reads=[pkey(bank)], writes=["rstd"])
        S.op("act", lambda e: e.activation(out=rstd[:, :N], in_=rstd[:, :N], func=AF.Sqrt),
             reads=["rstd"], writes=["rstd"])
        S.op("dve", lambda e: e.reciprocal(out=rstd[:, :N], in_=rstd[:, :N]), reads=["rstd"], writes=["rstd"])

    def norm_mod(hsrc, hkey, N, l, wsh, wsc, col, dst, dkey, eng="dve"):
        rms_stats(lambda kc: hsrc[:, kc, :N], [(hkey, kc) for kc in range(8)], 8, N, D)
        for kc in range(8):
            i_sc = mvi(l, wsc, kc, col)
            i_sh = mvi(l, wsh, kc, col)
            if eng == "pool" and kc % 2 == 1:
                S.op("pool", lambda e, kc=kc, i_sc=i_sc: e.tensor_scalar_mul(
                    out=tmpf[:, kc % 2, :N], in0=hsrc[:, kc, :N], scalar1=onep[:, i_sc:i_sc + 1]),
                    reads=[(hkey, kc), ("onep", l)], writes=[("tmpf", kc % 2)])
                S.op("pool", lambda e, kc=kc: e.tensor_tensor(
                    out=tmpf[:, kc % 2, :N], in0=tmpf[:, kc % 2, :N], in1=rstd[:, :N], op=ALU.mult),
                    reads=[("tmpf", kc % 2), "rstd"], writes=[("tmpf", kc % 2)])
            else:
                S.op("dve", lambda e, kc=kc, i_sc=i_sc: e.scalar_tensor_tensor(
                    out=tmpf[:, kc % 2, :N], in0=hsrc[:, kc, :N], scalar=onep[:, i_sc:i_sc + 1], in1=rstd[:, :N],
                    op0=ALU.mult, op1=ALU.mult),
                    reads=[(hkey, kc), "rstd", ("onep", l)], writes=[("tmpf", kc % 2)])
            S.op("act", lambda e, kc=kc, i_sh=i_sh: e.activation(
                out=dst[:, kc, :N], in_=tmpf[:, kc % 2, :N], func=AF.Identity, bias=modv[:, i_sh:i_sh + 1], scale=1.0),
                reads=[("tmpf", kc % 2), ("modv", l)], writes=[(dkey, kc)])

    def wload(wname, KC, c0, ncols, k0=0):
        s = cnt["ws"] % NWS
        cnt["ws"] += 1
        view = wslot[s][:, 0:KC * ncols].rearrange("p (k c) -> p k c", k=KC)
        P = min(128, WSHAPES[wname][0])
        src = wb[wname].rearrange("(k p) c -> p k c", p=P)[:, k0:k0 + KC, c0:c0 + ncols]
        S.dma("sp", lambda e: e.dma_start(out=view[:P], in_=src), "ws%d" % s,
              reads=[("wb", wname, q0) for q0 in range(0, WSHAPES[wname][0] * WSHAPES[wname][1] // 128, 8192)],
              writes=["wslot%d" % s])
        return view, "wslot%d" % s

    def linear(wname, KC, rhs_fn, rkeys, N, mchunks, evac, blockcols=None, kparts=None):
        if blockcols is None:
            blockcols = max(128, (4096 // KC) // 128 * 128)
        i = 0
        while i < len(mchunks):
            c0 = mchunks[i][0]
            j = i
            while j < len(mchunks) and mchunks[j][0] + mchunks[j][1] - c0 <= blockcols:
                j += 1
            ncols = mchunks[j - 1][0] + mchunks[j - 1][1] - c0
            view, wkey = wload(wname, KC, c0, ncols)
            for ii in range(i, j):
                mc0, mn = mchunks[ii]
                bank = mm_bank()
                for kc in range(KC):
                    kp = 128 if kparts is None else kparts
                    S.op("pe", lambda e, view=view, kc=kc, mc0=mc0, mn=mn, c0=c0, bank=bank, kp=kp: e.matmul(
                        psb[bank][:mn, :N], lhsT=view[:kp, kc, mc0 - c0:mc0 - c0 + mn], rhs=rhs_fn(kc),
                        start=(kc == 0), stop=(kc == KC - 1)),
                        reads=[wkey, rkeys[kc]], writes=[pkey(bank)])
                evac(ii, psb[bank][:mn, :N], pkey(bank))
            i = j

    def linear_tm(wname, KC, lhs_fn, lkeys, ntb, tbsz, ncols, evac):
        view, wkey = wload(wname, KC, 0, ncols)
        for tb in range(ntb):
            bank = mm_bank()
            for kc in range(KC):
                S.op("pe", lambda e, kc=kc, tb=tb, bank=bank: e.matmul(
                    psb[bank][:tbsz, :ncols], lhsT=lhs_fn(kc, tb), rhs=view[:, kc, :ncols],
                    start=(kc == 0), stop=(kc == KC - 1)),
                    reads=[wkey, lkeys[kc]], writes=[pkey(bank)])
            evac(tb, psb[bank][:tbsz, :ncols], pkey(bank))

    def copy_evac(dst_fn, dkey_fn, alt=True):
        def ev(i, ps, bk):
            if alt and i % 2 == 1:
                S.op("act", lambda e: e.activation(out=dst_fn(i), in_=ps, func=AF.Copy),
                     reads=[bk], writes=[dkey_fn(i)])
            else:
                S.op("dve", lambda e: e.tensor_copy(out=dst_fn(i), in_=ps), reads=[bk], writes=[dkey_fn(i)])
        return ev

    ropeT = [sb("ropeT%d" % i, [128, T], F32) for i in range(1)]

    NPT = len(pT)

    def run_attn(blocks, skew=2, defer=3):
        deferred = []
        nb = len(blocks)
        for k in range(nb + skew):
            if k < nb:
                b = blocks[k]
                if b.get("pre"):
                    b["pre"]()
                bank = mm_bank()
                np_, c0, c1 = b["np_"], b["c0"], b["c1"]
                n = len(b["qk"])
                for i, (lhsT, rhs, rk, oc0, oc1) in enumerate(b["qk"]):
                    o_ap = psb[bank][:np_, oc0:oc1]
                    if len(rhs.shape) == 3:
                        o_ap = o_ap.rearrange("p (h t) -> p h t", h=rhs.shape[1])
                    S.op("pe", lambda e, lhsT=lhsT, rhs=rhs, i=i, o_ap=o_ap, n=n: e.matmul(
                        o_ap, lhsT=lhsT, rhs=rhs, start=(i == 0), stop=(i == n - 1)),
                        reads=rk, writes=[pkey(bank)])
                pi = cnt["pt"] % NPT
                cnt["pt"] += 1
                b["pi"] = pi
                S.op("act", lambda e, bank=bank, pi=pi, np_=np_, c0=c0, c1=c1, sc=b["scale"]: e.activation(
                    out=pT[pi][:np_, c0:c1], in_=psb[bank][:np_, c0:c1], func=AF.Exp, scale=sc),
                    reads=[pkey(bank)], writes=["pT%d" % pi])
            still = []
            for (when, fn) in deferred:
                if when <= k:
                    fn()
                else:
                    still.append((when, fn))
            deferred = still
            kk = k - skew
            if kk >= 0:
                b = blocks[kk]
                pi, np_, c0, c1 = b["pi"], b["np_"], b["c0"], b["c1"]
                acc = b["acc"]
                S.op("pe", lambda e, b=b, pi=pi, np_=np_, c0=c0, c1=c1, acc=acc: e.matmul(
                    psb[acc][:, c0:c1], lhsT=b["v_lhsT"], rhs=pT[pi][:np_, c0:c1], start=b["first"], stop=b["last"]),
                    reads=["pT%d" % pi] + b["vkeys"], writes=[pkey(acc)])
                if b.get("epi"):
                    fn = b["epi"]()
                    if fn is not None:
                        deferred.append((k + defer, fn))
        for (_, fn) in deferred:
            fn()

    acc_rot = {"i": 0}

    def next_acc():
        i = acc_rot["i"] % 4
        acc_rot["i"] += 1
        return 4 + i

    def pair_epilogue(accA, accB, N, final_fn, sink_heads=None):
        for (acc, lo) in ((accA, 64), (accB, 0)):
            if sink_heads is None:
                S.op("dve", lambda e, acc=acc, lo=lo: e.reciprocal(out=recb[lo:lo + 64, 0, :N], in_=psb[acc][lo:lo + 64, :N]),
                     reads=[pkey(acc)], writes=[("recb", 0)])
            else:
                heads = sink_heads[0] if lo == 64 else sink_heads[1]
                for hh, head in enumerate(heads):
                    S.op("dve", lambda e, acc=acc, lo=lo, hh=hh, head=head: e.tensor_scalar_add(
                        out=recb[lo:lo + 64, 0, hh * 128:(hh + 1) * 128], in0=psb[acc][lo:lo + 64, hh * 128:(hh + 1) * 128],
                        scalar1=esink[lo:lo + 64, head:head + 1]),
                        reads=[pkey(acc), "esink"], writes=[("recb", 0)])
                S.op("dve", lambda e, lo=lo: e.reciprocal(out=recb[lo:lo + 64, 0, :N], in_=recb[lo:lo + 64, 0, :N]),
                     reads=[("recb", 0)], writes=[("recb", 0)])

        def pe_part():
            bank = mm_bank()
            S.op("pe", lambda e: e.matmul(psb[bank][:, :N], lhsT=pswap[:, :], rhs=recb[:, 0, :N], start=True, stop=True),
                 reads=[("recb", 0), "pswap"], writes=[pkey(bank)])
            S.op("act", lambda e: e.activation(out=recb[:, 1, :N], in_=psb[bank][:, :N], func=AF.Copy),
                 reads=[pkey(bank)], writes=[("recb", 1)])
            final_fn(recb[:, 1, :N], ("recb", 1))
        return pe_part

    KV_KEYS = []
    for t_ in range(17):
        k0_ = t_ * T
        KV_KEYS.append(("kr_d", k0_))
        KV_KEYS += [("khp_d", k0_, i_) for i_ in range(4)]
        KV_KEYS += [("vh_d", k0_, tb_) for tb_ in range(4)]

    def mla_heads_tile(N, key_chunks, cat, catkey):
        scale = 96.0 ** -0.5
        blocks = []
        nblk_tot = sum(nb for (_, nb) in key_chunks)
        accs = {}
        for h in range(8):
            i, par = h // 2, h % 2
            acc = next_acc()
            accs[h] = acc
            done = 0
            for (k0, nb) in key_chunks:
                s = cnt["uid"] % 2
                cnt["uid"] += 1

                def pre(s=s, k0=k0, nb=nb, h=h, i=i):
                    S.dma("sp", lambda e: e.dma_start(out=mck[s][0:64, :nb * 128],
                                                      in_=khp_d[i, 64 * (h % 2):64 * (h % 2) + 64, k0:k0 + nb * 128]),
                          "mck%d" % s, reads=KV_KEYS, writes=["mck%d" % s])
                    S.dma("sp", lambda e: e.dma_start(out=mck[s][64:96, :nb * 128], in_=kr_d[:, k0:k0 + nb * 128]),
                          "mck%d" % s, reads=KV_KEYS, writes=["mck%d" % s])
                    S.dma("sp", lambda e: e.dma_start(
                        out=mv[s][:, :nb, :],
                        in_=vh_d[k0:k0 + nb * 128, 192 * i:192 * i + 192].rearrange("(b p) c -> p b c", p=128)),
                        "mv%d" % s, reads=KV_KEYS, writes=["mv%d" % s])
                for kb in range(nb):
                    qk = [(mck[s][:, kb * 128:(kb + 1) * 128], Qh[:, h, :N], ["mck%d" % s, ("Qh", h)], 0, N)]
                    blk = dict(np_=128, c0=0, c1=N, qk=qk, scale=scale,
                               v_lhsT=mv[s][:, kb, 64 * par:64 * par + 128], vkeys=["mv%d" % s],
                               acc=acc, first=(done == 0), last=(done == nblk_tot - 1),
                               pre=(pre if kb == 0 else None))
                    done += 1
                    blocks.append(blk)
            if par == 1:
                def epi(i=i, accA=accs[h - 1], accB=acc):
                    def fin(rec, rkey):
                        for (p0, a_) in ((0, accA), (64, accB)):
                            S.op("dve", lambda e, p0=p0, a_=a_: e.tensor_tensor(
                                out=cat[p0:p0 + 64, i, :N], in0=psb[a_][p0:p0 + 64, :N], in1=rec[p0:p0 + 64, :],
                                op=ALU.mult), reads=[pkey(a_), rkey], writes=[(catkey, i)])
                    return pair_epilogue(accA, accB, N, fin)
                blocks[-1]["epi"] = epi
        run_attn(blocks)

    NA_KEYS = [("nakT_d", t_ * T, i_) for t_ in range(NT_NAKV) for i_ in range(4)] + \
              [("nav_d", t_ * T, tb_) for t_ in range(NT_NAKV) for tb_ in range(4)]

    def na_heads_tile(j, N, cat, catkey, latent=True):
        scale = 0.125
        blocks = []
        for i in range(4):
            pre = None
            if latent:
                nqr = N // 64
                w0 = max(0, 8 * j - 4)
                w1 = min(NT_NAKV * 8, 8 * j + nqr + 4)
                nr = w1 - w0
                s = i % 2

                def pre(s=s, i=i, w0=w0, nr=nr):
                    S.dma("sp", lambda e: e.dma_start(out=nakw[:, s, :nr * 64], in_=nakT_d[:, i, w0 * 64:(w0 + nr) * 64]),
                          "nakw%d" % s, reads=NA_KEYS, writes=["nakw%d" % s])
                    S.dma("sp", lambda e: e.dma_start(
                        out=navw[:, s, :nr * 192].rearrange("p (r f) -> p r f", f=192),
                        in_=nav_d[w0 * 64:(w0 + nr) * 64, i * 192:(i + 1) * 192].rearrange("(r t) f -> t r f", t=64)),
                        "navw%d" % s, reads=NA_KEYS, writes=["navw%d" % s])
                    for pp in range(2):
                        S.dma("sp", lambda e, pp=pp: e.dma_start(
                            out=nastrip[64 * pp:64 * pp + 64, s, :], in_=nastr_d[2 * i + pp, 0]),
                            "nastrip%d" % s, reads=["nastr_d"], writes=["nastrip%d" % s])
                        if j == 0:
                            S.dma("sp", lambda e, pp=pp: e.dma_start(
                                out=naedge[64 * pp:64 * pp + 64, 0, :], in_=nastr_d[2 * i + pp, 1]),
                                "naedge0", reads=["nastr_d"], writes=["naedge0"])
            for par in range(2):
                h = 2 * i + par
                p0 = 64 * par
                acc = next_acc()
                if par == 0:
                    accA = acc
                vo = 192 * i + 64 * par
                hb = []
                for cb in range(2):
                    qk = [(nakc[p0:p0 + 64, i, cb * 128:(cb + 1) * 128], naq[p0:p0 + 64, i, :N],
                           [("nakc", i), ("naq", i)], 0, N)]
                    hb.append(dict(np_=128, c0=0, c1=N, qk=qk, scale=scale, v_lhsT=navc[:, cb, vo:vo + 128],
                                   vkeys=["navc"], acc=acc))
                if latent:
                    for kr in range(w0, w1):
                        if j == 0:
                            r0, r1 = (0, 7) if kr <= 7 else (kr - 4, 7)
                        else:
                            r0, r1 = max(8 * j, kr - 4), min(8 * j + nqr - 1, kr + 4)
                        if r0 > r1:
                            continue
                        a, b = r0 - 8 * j, r1 - 8 * j + 1
                        c0, c1 = a * 64, b * 64
                        lk = (kr - w0) * 64
                        segs = []
                        if j == 0 and a < 4:
                            be = min(b, 4)
                            segs.append((a, be, naedge[p0:p0 + 64, 0, :], "naedge0"))
                            if b > 4:
                                segs.append((4, b, nastrip[p0:p0 + 64, s, :], "nastrip%d" % s))
                        else:
                            segs.append((a, b, nastrip[p0:p0 + 64, s, :], "nastrip%d" % s))
                        qk = [(nakw[p0:p0 + 64, s, lk:lk + 64], naq[p0:p0 + 64, i, c0:c1],
                               ["nakw%d" % s, ("naq", i)], c0, c1)]
                        for (sa, sb_, strip, skey) in segs:
                            e0 = (7 - kr + 8 * j + sa) * 64
                            qk.append((ident_b[p0:p0 + 64, p0:p0 + 64], strip[:, e0:e0 + (sb_ - sa) * 64],
                                       [skey, "ident_b"], sa * 64, sb_ * 64))
                        vw = (kr - w0) * 192 + 64 * par
                        hb.append(dict(np_=64, c0=c0, c1=c1, qk=qk, scale=scale,
                                       v_lhsT=navw[:, s, vw:vw + 128], vkeys=["navw%d" % s], acc=acc))
                for bi, blk in enumerate(hb):
                    blk["first"] = (bi == 0)
                    blk["last"] = (bi == len(hb) - 1)
                if par == 0 and pre is not None:
                    hb[0]["pre"] = pre

                if par == 1:
                    def epi(i=i, accA=accA, accB=acc):
                        def fin(rec, rkey):
                            for (q0, a_) in ((0, accA), (64, accB)):
                                S.op("dve", lambda e, q0=q0, a_=a_: e.tensor_tensor(
                                    out=cat[q0:q0 + 64, 4 + i, :N], in0=psb[a_][q0:q0 + 64, :N], in1=rec[q0:q0 + 64, :],
                                    op=ALU.mult), reads=[pkey(a_), rkey], writes=[(catkey, 4 + i)])
                        return pair_epilogue(accA, accB, N, fin)
                    hb[-1]["epi"] = epi
                blocks.extend(hb)
        run_attn(blocks)

    def gated_residual(hdst, hkey, N, l, wg, col):
        def ev(i, ps, bk):
            ig = mvi(l, wg, i, col)
            S.op("dve", lambda e: e.scalar_tensor_tensor(
                out=hdst[:, i, :N], in0=ps, scalar=modv[:, ig:ig + 1], in1=hdst[:, i, :N], op0=ALU.mult, op1=ALU.add),
                reads=[bk, (hkey, i), ("modv", l)], writes=[(hkey, i)])
        return ev

    def ffn(hdst, hkey, N, l, col, ab, abkey):
        norm_mod(hdst, hkey, N, l, 3, 4, col, ab, abkey)
        wn = "wgu%d" % l
        for f0 in range(0, 22, 2):
            nf = min(2, 22 - f0)
            gview, gkey = wload(wn, 8, f0 * 128, nf * 128)
            uview, ukey = wload(wn, 8, DFF + f0 * 128, nf * 128)
            for ff in range(nf):
                fi = f0 + ff
                bg = mm_bank()
                for kc in range(8):
                    S.op("pe", lambda e, kc=kc, ff=ff, bg=bg, gview=gview: e.matmul(
                        psb[bg][:, :N], lhsT=gview[:, kc, ff * 128:(ff + 1) * 128], rhs=ab[:, kc, :N],
                        start=(kc == 0), stop=(kc == 7)), reads=[gkey, (abkey, kc)], writes=[pkey(bg)])
                bu = mm_bank()
                for kc in range(8):
                    S.op("pe", lambda e, kc=kc, ff=ff, bu=bu, uview=uview: e.matmul(
                        psb[bu][:, :N], lhsT=uview[:, kc, ff * 128:(ff + 1) * 128], rhs=ab[:, kc, :N],
                        start=(kc == 0), stop=(kc == 7)), reads=[ukey, (abkey, kc)], writes=[pkey(bu)])
                S.op("act", lambda e, fi=fi, bg=bg: e.activation(out=silt[:, fi % 2, :N], in_=psb[bg][:, :N],
                                                                 func=AF.Silu),
                     reads=[pkey(bg)], writes=[("silt", fi % 2)])
                S.op("dve", lambda e, fi=fi, bu=bu: e.tensor_tensor(out=actT[:, fi, :N], in0=psb[bu][:, :N],
                                                                   in1=silt[:, fi % 2, :N], op=ALU.mult),
                     reads=[pkey(bu), ("silt", fi % 2)], writes=[("actT", fi)])
        linear("wdn%d" % l, 22, lambda kc: actT[:, kc, :N], [("actT", kc) for kc in range(22)], N,
               [(c * 128, 128) for c in range(8)], gated_residual(hdst, hkey, N, l, 5, col), blockcols=128)

    def rope_pair(k, ps, bk, npp, N, dst_ap, dkey):
        ii = k // 2
        if k % 2 == 0:
            S.op("dve", lambda e: e.tensor_tensor(out=ropeT[0][:npp, :N], in0=ps, in1=ropeC[:npp, :N], op=ALU.mult),
                 reads=[bk, "ropeC"], writes=[("ropeT", 0)])
        else:
            S.op("dve", lambda e: e.tensor_tensor(out=ropeS[:npp, 1, :N], in0=ps, in1=ropeS[:npp, 0, :N], op=ALU.mult),
                 reads=[bk, "ropeS0"], writes=["ropeS1"])
            S.op("dve", lambda e: e.tensor_tensor(out=dst_ap, in0=ropeT[0][:npp, :N], in1=ropeS[:npp, 1, :N],
                                                  op=ALU.add),
                 reads=[("ropeT", 0), "ropeS1"], writes=[dkey])

    def l0_q_side(ab, abkey, N, tok0, rotate):
        def ev(i, ps, bk):
            if i < 2:
                S.op("dve", lambda e: e.tensor_copy(out=cq[:, i, :N], in_=ps), reads=[bk], writes=[("tmpf", i)])
            else:
                S.op("act", lambda e: e.activation(out=naq[:, i - 2, :N], in_=ps, func=AF.Copy),
                     reads=[bk], writes=[("naq", i - 2)])
        linear("w0q", 8, lambda kc: ab[:, kc, :N], [(abkey, kc) for kc in range(8)], N,
               [(c * 128, 128) for c in range(6)], ev)
        rms_stats(lambda kc: cq[:, kc, :N], [("tmpf", kc) for kc in range(2)], 2, N, 256)
        for kc in range(2):
            S.op("dve", lambda e, kc=kc: e.scalar_tensor_tensor(
                out=cqn[:, kc, :N], in0=cq[:, kc, :N], scalar=qnorm[:, kc:kc + 1], in1=rstd[:, :N],
                op0=ALU.mult, op1=ALU.mult), reads=[("tmpf", kc), "rstd", "qnorm"], writes=[("cqn", kc)])
        if rotate:
            mch = [(h * 192 + 96 * v, 96) for h in range(8) for v in range(2)]
            S.dma("sp", lambda e: e.dma_start(out=ropeC[:96, :N], in_=ropeMc[:, tok0:tok0 + N]), "ropeC",
                  writes=["ropeC"])
            S.dma("sp", lambda e: e.dma_start(out=ropeS[:96, 0, :N], in_=ropeMs[:, tok0:tok0 + N]), "ropeS",
                  writes=["ropeS0"])
        else:
            mch = [(h * 192, 96) for h in range(8)]

        def ev2(i, ps, bk):
            if not rotate:
                h = i
                if h % 2 == 0:
                    S.op("dve", lambda e: e.tensor_copy(out=Qh[:, h, :N], in_=ps), reads=[bk], writes=[("Qh", h)])
                else:
                    S.op("act", lambda e: e.activation(out=Qh[:, h, :N], in_=ps, func=AF.Copy),
                         reads=[bk], writes=[("Qh", h)])
                return
            h, v = i // 2, i % 2
            if v == 0:
                S.op("act", lambda e: e.activation(out=Qh[0:64, h, :N], in_=ps[0:64, :], func=AF.Copy),
                     reads=[bk], writes=[("Qh", h)])
                S.op("dve", lambda e: e.tensor_tensor(out=ropeT[0][64:96, :N], in0=ps[64:96, :], in1=ropeC[64:96, :N],
                                                      op=ALU.mult),
                     reads=[bk, "ropeC"], writes=[("ropeT", 0)])
            else:
                S.op("dve", lambda e: e.tensor_tensor(out=ropeS[64:96, 1, :N], in0=ps[64:96, :], in1=ropeS[64:96, 0, :N],
                                                      op=ALU.mult),
                     reads=[bk, "ropeS0"], writes=["ropeS1"])
                S.op("dve", lambda e: e.tensor_tensor(out=Qh[64:96, h, :N], in0=ropeT[0][64:96, :N],
                                                      in1=ropeS[64:96, 1, :N], op=ALU.add),
                     reads=[("ropeT", 0), "ropeS1"], writes=[("Qh", h)])
        linear("wqup", 2, lambda kc: cqn[:, kc, :N], [("cqn", kc) for kc in range(2)], N, mch, ev2, blockcols=1536)

    def l0_kv(ab, abkey, N, key0, tok0, rotate):
        if rotate:
            S.dma("sp", lambda e: e.dma_start(out=ropeC[:32, :N], in_=ropeMc[0:32, tok0:tok0 + N]), "ropeC",
                  writes=["ropeC"])
            S.dma("sp", lambda e: e.dma_start(out=ropeS[:32, 0, :N], in_=ropeMs[0:32, tok0:tok0 + N]), "ropeS",
                  writes=["ropeS0"])

        def ev(i, ps, bk):
            if i == 0:
                S.op("dve", lambda e: e.tensor_copy(out=ckvf[:, :N], in_=ps), reads=[bk], writes=[("recb", 0)])
            elif not rotate:
                S.op("act", lambda e: e.activation(out=pT[1][:32, :N], in_=ps, func=AF.Copy),
                     reads=[bk], writes=["pT1"])
            else:
                rope_pair(i - 1, ps, bk, 32, N, pT[1][:32, :N], "pT1")
        mch = [(0, 128), (128, 32)] + ([(160, 32)] if rotate else [])
        linear("w0kv", 8, lambda kc: ab[:, kc, :N], [(abkey, kc) for kc in range(8)], N, mch, ev)
        rms_stats(lambda kc: ckvf[:, :N], [("recb", 0)], 1, N, 128)
        S.op("dve", lambda e: e.scalar_tensor_tensor(out=pT[0][:, :N], in0=ckvf[:, :N], scalar=kvnorm[:, 0:1],
                                                     in1=rstd[:, :N], op0=ALU.mult, op1=ALU.mult),
             reads=[("recb", 0), "rstd", "kvnorm"], writes=["pT0"])
        S.dma("pool", lambda e: e.dma_start(out=kr_d[:, key0:key0 + N], in_=pT[1][:32, :N]),
              "kvo1", reads=["pT1"], writes=[("kr_d", key0)])
        ukv, ukkey = wload("wukP", 1, 0, 512)
        for i in range(4):
            bank = mm_bank()
            S.op("pe", lambda e, i=i, bank=bank: e.matmul(psb[bank][:, :N], lhsT=ukv[:, 0, i * 128:(i + 1) * 128],
                                                          rhs=pT[0][:, :N], start=True, stop=True),
                 reads=[ukkey, "pT0"], writes=[pkey(bank)])
            st_ = pT[2]
            sk = "pT2"
            if i % 2 == 0:
                S.op("dve", lambda e, st_=st_, bank=bank: e.tensor_copy(out=st_[:, :N], in_=psb[bank][:, :N]),
                     reads=[pkey(bank)], writes=[sk])
            else:
                S.op("act", lambda e, st_=st_, bank=bank: e.activation(out=st_[:, :N], in_=psb[bank][:, :N],
                                                                       func=AF.Copy),
                     reads=[pkey(bank)], writes=[sk])
            S.dma("pool", lambda e, i=i, st_=st_: e.dma_start(out=khp_d[i, :, key0:key0 + N], in_=st_[:, :N]),
                  "kvo0", reads=[sk], writes=[("khp_d", key0, i)])
        uvv, uvkey = wload("wuv", 1, 0, 512)
        for tb in range(N // 128):
            bank = mm_bank()
            S.op("pe", lambda e, tb=tb, bank=bank: e.matmul(psb[bank][:, :512], lhsT=pT[0][:, tb * 128:(tb + 1) * 128],
                                                            rhs=uvv[:, 0, :], start=True, stop=True),
                 reads=[uvkey, "pT0"], writes=[pkey(bank)])
            v_evac(psb[bank][:, :512], pkey(bank), vstage[:, :], "vstage")
            S.dma("pool", lambda e, tb=tb: e.dma_start(out=vh_d[key0 + tb * 128:key0 + (tb + 1) * 128, :], in_=vstage[:, :]),
                  "kvo2", reads=["vstage"], writes=[("vh_d", key0, tb)])

    def v_evac(ps, bk, dst2d, dkey):
        src = ps.rearrange("p (i t c) -> p i t c", t=2, c=64)
        dst = dst2d.rearrange("p (i u) -> p i u", u=192)
        S.op("dve", lambda e: e.tensor_copy(out=dst[:, :, 0:64], in_=src[:, :, 0, :]), reads=[bk], writes=[dkey])
        S.op("act", lambda e: e.activation(out=dst[:, :, 128:192], in_=src[:, :, 1, :], func=AF.Copy),
             reads=[bk], writes=[dkey])

    def l0_nakv(ab, abkey, N, tok0, to_ctx):
        if to_ctx:
            linear("w0nak", 8, lambda kc: ab[:, kc, :N], [(abkey, kc) for kc in range(8)], N,
                   [(c * 128, 128) for c in range(4)],
                   copy_evac(lambda i: nakc[:, i, :N], lambda i: ("nakc", i)))
            linear_tm("w0nav", 8, lambda kc, tb: ab[:, kc, tb * 128:(tb + 1) * 128],
                      [(abkey, kc) for kc in range(8)], N // 128, 128, 512,
                      lambda tb, ps, bk: v_evac(ps, bk, navc[:, tb, :], "navc"))
            return

        def evk(i, ps, bk):
            S.op("act" if i % 2 else "dve",
                 (lambda e: e.activation(out=pT[i % 2][:, :N], in_=ps, func=AF.Copy)) if i % 2 else
                 (lambda e: e.tensor_copy(out=pT[i % 2][:, :N], in_=ps)),
                 reads=[bk], writes=["pT%d" % (i % 2)])
            S.dma("pool", lambda e: e.dma_start(out=nakT_d[:, i, tok0:tok0 + N], in_=pT[i % 2][:, :N]),
                  "nako%d" % (i % 2), reads=["pT%d" % (i % 2)], writes=[("nakT_d", tok0, i)])
        linear("w0nak", 8, lambda kc: ab[:, kc, :N], [(abkey, kc) for kc in range(8)], N,
               [(c * 128, 128) for c in range(4)], evk)

        def evv(tb, ps, bk):
            v_evac(ps, bk, vstage[:, :], "vstage")
            S.dma("pool", lambda e: e.dma_start(out=nav_d[tok0 + tb * 128:tok0 + (tb + 1) * 128, :], in_=vstage[:, :]),
                  "navo", reads=["vstage"], writes=[("nav_d", tok0, tb)])
        linear_tm("w0nav", 8, lambda kc, tb: ab[:, kc, tb * 128:(tb + 1) * 128],
                  [(abkey, kc) for kc in range(8)], N // 128, 128, 512, evv)

    def v1_evac(ps, bk, dst2d, dkey):
        src = ps.rearrange("p (g c) -> p g c", c=64)
        dst = dst2d.rearrange("p (g u) -> p g u", u=192)
        S.op("dve", lambda e: e.tensor_copy(out=dst[:, :, 0:64], in_=src), reads=[bk], writes=[dkey])
        S.op("act", lambda e: e.activation(out=dst[:, :, 128:192], in_=src, func=AF.Copy), reads=[bk], writes=[dkey])

    def l0_finish(hdst, hkey, N, col, cat, catkey, ab2, ab2key):
        linear("wout0", 8, lambda kc: cat[:, kc, :N], [(catkey, kc) for kc in range(8)], N,
               [(c * 128, 128) for c in range(8)], gated_residual(hdst, hkey, N, 0, 2, col))
        ffn(hdst, hkey, N, 0, col, ab2, ab2key)

    def dump(name, src_fn, nchunk, N):
        if name not in dbg_out:
            return
        for kc in range(nchunk):
            src, keys = src_fn(kc)
            S.op("dve", lambda e, src=src: e.tensor_copy(out=ostage[0][:, :N], in_=src), reads=keys, writes=["xstage0"])
            S.dma("pool", lambda e, kc=kc: e.dma_start(out=dbg_out[name][kc * 128:(kc + 1) * 128, :], in_=ostage[0][:, :N]),
                  "dbg", reads=["xstage0"], out=True)

    def prologue_A():
        load_T(ctxl, 2, hctx, "hbuf1")
        norm_mod(hctx, "hbuf1", CTX, 0, 0, 1, 1, abuf[0], "abuf0")
        l0_kv(abuf[0], "abuf0", CTX, S_FULL, 0, False)
        l0_nakv(abuf[0], "abuf0", CTX, 0, True)
        l0_q_side(abuf[0], "abuf0", CTX, 0, False)
        mla_heads_tile(CTX, [(S_FULL, 2)], abuf[1], "abuf1")
        na_heads_tile(0, CTX, abuf[1], "abuf1", latent=False)
        l0_finish(hctx, "hbuf1", CTX, 1, abuf[1], "abuf1", abuf[0], "abuf0")
        dump("hctx1", lambda kc: (hctx[:, kc, :CTX], [("hbuf1", kc)]), 8, CTX)
        norm_mod(hctx, "hbuf1", CTX, 1, 0, 1, 1, abuf[0], "abuf0")
        linear("w1k", 8, lambda kc: abuf[0][:, kc, :CTX], [("abuf0", kc) for kc in range(8)], CTX,
               [(0, 128), (256, 128)], copy_evac(lambda i: k1c[:, i, :], lambda i: ("k1c", i)))
        linear_tm("w1v", 8, lambda kc, tb: abuf[0][:, kc, tb * 128:(tb + 1) * 128], [("abuf0", kc) for kc in range(8)],
                  2, 128, 128, lambda tb, ps, bk: v1_evac(ps, bk, v1c[:, tb, :], "v1c"))


    if stage < 5:
        return _finish()
    NPRO = S_FULL // T

    def pro_head(i):
        for k_ in CAST_AT.get(i, ()):
            cast_weight(k_)
        build_strip(i // 2, i % 2)
        mod_blocks(1, range(3 * i, 3 * i + 3), 7)
        load_T_big(xl[i * T:(i + 1) * T, :], hbuf[i % 2], "hbuf%d" % (i % 2), hbuf[(i + 1) % 2], "hbuf%d" % ((i + 1) % 2))
        norm_mod(hbuf[i % 2], "hbuf%d" % (i % 2), T, 0, 0, 1, 0, abuf[i % 2], "abuf%d" % (i % 2))

    pro_head(0)
    for i in range(NPRO):
        ab = abuf[i % 2]
        ak = "abuf%d" % (i % 2)
        if i + 1 < NPRO:
            pro_head(i + 1)
        l0_kv(ab, ak, T, i * T, i * T, True)
        if i < NT_NAKV:
            l0_nakv(ab, ak, T, i * T, False)

    def l1_kv(j, hb, hk, N=T):
        ab, ak = abuf[0], "abuf0"
        norm_mod(hb, hk, N, 1, 0, 1, 0, ab, ak)
        slot = j % 3
        S.dma("sp", lambda e: e.dma_start(out=ropeC[:, :N], in_=ropeSc[:, j * T:j * T + N]), "ropeC", writes=["ropeC"])
        S.dma("sp", lambda e: e.dma_start(out=ropeS[:, 0, :N], in_=ropeSs[:, j * T:j * T + N]), "ropeS",
              writes=["ropeS0"])

        def ev(i, ps, bk):
            rope_pair(i, ps, bk, 128, N, k1[:, i // 2, slot * T:slot * T + N], ("k1", i // 2, slot))
        linear("w1k", 8, lambda kc: ab[:, kc, :N], [(ak, kc) for kc in range(8)], N,
               [(c * 128, 128) for c in range(4)], ev)
        linear_tm("w1v", 8, lambda kc, tb: ab[:, kc, tb * 128:(tb + 1) * 128], [(ak, kc) for kc in range(8)],
                  N // 128, 128, 128, lambda tb, ps, bk: v1_evac(ps, bk, v1[:, slot * 4 + tb, :], ("v1", slot)))

    def l1_main(t, hb, hk):
        ab, ak = abuf[1], "abuf1"
        norm_mod(hb, hk, T, 1, 0, 1, 0, ab, ak)
        S.dma("sp", lambda e: e.dma_start(out=ropeC[:, :], in_=ropeSc[:, t * T:(t + 1) * T]), "ropeC", writes=["ropeC"])
        S.dma("sp", lambda e: e.dma_start(out=ropeS[:, 0, :], in_=ropeSs[:, t * T:(t + 1) * T]), "ropeS",
              writes=["ropeS0"])

        def evq(i, ps, bk):
            rope_pair(i, ps, bk, 128, T, q1[:, i // 2, :], ("q1", i // 2))
        linear("w1q", 8, lambda kc: ab[:, kc, :], [(ak, kc) for kc in range(8)], T,
               [(c * 128, 128) for c in range(16)], evq)
        cat, catkey = abuf[0], "abuf0"
        blocks = []
        for qb in range(4):
            gq = 4 * t + qb
            for g in range(2):
                for par in range(2):
                    p0 = 64 * par
                    acc = next_acc()
                    if par == 0:
                        accA = acc
                    vo = 192 * g + 64 * par
                    rhs = q1[p0:p0 + 64, 4 * g:4 * g + 4, qb * 128:(qb + 1) * 128]
                    rk = [("q1", 4 * g + c) for c in range(4)]
                    hbl = []
                    for cb in range(2):
                        qk = [(k1c[p0:p0 + 64, g, cb * 128:(cb + 1) * 128], rhs, [("k1c", g)] + rk, 0, 512)]
                        hbl.append(dict(np_=128, c0=0, c1=512, qk=qk, scale=0.125,
                                       v_lhsT=v1c[:, cb, vo:vo + 128], vkeys=["v1c"], acc=acc))
                    for dk in (-1, 0, 1):
                        kb = gq + dk
                        if kb < 0:
                            continue
                        slot = (kb // 4) % 3
                        kcol = slot * T + (kb % 4) * 128
                        qk = [(k1[p0:p0 + 64, g, kcol:kcol + 128], rhs, [("k1", g, slot)] + rk, 0, 512)]
                        if dk != 0:
                            mi = 0 if dk < 0 else 1
                            qk.append((ident_b[:, :], swam[:, mi, :], ["ident_b", "swam"], 0, 512))
                        hbl.append(dict(np_=128, c0=0, c1=512, qk=qk, scale=0.125,
                                       v_lhsT=v1[:, slot * 4 + (kb % 4), vo:vo + 128], vkeys=[("v1", slot)],
                                       acc=acc))
                    for bi, blk in enumerate(hbl):
                        blk["first"] = (bi == 0)
                        blk["last"] = (bi == len(hbl) - 1)

                    if par == 1:
                        def epi(g=g, qb=qb, accA=accA, accB=acc):
                            def fin(rec, rkey):
                                for (q0, a_) in ((0, accA), (64, accB)):
                                    S.op("dve", lambda e, q0=q0, a_=a_: e.tensor_tensor(
                                        out=cat[q0:q0 + 64, 4 * g:4 * g + 4, qb * 128:(qb + 1) * 128],
                                        in0=psb[a_][q0:q0 + 64, :].rearrange("p (h t) -> p h t", h=4),
                                        in1=rec[q0:q0 + 64, :].rearrange("p (h t) -> p h t", h=4), op=ALU.mult),
                                        reads=[pkey(a_), rkey], writes=[(catkey, 4 * g + c) for c in range(4)])
                            sink = ([8 * g + 2 * hh for hh in range(4)], [8 * g + 2 * hh + 1 for hh in range(4)])
                            return pair_epilogue(accA, accB, T, fin, sink_heads=sink)
                        hbl[-1]["epi"] = epi
                    blocks.extend(hbl)
        run_attn(blocks)
        linear("wout1", 8, lambda kc: cat[:, kc, :], [(catkey, kc) for kc in range(8)], T,
               [(c * 128, 128) for c in range(8)], gated_residual(hb, hk, T, 1, 2, 0))
        ffn(hb, hk, T, 1, 0, abuf[1], "abuf1")
        rms_stats(lambda kc: hb[:, kc, :], [(hk, kc) for kc in range(8)], 8, T, D)
        for kc in range(8):
            S.op("dve", lambda e, kc=kc: e.scalar_tensor_tensor(
                out=hb[:, kc, :], in0=hb[:, kc, :], scalar=fnorm[:, kc:kc + 1], in1=rstd[:, :],
                op0=ALU.mult, op1=ALU.mult), reads=[(hk, kc), "rstd", "fnorm"], writes=[(hk, kc)])
        for tb in range(4):
            os_ = ostage[tb % 2]
            ok = "xstage%d" % (tb % 2)
            for g2 in range(2):
                bank = mm_bank()
                for q in range(4):
                    kc = 4 * g2 + q
                    S.op("pe", lambda e, kc=kc, q=q, tb=tb, bank=bank: e.transpose(
                        out=psb[bank][:, q * 128:(q + 1) * 128], in_=hb[:, kc, tb * 128:(tb + 1) * 128],
                        identity=ident_f[:]), reads=[(hk, kc), "ident_f"], writes=[pkey(bank)])
                if g2 == 0:
                    S.op("dve", lambda e, os_=os_, bank=bank: e.tensor_copy(out=os_[:, 0:512], in_=psb[bank][:, :]),
                         reads=[pkey(bank)], writes=[ok])
                else:
                    S.op("act", lambda e, os_=os_, bank=bank: e.activation(out=os_[:, 512:1024], in_=psb[bank][:, :],
                                                                           func=AF.Copy),
                         reads=[pkey(bank)], writes=[ok])
            r0 = t * T + tb * 128
            S.dma("pool", lambda e, os_=os_, r0=r0: e.dma_start(out=y[r0:r0 + 128, :], in_=os_[:, :]),
                  "yout%d" % (tb % 2), reads=[ok], out=True)

    mod_finish(1, 7)
    prologue_A()
    if stage < 6:
        return _finish()
    for j in range(n_main):
        if j == 0:
            for k_ in ("w1q", "wout1", "wgu1", "wdn1"):
                cast_weight(k_)
        hb = hbuf[j % 2]
        hk = "hbuf%d" % (j % 2)
        NJ = T if j < 8 else 128
        load_T(xl[j * T:j * T + NJ, :], NJ // 128, hb, hk)
        norm_mod(hb, hk, NJ, 0, 0, 1, 0, abuf[0], "abuf0")
        l0_q_side(abuf[0], "abuf0", NJ, j * T, True)
        if stage == 61:
            return _finish()
        chunks = [(c * 1024, 8) for c in range(8)] + [(S_FULL, 2)]
        mla_heads_tile(NJ, chunks, abuf[1], "abuf1")
        if stage == 62:
            if j == 0:
                dump("h1t0", lambda kc: (abuf[1][:, kc, :], [("abuf1", kc)]), 4, T)
            return _finish()
        na_heads_tile(j, NJ, abuf[1], "abuf1", latent=True)
        if stage in (63, 631):
            if j == 0:
                dump("h1t0", lambda kc: (abuf[1][:, kc, :], [("abuf1", kc)]), 8, T)
            return _finish()
        l0_finish(hb, hk, NJ, 0, abuf[1], "abuf1", abuf[0], "abuf0")
        if j == 0:
            dump("h1t0", lambda kc: (hb[:, kc, :], [(hk, kc)]), 8, T)
        if stage == 64:
            return _finish()
        l1_kv(j, hb, hk, NJ)
        if j >= 1:
            l1_main(j - 1, hbuf[(j - 1) % 2], "hbuf%d" % ((j - 1) % 2))

    S.emit(st)
    st.close()
    ninst = {e: len(S.streams[e]) for e in ENGS}
    print("instructions per engine:", ninst)
    return nc


_CACHE = {}


def kernel(**inputs):
    inp = {k: np.asarray(v) for k, v in inputs.items()}
    w, sh = _prep_shared(inp)
    in_maps = []
    for cid in range(8):
        m = _prep_core(inp, cid, sh)
        m.update(w)
        m.update(sh)
        in_maps.append(m)
    if "nc" not in _CACHE:
        _CACHE["nc"] = build_program()
    res = run_bass_kernel_spmd(_CACHE["nc"], in_maps, core_ids=list(range(8)))
    out = np.zeros((4, S_FULL, D), np.float32)
    for cid in range(8):
        b, half = cid // 2, cid % 2
        yl = np.asarray(res.results[cid]["y"], dtype=np.float32)
        if half == 0:
            out[b, :OWN] = yl
        else:
            out[b, OWN:] = yl[::-1]
    return out
```

```python
from contextlib import ExitStack
import numpy as np
import concourse.bass as bass
import concourse.mybir as mybir
from concourse.bass_utils import run_bass_kernel_spmd

F32 = mybir.dt.float32
BF16 = mybir.dt.bfloat16
AF = mybir.ActivationFunctionType
ALU = mybir.AluOpType

D = 1024
S_FULL = 8192
CTX = 256
T = 512
NT_MAIN = 9
NT_NAKV = 9
OWN = 4096
EPS = 1e-6
DFF = 2816
NEG = -1e30

ENGS = ("pe", "act", "dve", "pool", "sp")


class Sched:
    def __init__(self, nc):
        self.nc = nc
        self.streams = {e: [] for e in ENGS}
        self.writer = {}
        self.readers = {}
        self.known = {e: {} for e in ENGS}
        self.dma_count = {}
        self.sig = {e: set() for e in ENGS}
        self.snap = {}
        self.out_slots = set()

    def _knows(self, eng, ev):
        k = self.known[eng]
        if ev[0] == 'c':
            return k.get(ev[1], -1) >= ev[2]
        return k.get(('d', ev[1]), 0) >= ev[2]

    def _learn(self, eng, ev):
        k = self.known[eng]
        key = ev[1] if ev[0] == 'c' else ('d', ev[1])
        if k.get(key, -1) < ev[2]:
            k[key] = ev[2]
        s = self.snap.get(ev)
        if s:
            for kk, vv in s.items():
                if k.get(kk, -1) < vv:
                    k[kk] = vv

    def _deps(self, eng, reads, writes):
        evs = []
        for key in reads:
            w = self.writer.get(key)
            if w is not None:
                evs.append(w)
        for key in writes:
            w = self.writer.get(key)
            if w is not None:
                evs.append(w)
            evs.extend(self.readers.get(key, ()))
        waits = []
        for ev in evs:
            if ev[0] == 'c' and ev[1] == eng and eng == 'pe':
                continue
            if self._knows(eng, ev):
                continue
            waits.append(ev)
            if ev[0] == 'c':
                self.sig[ev[1]].add(ev[2])
            self._learn(eng, ev)
        return waits

    def _record(self, ev, reads, writes):
        for key in reads:
            self.readers.setdefault(key, []).append(ev)
        for key in writes:
            self.writer[key] = ev
            self.readers[key] = []

    def op(self, eng, fn, reads=(), writes=()):
        waits = self._deps(eng, reads, writes)
        idx = len(self.streams[eng])
        ev = ('c', eng, idx)
        self.snap[ev] = {k: v for k, v in self.known[eng].items() if not isinstance(k, tuple)}
        self.streams[eng].append(('op', fn, waits, idx))
        self._record(ev, reads, writes)
        return ev

    def dma(self, eng, fn, slot, reads=(), writes=(), out=False):
        waits = self._deps(eng, reads, writes)
        n = self.dma_count.get(slot, 0) + 1
        self.dma_count[slot] = n
        ev = ('d', slot, n)
        self.snap[ev] = {k: v for k, v in self.known[eng].items() if not isinstance(k, tuple)}
        self.streams[eng].append(('dma', fn, waits, slot))
        self._record(ev, reads, writes)
        if out:
            self.out_slots.add(slot)
        return ev

    def emit(self, stack):
        nc = self.nc
        csem = {e: stack.enter_context(nc.semaphore("c_" + e)) for e in ENGS if e != "sp"}
        dsem = {s: stack.enter_context(nc.semaphore("d_%s" % (s,))) for s in self.dma_count}
        rank = {}
        for e in ENGS:
            srt = sorted(self.sig[e])
            rank[e] = {i: r + 1 for r, i in enumerate(srt)}
        block = stack.enter_context(nc.Block())

        def run(ename, engobj):
            for kind, fn, waits, x in self.streams[ename]:
                for ev in waits:
                    if ev[0] == 'c':
                        engobj.wait_ge(csem[ev[1]], rank[ev[1]][ev[2]])
                    else:
                        engobj.wait_ge(dsem[ev[1]], 16 * ev[2])
                ins = fn(engobj)
                if kind == 'dma':
                    ins.then_inc(dsem[x], 16)
                elif x in rank[ename]:
                    ins.then_inc(csem[ename], 1)
            if ename == "sp":
                for s in sorted(self.out_slots, key=str):
                    engobj.wait_ge(dsem[s], 16 * self.dma_count[s])

        @block.tensor
        def _(e):
            run("pe", e)

        @block.scalar
        def _(e):
            run("act", e)

        @block.vector
        def _(e):
            run("dve", e)

        @block.gpsimd
        def _(e):
            run("pool", e)

        @block.sync
        def _(e):
            run("sp", e)


def _rope_tables(pos_r, pos_c, dim):
    half = dim // 2
    inv = (10000.0 ** (-np.arange(0, half, 2, dtype=np.float32) / half)).astype(np.float32)
    nf = half // 2
    C = np.zeros((dim, pos_r.shape[0]), np.float32)
    Sg = np.zeros((dim, pos_r.shape[0]), np.float32)
    for f in range(dim):
        pos = pos_r if f < half else pos_c
        i = f % half
        ang = pos.astype(np.float32) * inv[i % nf]
        C[f] = np.cos(ang)
        Sg[f] = -np.sin(ang) if i < nf else np.sin(ang)
    return C, Sg


def _swap_idx(dim):
    half = dim // 2
    nf = half // 2
    idx = np.zeros(dim, np.int64)
    for f in range(dim):
        base = 0 if f < half else half
        i = f % half
        idx[f] = base + (i + nf if i < nf else i - nf)
    return idx


def _na_tables(rel_bias, rev):
    kc = np.arange(64)[:, None]
    qc = np.arange(64)[None, :]
    if rev:
        kt, qt = 63 - kc, 63 - qc
    else:
        kt, qt = kc, qc
    q_start = np.clip(qt - 8, 0, 48)
    col_ok = (kt >= q_start) & (kt < q_start + 16)
    dc_idx = np.clip(kt - qt + 15, 0, 30)
    bias = np.zeros((8, 2, 64, 15, 64), np.float32)
    mask = np.zeros((2, 64, 15, 64), np.float32)
    for e in range(15):
        dr_l = 7 - e
        dr_t = -dr_l if rev else dr_l
        g = rel_bias[:, dr_t + 7][:, dc_idx]
        bias[:, 0, :, e, :] = g
        bias[:, 1, :, e, :] = g
        row_ok = (-4 <= dr_t <= 3)
        mask[0, :, e, :] = np.where(col_ok & row_ok, 0.0, NEG)
        mask[1, :, e, :] = np.where(col_ok, 0.0, NEG)
    return bias.reshape(8, 2, 64, 960), mask.reshape(2, 64, 960)


def _prep_shared(inp):
    f = np.float32
    w = {}
    wi = inp["even_w_in"][0]
    kr = wi[:, 384:416]
    sw32 = _swap_idx(32)
    w["w0kv"] = np.concatenate([wi[:, 256:384], kr, kr[:, sw32]], 1)
    w["w0nak"] = wi[:, 928:1440]
    w["w0nav"] = wi[:, 1440:1952]
    w["w0q"] = np.concatenate([wi[:, 0:256], wi[:, 416:928]], 1)
    qu = inp["mla_w_q_up"][0].reshape(256, 8, 96)
    nope = qu[:, :, :64].reshape(256, 512)
    rope = qu[:, :, 64:]
    rsw = rope[:, :, sw32]
    nope3 = qu[:, :, :64]
    parts = []
    for h in range(8):
        parts += [nope3[:, h], rope[:, h], nope3[:, h], rsw[:, h]]
    w["wqup"] = np.concatenate(parts, 1)
    w["wukP"] = inp["mla_w_uk"][0].transpose(1, 0, 2).reshape(128, 512)
    w["wuv"] = inp["mla_w_uv"][0].transpose(1, 0, 2).reshape(128, 512)
    w["wout0"] = inp["even_w_out"][0]
    for l in range(2):
        w["wgu%d" % l] = inp["ffn_w_gate_up"][l]
        w["wdn%d" % l] = inp["ffn_w_down"][l]
    wo = inp["odd_w_in"][0]
    sw64 = _swap_idx(64)
    q = wo[:, :1024].reshape(1024, 16, 64)
    qs = q[:, :, sw64]
    w["w1q"] = np.concatenate([np.concatenate([q[:, 2 * c:2 * c + 2].reshape(1024, 128),
                                               qs[:, 2 * c:2 * c + 2].reshape(1024, 128)], 1) for c in range(8)], 1)
    k = wo[:, 1024:1152].reshape(1024, 2, 64)
    ksw = k[:, :, sw64]
    w["w1k"] = np.concatenate([k[:, 0], k[:, 0], ksw[:, 0], ksw[:, 0], k[:, 1], k[:, 1], ksw[:, 1], ksw[:, 1]], 1)
    v = wo[:, 1152:1280].reshape(1024, 2, 64)
    w["w1v"] = np.concatenate([v[:, 0], v[:, 1]], 1)
    w["wout1"] = inp["odd_w_out"][0]
    w = {k_: np.ascontiguousarray(v_, dtype=f) for k_, v_ in w.items()}
    sh = {}
    sh["modw"] = np.ascontiguousarray(inp["mod_w"], dtype=f)
    mb = inp["mod_b"].reshape(2, 48, 128).transpose(2, 0, 1)
    sh["modb"] = np.ascontiguousarray(np.repeat(mb[:, :, :, None], 2, axis=3), dtype=f)
    sh["qnorm"] = np.ascontiguousarray(inp["mla_q_norm"][0].reshape(2, 128).T, dtype=f)
    sh["kvnorm"] = np.ascontiguousarray(inp["mla_kv_norm"][0].reshape(1, 128).T, dtype=f)
    sh["fnorm"] = np.ascontiguousarray(inp["final_norm"].reshape(8, 128).T, dtype=f)
    sh["sinks"] = np.ascontiguousarray(np.repeat(inp["swa_sinks"][0][None, :], 128, axis=0), dtype=f)
    kk = np.arange(128)[:, None]
    qq = np.arange(128)[None, :]
    mlo = np.where(qq <= kk, 0.0, NEG).astype(f)
    mhi = np.where(kk <= qq, 0.0, NEG).astype(f)
    sh["swamask"] = np.ascontiguousarray(np.stack([np.tile(mlo, (1, 4)), np.tile(mhi, (1, 4))]), dtype=f)
    return w, sh


WSHAPES = {"w0kv": (1024, 192), "w0nak": (1024, 512), "w0nav": (1024, 512), "w0q": (1024, 768),
           "wqup": (256, 1536), "wukP": (128, 512), "wuv": (128, 512), "wout0": (1024, 1024),
           "wgu0": (1024, 5632), "wdn0": (2816, 1024), "w1q": (1024, 2048), "w1k": (1024, 512),
           "w1v": (1024, 128), "wout1": (1024, 1024), "wgu1": (1024, 5632), "wdn1": (2816, 1024)}
WORDER = ["w0kv", "w0nak", "w0nav", "w0q", "wqup", "wukP", "wuv", "wout0", "wgu0", "wdn0",
          "w1k", "w1v", "w1q", "wout1", "wgu1", "wdn1"]


def _prep_core(inp, cid, sh):
    f = np.float32
    b, half = cid // 2, cid % 2
    perm = np.arange(S_FULL) if half == 0 else np.arange(S_FULL)[::-1]
    m = {}
    m["xl"] = np.ascontiguousarray(inp["x"][b][perm], dtype=f)
    m["ctxl"] = np.ascontiguousarray(inp["ctx"][b], dtype=f)
    cv = np.stack([inp["c"][b].reshape(8, 128).T, inp["c_ctx"].reshape(8, 128).T], axis=2)
    m["cvec"] = np.ascontiguousarray(cv, dtype=f)
    rows, cols = perm // 64, perm % 64
    C, Sg = _rope_tables(rows, cols, 32)
    m["ropeMc"] = np.ascontiguousarray(np.tile(C, (3, 1)), dtype=f)
    m["ropeMs"] = np.ascontiguousarray(np.tile(Sg, (3, 1)), dtype=f)
    C, Sg = _rope_tables(rows[:5120], cols[:5120], 64)
    m["ropeSc"] = np.ascontiguousarray(np.tile(C, (2, 1)), dtype=f)
    m["ropeSs"] = np.ascontiguousarray(np.tile(Sg, (2, 1)), dtype=f)
    nb, nm = _na_tables(inp["na_rel_bias"][0], half == 1)
    m["nabias"] = np.ascontiguousarray(nb, dtype=f)
    m["namask"] = np.ascontiguousarray(nm, dtype=f)
    return m


def build_program(n_main=NT_MAIN, dbg=None, stage=9):
    nc = bass.Bass("TRN2", target_bir_lowering=False)
    S = Sched(nc)
    st = ExitStack()

    def din(name, shape, dt=F32):
        return nc.dram_tensor(name, list(shape), dt, kind="ExternalInput").ap()

    xl = din("xl", [S_FULL, D])
    ctxl = din("ctxl", [CTX, D])
    cvec = din("cvec", [128, 8, 2])
    modw = din("modw", [2, D, 6 * D])
    modb = din("modb", [128, 2, 48, 2])
    qnorm_d = din("qnorm", [128, 2])
    kvnorm_d = din("kvnorm", [128, 1])
    fnorm_d = din("fnorm", [128, 8])
    sinks_d = din("sinks", [128, 16])
    swamask_d = din("swamask", [2, 128, 512])
    ropeMc = din("ropeMc", [96, S_FULL])
    ropeMs = din("ropeMs", [96, S_FULL])
    ropeSc = din("ropeSc", [128, 5120])
    ropeSs = din("ropeSs", [128, 5120])
    nabias = din("nabias", [8, 2, 64, 960])
    namask = din("namask", [2, 64, 960])
    wf = {k: din(k, WSHAPES[k]) for k in WORDER}
    y = nc.dram_tensor("y", [OWN, D], F32, kind="ExternalOutput").ap()
    dbg_out = {}
    if dbg:
        for k, shp in dbg.items():
            dbg_out[k] = nc.dram_tensor("dbg_" + k, list(shp), F32, kind="ExternalOutput").ap()

    wb = {k: nc.dram_tensor("wb_" + k, list(WSHAPES[k]), BF16).ap() for k in WORDER}
    NK = S_FULL + CTX
    khp_d = nc.dram_tensor("khp_d", [4, 128, NK], BF16).ap()
    kr_d = nc.dram_tensor("kr_d", [32, NK], BF16).ap()
    vh_d = nc.dram_tensor("vh_d", [NK, 768], BF16).ap()
    NAT = NT_NAKV * T
    nakT_d = nc.dram_tensor("nakT_d", [128, 4, NAT], BF16).ap()
    nav_d = nc.dram_tensor("nav_d", [NAT, 768], BF16).ap()
    nastr_d = nc.dram_tensor("nastr_d", [8, 2, 64, 960], BF16).ap()

    def sb(name, shape, dt):
        return st.enter_context(nc.sbuf_tensor(name, list(shape), dt))

    ident_f = sb("ident_f", [128, 128], F32)
    ident_b = sb("ident_b", [128, 128], BF16)
    ones_b = sb("ones_b", [128, 128], BF16)
    modv = sb("modv", [128, 2 * 6 * 8 * 2], F32)
    onep = sb("onep", [128, 2 * 6 * 8 * 2], F32)
    silc = sb("silc", [128, 16], F32)
    qnorm = sb("qnorm_s", [128, 2], F32)
    kvnorm = sb("kvnorm_s", [128, 1], F32)
    fnorm = sb("fnorm_s", [128, 8], F32)
    esink = sb("esink", [128, 16], F32)
    swam = sb("swam", [128, 2, 512], BF16)
    hbuf = [sb("hbuf%d" % i, [128, 8, T], F32) for i in range(2)]
    hctx = hbuf[1]
    xstage = [sb("xstage%d" % i, [128, D], F32) for i in range(2)]
    ostage = xstage
    abuf = [sb("abuf%d" % i, [128, 8, T], BF16) for i in range(2)]
    NWS = 3
    wslot = [sb("wslot%d" % i, [128, 4096], BF16) for i in range(NWS)]
    sqb = sb("sqb", [128, 2, T], BF16)
    tmpf = sb("tmpf", [128, 2, T], F32)
    rstd = sb("rstd", [128, T], F32)
    cq = tmpf
    cqn = sb("cqn", [128, 2, T], BF16)
    uni = sb("uni", [128, 8 * T], BF16)
    qnope = uni[:, 0:4 * T].rearrange("p (k t) -> p k t", k=4)
    Qh = sb("Qh", [96, 8, T], BF16)
    pswap = sb("pswap", [128, 128], F32)
    vstage = sb("vstage", [128, 768], BF16)
    recb = sb("recb", [128, 2, T], F32)
    ckvf = recb[:, 0, :]
    naq = uni[:, 4 * T:8 * T].rearrange("p (k t) -> p k t", k=4)
    nakw = sb("nakw", [128, 2, 1024], BF16)
    navw = sb("navw", [64, 2, 16 * 192], BF16)
    nastrip = sb("nastrip", [128, 2, 960], BF16)
    naedge = sb("naedge", [128, 1, 960], BF16)
    nakc = sb("nakc", [128, 4, CTX], BF16)
    navc = sb("navc", [128, 2, 768], BF16)
    actT = sb("actT", [128, 22, T], BF16)
    silt = sb("silt", [128, 2, T], F32)
    q1 = uni[:, :].rearrange("p (k t) -> p k t", k=8)
    k1 = sb("k1", [128, 2, 3 * T], BF16)
    v1 = sb("v1", [128, 12, 384], BF16)
    k1c = sb("k1c", [128, 2, CTX], BF16)
    v1c = sb("v1c", [128, 2, 384], BF16)
    pT = [sb("pT%d" % i, [128, T], BF16) for i in range(3)]
    ropeC = sb("ropeC", [128, T], F32)
    ropeS = sb("ropeS", [128, 2, T], F32)
    mck = [sb("mck%d" % i, [96, 1024], BF16) for i in range(2)]
    mv = [sb("mv%d" % i, [128, 8, 192], BF16) for i in range(2)]

    psb = [st.enter_context(nc.psum_tensor("psb%d" % i, [128, 512], F32)) for i in range(8)]
    print("sbuf bytes remaining/partition:", nc.sbuf_bytes_remaining)

    cnt = {"mm": 0, "ws": 0, "pt": 0, "uid": 0}

    def mm_bank():
        b = cnt["mm"] % 4
        cnt["mm"] += 1
        return b

    def pkey(b):
        return "ps%d" % b

    def next_pt():
        i = cnt["pt"] % 3
        cnt["pt"] += 1
        return i

    def mvi(l, w, kc, col):
        i = ((l * 6 + w) * 8 + kc) * 2 + col
        return i

    S.op("pool", lambda e: e.memset(ident_f[:], 1.0), writes=["ident_f"])
    S.op("pool", lambda e: e.affine_select(out=ident_f[:], in_=ident_f[:], pattern=[[-1, 128]],
                                           compare_op=ALU.is_equal, fill=0.0, base=0, channel_multiplier=1),
         reads=["ident_f"], writes=["ident_f"])
    S.op("dve", lambda e: e.tensor_copy(out=ident_b[:], in_=ident_f[:]), reads=["ident_f"], writes=["ident_b"])
    S.op("pool", lambda e: e.memset(ones_b[:], 1.0), writes=["ones_b"])
    S.op("pool", lambda e: e.memset(pswap[:], 1.0), writes=["pswap"])
    S.op("pool", lambda e: e.affine_select(out=pswap[:], in_=pswap[:], pattern=[[-1, 128]],
                                           compare_op=ALU.is_equal, fill=0.0, base=64, channel_multiplier=1),
         reads=["pswap"], writes=["pswap"])
    S.op("pool", lambda e: e.memset(xstage[0][:, 0:128], 1.0), writes=["xstage0"])
    S.op("pool", lambda e: e.affine_select(out=xstage[0][:, 0:128], in_=xstage[0][:, 0:128], pattern=[[-1, 128]],
                                           compare_op=ALU.is_equal, fill=0.0, base=-64, channel_multiplier=1),
         reads=["xstage0"], writes=["xstage0"])
    S.op("pool", lambda e: e.tensor_tensor(out=pswap[:], in0=pswap[:], in1=xstage[0][:, 0:128], op=ALU.add),
         reads=["pswap", "xstage0"], writes=["pswap"])
    S.op("pool", lambda e: e.memset(vstage[:, :].rearrange("p (i u) -> p i u", u=192)[:, :, 64:128], 1.0),
         writes=["vstage"])
    S.op("pool", lambda e: e.memset(navc[:, :, :].rearrange("p b (i u) -> p b i u", u=192)[:, :, :, 64:128], 1.0),
         writes=["navc"])
    S.op("pool", lambda e: e.memset(v1[:, :, :].rearrange("p b (i u) -> p b i u", u=192)[:, :, :, 64:128], 1.0),
         writes=[("v1", 0), ("v1", 1), ("v1", 2)])
    S.op("pool", lambda e: e.memset(v1c[:, :, :].rearrange("p b (i u) -> p b i u", u=192)[:, :, :, 64:128], 1.0),
         writes=["v1c"])

    def cast_weight(k):
        r, c = WSHAPES[k]
        per = r * c // 128
        src = wf[k].rearrange("r c -> (r c)").rearrange("(p n) -> p n", p=128)
        dst = wb[k].rearrange("r c -> (r c)").rearrange("(p n) -> p n", p=128)
        for c0 in range(0, per, 8192):
            c1 = min(per, c0 + 8192)
            S.dma("pool", lambda e, s_=src[:, c0:c1], d_=dst[:, c0:c1]: e.dma_start(out=d_, in_=s_),
                  "wc_" + k, writes=[("wb", k, c0)])

    for k in ("w0kv", "wukP", "wuv", "w0nak", "w0nav"):
        cast_weight(k)
    CAST_AT = {1: ["w0q", "wqup"], 2: ["wout0"], 3: ["wgu0"], 7: ["wdn0"], 10: ["w1k", "w1v"]}

    def ld(dst_ap, src_ap, key, slot):
        S.dma("sp", lambda e: e.dma_start(out=dst_ap, in_=src_ap), slot, writes=[key])

    ld(qnorm[:], qnorm_d, "qnorm", "c0_qnorm")
    ld(kvnorm[:], kvnorm_d, "kvnorm", "c0_kvnorm")
    ld(fnorm[:], fnorm_d, "fnorm", "c0_fnorm")
    ld(esink[:], sinks_d, "esink", "c0_esink")
    ld(silc[:], cvec.rearrange("p k c -> p (k c)"), "silc", "c0_silc")
    S.dma("sp", lambda e: e.dma_start(out=modv[:, 0:192], in_=modb.rearrange("p l m c -> p (l m c)")), "c0_modv",
          writes=[("modv", 0), ("modv", 1)])
    S.op("act", lambda e: e.activation(out=esink[:], in_=esink[:], func=AF.Exp), reads=["esink"], writes=["esink"])
    S.op("act", lambda e: e.activation(out=silc[:], in_=silc[:], func=AF.Silu), reads=["silc"], writes=["silc"])
    for i in range(2):
        S.dma("sp", lambda e, i=i: e.dma_start(out=tmpf[:, i, :], in_=swamask_d[i]), "c1", writes=["tmpf"])
    S.op("act", lambda e: e.activation(out=swam[:, :, :], in_=tmpf[:, :, :], func=AF.Copy),
         reads=["tmpf"], writes=["swam"])


    def _finish():
        S.emit(st)
        st.close()
        print("instructions per engine:", {e: len(S.streams[e]) for e in ENGS})
        return nc
    if stage < 2:
        return _finish()
    def mod_blocks(l, ms, bank):
        for m in ms:
            xs = xstage[m % 2]
            xk = "xstage%d" % (m % 2)
            S.dma("sp", lambda e, xs=xs, m=m: e.dma_start(
                out=xs[:, :].rearrange("p (k c) -> p k c", k=8),
                in_=modw[l].rearrange("(k p) c -> p k c", p=128)[:, :, m * 128:(m + 1) * 128]),
                "xst%d" % (m % 2), writes=[xk])
            for kc in range(8):
                S.op("pe", lambda e, xs=xs, kc=kc, m=m: e.matmul(
                    psb[bank][:, 2 * m:2 * m + 2], lhsT=xs[:, kc * 128:(kc + 1) * 128],
                    rhs=silc[:, 2 * kc:2 * kc + 2], start=(kc == 0), stop=(kc == 7)),
                    reads=[xk, "silc"], writes=[pkey(bank)])

    def mod_finish(l, bank):
        S.op("dve", lambda e: e.tensor_tensor(
            out=modv[:, 96 * l:96 * l + 96], in0=psb[bank][:, 0:96], in1=modv[:, 96 * l:96 * l + 96], op=ALU.add),
            reads=[pkey(bank), ("modv", l)], writes=[("modv", l)])
        S.op("dve", lambda e: e.tensor_scalar_add(out=onep[:, 96 * l:96 * l + 96], in0=modv[:, 96 * l:96 * l + 96],
                                                  scalar1=1.0), reads=[("modv", l)], writes=[("onep", l)])

    def mod_big(l, bank):
        for blk in range(12):
            hbk = hbuf[blk % 2]
            hkk = "hbuf%d" % (blk % 2)
            S.dma("sp", lambda e, hbk=hbk, blk=blk: e.dma_start(
                out=hbk[:, :, :], in_=modw[l].rearrange("(k p) c -> p k c", p=128)[:, :, blk * 512:(blk + 1) * 512]),
                "modld%d" % (blk % 2), writes=[(hkk, kc) for kc in range(8)])
            for mm_ in range(4):
                m = blk * 4 + mm_
                for kc in range(8):
                    S.op("pe", lambda e, hbk=hbk, kc=kc, m=m, mm_=mm_: e.matmul(
                        psb[bank][:, 2 * m:2 * m + 2], lhsT=hbk[:, kc, mm_ * 128:(mm_ + 1) * 128],
                        rhs=silc[:, 2 * kc:2 * kc + 2], start=(kc == 0), stop=(kc == 7)),
                        reads=[(hkk, kc), "silc"], writes=[pkey(bank)])

    bank0 = mm_bank()
    mod_big(0, bank0)
    mod_finish(0, bank0)

    if stage < 3:
        return _finish()
    def build_strip(h, v):
        S.dma("sp", lambda e: e.dma_start(out=xstage[0][0:64, 0:960], in_=nabias[h, v]),
              "xst0", writes=["xstage0"])
        S.dma("sp", lambda e: e.dma_start(out=xstage[1][0:64, 0:960], in_=namask[v]),
              "xst1", writes=["xstage1"])
        S.op("pool", lambda e: e.tensor_scalar_mul(out=xstage[0][0:64, 0:960], in0=xstage[0][0:64, 0:960], scalar1=8.0),
             reads=["xstage0"], writes=["xstage0"])
        S.op("pool", lambda e: e.tensor_tensor(out=naedge[0:64, 0, :], in0=xstage[0][0:64, 0:960],
                                               in1=xstage[1][0:64, 0:960], op=ALU.add),
             reads=["xstage0", "xstage1"], writes=["naedge0"])
        S.dma("pool", lambda e: e.dma_start(out=nastr_d[h, v], in_=naedge[0:64, 0, :]), "nb3",
              reads=["naedge0"], writes=["nastr_d"])

    def load_T(src_rows_ap, nblk, dst, dkey):
        for tb in range(nblk):
            xs = xstage[tb % 2]
            xk = "xstage%d" % (tb % 2)
            S.dma("sp", lambda e, xs=xs, tb=tb: e.dma_start(out=xs[:, :], in_=src_rows_ap[tb * 128:(tb + 1) * 128, :]),
                  "xst%d" % (tb % 2), writes=[xk])
            for g in range(2):
                bank = mm_bank()
                for q in range(4):
                    kc = 4 * g + q
                    S.op("pe", lambda e, xs=xs, kc=kc, q=q, bank=bank: e.transpose(
                        out=psb[bank][:, q * 128:(q + 1) * 128], in_=xs[:, kc * 128:(kc + 1) * 128],
                        identity=ident_f[:]), reads=[xk, "ident_f"], writes=[pkey(bank)])
                eng = "dve" if g == 0 else "act"
                if eng == "dve":
                    S.op("dve", lambda e, g=g, tb=tb, bank=bank: e.tensor_copy(
                        out=dst[:, 4 * g:4 * g + 4, tb * 128:(tb + 1) * 128],
                        in_=psb[bank][:, :].rearrange("p (q t) -> p q t", q=4)),
                        reads=[pkey(bank)], writes=[(dkey, 4 * g + q) for q in range(4)])
                else:
                    S.op("act", lambda e, g=g, tb=tb, bank=bank: e.activation(
                        out=dst[:, 4 * g:4 * g + 4, tb * 128:(tb + 1) * 128],
                        in_=psb[bank][:, :].rearrange("p (q t) -> p q t", q=4), func=AF.Copy),
                        reads=[pkey(bank)], writes=[(dkey, 4 * g + q) for q in range(4)])

    def load_T_big(src_rows_ap, dst, dkey, stg, stgkey):
        stgv = stg[:, :, :].rearrange("p k t -> p (k t)").rearrange("p (b d) -> p b d", b=4)
        skeys = [(stgkey, kc) for kc in range(8)]
        S.dma("sp", lambda e: e.dma_start(out=stgv, in_=src_rows_ap.rearrange("(b p) d -> p b d", p=128)),
              "xbig", writes=skeys)
        for tb in range(4):
            for g in range(2):
                bank = mm_bank()
                for q in range(4):
                    kc = 4 * g + q
                    S.op("pe", lambda e, kc=kc, q=q, bank=bank, tb=tb: e.transpose(
                        out=psb[bank][:, q * 128:(q + 1) * 128], in_=stgv[:, tb, kc * 128:(kc + 1) * 128],
                        identity=ident_f[:]), reads=skeys + ["ident_f"], writes=[pkey(bank)])
                if g == 0:
                    S.op("dve", lambda e, g=g, tb=tb, bank=bank: e.tensor_copy(
                        out=dst[:, 4 * g:4 * g + 4, tb * 128:(tb + 1) * 128],
                        in_=psb[bank][:, :].rearrange("p (q t) -> p q t", q=4)),
                        reads=[pkey(bank)], writes=[(dkey, 4 * g + q) for q in range(4)])
                else:
                    S.op("act", lambda e, g=g, tb=tb, bank=bank: e.activation(
                        out=dst[:, 4 * g:4 * g + 4, tb * 128:(tb + 1) * 128],
                        in_=psb[bank][:, :].rearrange("p (q t) -> p q t", q=4), func=AF.Copy),
                        reads=[pkey(bank)], writes=[(dkey, 4 * g + q) for q in range(4)])

    def rms_stats(src_fn, skeys, KC, N, nfeat):
        bank = mm_bank()
        for kc in range(KC):
            S.op("act", lambda e, kc=kc: e.activation(out=sqb[:, kc % 2, :N], in_=src_fn(kc), func=AF.Square),
                 reads=[skeys[kc]], writes=[("sqb", kc % 2)])
            S.op("pe", lambda e, kc=kc, bank=bank: e.matmul(
                psb[bank][:, :N], lhsT=ones_b[:, :], rhs=sqb[:, kc % 2, :N], start=(kc == 0), stop=(kc == KC - 1)),
                reads=[("sqb", kc % 2), "ones_b"], writes=[pkey(bank)])
        S.op("dve", lambda e, bank=bank: e.tensor_scalar(
            out=rstd[:, :N], in0=psb[bank][:, :N], scalar1=1.0 / nfeat, scalar2=EPS, op0=ALU.mult, op1=ALU.add),
            reads=[pkey(bank)], writes=["rstd"])
        S.op("act", lambda e: e.activation(out=rstd[:, :N], in_=rstd[:, :N], func=AF.Sqrt),
             reads=["rstd"], writes=["rstd"])
        S.op("dve", lambda e: e.reciprocal(out=rstd[:, :N], in_=rstd[:, :N]), reads=["rstd"], writes=["rstd"])

    def norm_mod(hsrc, hkey, N, l, wsh, wsc, col, dst, dkey, eng="dve"):
        rms_stats(lambda kc: hsrc[:, kc, :N], [(hkey, kc) for kc in range(8)], 8, N, D)
        for kc in range(8):
            i_sc = mvi(l, wsc, kc, col)
            i_sh = mvi(l, wsh, kc, col)
            if eng == "pool" and kc % 2 == 1:
                S.op("pool", lambda e, kc=kc, i_sc=i_sc: e.tensor_scalar_mul(
                    out=tmpf[:, kc % 2, :N], in0=hsrc[:, kc, :N], scalar1=onep[:, i_sc:i_sc + 1]),
                    reads=[(hkey, kc), ("onep", l)], writes=[("tmpf", kc % 2)])
                S.op("pool", lambda e, kc=kc: e.tensor_tensor(
                    out=tmpf[:, kc % 2, :N], in0=tmpf[:, kc % 2, :N], in1=rstd[:, :N], op=ALU.mult),
                    reads=[("tmpf", kc % 2), "rstd"], writes=[("tmpf", kc % 2)])
            else:
                S.op("dve", lambda e, kc=kc, i_sc=i_sc: e.scalar_tensor_tensor(
                    out=tmpf[:, kc % 2, :N], in0=hsrc[:, kc, :N], scalar=onep[:, i_sc:i_sc + 1], in1=rstd[:, :N],
                    op0=ALU.mult, op1=ALU.mult),
                    reads=[(hkey, kc), "rstd", ("onep", l)], writes=[("tmpf", kc % 2)])
            S.op("act", lambda e, kc=kc, i_sh=i_sh: e.activation(
                out=dst[:, kc, :N], in_=tmpf[:, kc % 2, :N], func=AF.Identity, bias=modv[:, i_sh:i_sh + 1], scale=1.0),
                reads=[("tmpf", kc % 2), ("modv", l)], writes=[(dkey, kc)])

    def wload(wname, KC, c0, ncols, k0=0):
        s = cnt["ws"] % NWS
        cnt["ws"] += 1
        view = wslot[s][:, 0:KC * ncols].rearrange("p (k c) -> p k c", k=KC)
        P = min(128, WSHAPES[wname][0])
        src = wb[wname].rearrange("(k p) c -> p k c", p=P)[:, k0:k0 + KC, c0:c0 + ncols]
        S.dma("sp", lambda e: e.dma_start(out=view[:P], in_=src), "ws%d" % s,
              reads=[("wb", wname, q0) for q0 in range(0, WSHAPES[wname][0] * WSHAPES[wname][1] // 128, 8192)],
              writes=["wslot%d" % s])
        return view, "wslot%d" % s

    def linear(wname, KC, rhs_fn, rkeys, N, mchunks, evac, blockcols=None, kparts=None):
        if blockcols is None:
            blockcols = max(128, (4096 // KC) // 128 * 128)
        i = 0
        while i < len(mchunks):
            c0 = mchunks[i][0]
            j = i
            while j < len(mchunks) and mchunks[j][0] + mchunks[j][1] - c0 <= blockcols:
                j += 1
            ncols = mchunks[j - 1][0] + mchunks[j - 1][1] - c0
            view, wkey = wload(wname, KC, c0, ncols)
            for ii in range(i, j):
                mc0, mn = mchunks[ii]
                bank = mm_bank()
                for kc in range(KC):
                    kp = 128 if kparts is None else kparts
                    S.op("pe", lambda e, view=view, kc=kc, mc0=mc0, mn=mn, c0=c0, bank=bank, kp=kp: e.matmul(
                        psb[bank][:mn, :N], lhsT=view[:kp, kc, mc0 - c0:mc0 - c0 + mn], rhs=rhs_fn(kc),
                        start=(kc == 0), stop=(kc == KC - 1)),
                        reads=[wkey, rkeys[kc]], writes=[pkey(bank)])
                evac(ii, psb[bank][:mn, :N], pkey(bank))
            i = j

    def linear_tm(wname, KC, lhs_fn, lkeys, ntb, tbsz, ncols, evac):
        view, wkey = wload(wname, KC, 0, ncols)
        for tb in range(ntb):
            bank = mm_bank()
            for kc in range(KC):
                S.op("pe", lambda e, kc=kc, tb=tb, bank=bank: e.matmul(
                    psb[bank][:tbsz, :ncols], lhsT=lhs_fn(kc, tb), rhs=view[:, kc, :ncols],
                    start=(kc == 0), stop=(kc == KC - 1)),
                    reads=[wkey, lkeys[kc]], writes=[pkey(bank)])
            evac(tb, psb[bank][:tbsz, :ncols], pkey(bank))

    def copy_evac(dst_fn, dkey_fn, alt=True):
        def ev(i, ps, bk):
            if alt and i % 2 == 1:
                S.op("act", lambda e: e.activation(out=dst_fn(i), in_=ps, func=AF.Copy),
                     reads=[bk], writes=[dkey_fn(i)])
            else:
                S.op("dve", lambda e: e.tensor_copy(out=dst_fn(i), in_=ps), reads=[bk], writes=[dkey_fn(i)])
        return ev

    ropeT = [sb("ropeT%d" % i, [128, T], F32) for i in range(1)]

    NPT = len(pT)

    def run_attn(blocks, skew=2, defer=3):
        deferred = []
        nb = len(blocks)
        for k in range(nb + skew):
            if k < nb:
                b = blocks[k]
                if b.get("pre"):
                    b["pre"]()
                bank = mm_bank()
                np_, c0, c1 = b["np_"], b["c0"], b["c1"]
                n = len(b["qk"])
                for i, (lhsT, rhs, rk, oc0, oc1) in enumerate(b["qk"]):
                    o_ap = psb[bank][:np_, oc0:oc1]
                    if len(rhs.shape) == 3:
                        o_ap = o_ap.rearrange("p (h t) -> p h t", h=rhs.shape[1])
                    S.op("pe", lambda e, lhsT=lhsT, rhs=rhs, i=i, o_ap=o_ap, n=n: e.matmul(
                        o_ap, lhsT=lhsT, rhs=rhs, start=(i == 0), stop=(i == n - 1)),
                        reads=rk, writes=[pkey(bank)])
                pi = cnt["pt"] % NPT
                cnt["pt"] += 1
                b["pi"] = pi
                S.op("act", lambda e, bank=bank, pi=pi, np_=np_, c0=c0, c1=c1, sc=b["scale"]: e.activation(
                    out=pT[pi][:np_, c0:c1], in_=psb[bank][:np_, c0:c1], func=AF.Exp, scale=sc),
                    reads=[pkey(bank)], writes=["pT%d" % pi])
            still = []
            for (when, fn) in deferred:
                if when <= k:
                    fn()
                else:
                    still.append((when, fn))
            deferred = still
            kk = k - skew
            if kk >= 0:
                b = blocks[kk]
                pi, np_, c0, c1 = b["pi"], b["np_"], b["c0"], b["c1"]
                acc = b["acc"]
                S.op("pe", lambda e, b=b, pi=pi, np_=np_, c0=c0, c1=c1, acc=acc: e.matmul(
                    psb[acc][:, c0:c1], lhsT=b["v_lhsT"], rhs=pT[pi][:np_, c0:c1], start=b["first"], stop=b["last"]),
                    reads=["pT%d" % pi] + b["vkeys"], writes=[pkey(acc)])
                if b.get("epi"):
                    fn = b["epi"]()
                    if fn is not None:
                        deferred.append((k + defer, fn))
        for (_, fn) in deferred:
            fn()

    acc_rot = {"i": 0}

    def next_acc():
        i = acc_rot["i"] % 4
        acc_rot["i"] += 1
        return 4 + i

    def pair_epilogue(accA, accB, N, final_fn, sink_heads=None):
        for (acc, lo) in ((accA, 64), (accB, 0)):
            if sink_heads is None:
                S.op("dve", lambda e, acc=acc, lo=lo: e.reciprocal(out=recb[lo:lo + 64, 0, :N], in_=psb[acc][lo:lo + 64, :N]),
                     reads=[pkey(acc)], writes=[("recb", 0)])
            else:
                heads = sink_heads[0] if lo == 64 else sink_heads[1]
                for hh, head in enumerate(heads):
                    S.op("dve", lambda e, acc=acc, lo=lo, hh=hh, head=head: e.tensor_scalar_add(
                        out=recb[lo:lo + 64, 0, hh * 128:(hh + 1) * 128], in0=psb[acc][lo:lo + 64, hh * 128:(hh + 1) * 128],
                        scalar1=esink[lo:lo + 64, head:head + 1]),
                        reads=[pkey(acc), "esink"], writes=[("recb", 0)])
                S.op("dve", lambda e, lo=lo: e.reciprocal(out=recb[lo:lo + 64, 0, :N], in_=recb[lo:lo + 64, 0, :N]),
                     reads=[("recb", 0)], writes=[("recb", 0)])

        def pe_part():
            bank = mm_bank()
            S.op("pe", lambda e: e.matmul(psb[bank][:, :N], lhsT=pswap[:, :], rhs=recb[:, 0, :N], start=True, stop=True),
                 reads=[("recb", 0), "pswap"], writes=[pkey(bank)])
            S.op("act", lambda e: e.activation(out=recb[:, 1, :N], in_=psb[bank][:, :N], func=AF.Copy),
                 reads=[pkey(bank)], writes=[("recb", 1)])
            final_fn(recb[:, 1, :N], ("recb", 1))
        return pe_part

    KV_KEYS = []
    for t_ in range(17):
        k0_ = t_ * T
        KV_KEYS.append(("kr_d", k0_))
        KV_KEYS += [("khp_d", k0_, i_) for i_ in range(4)]
        KV_KEYS += [("vh_d", k0_, tb_) for tb_ in range(4)]

    def mla_heads_tile(N, key_chunks, cat, catkey):
        scale = 96.0 ** -0.5
        blocks = []
        nblk_tot = sum(nb for (_, nb) in key_chunks)
        accs = {}
        for h in range(8):
            i, par = h // 2, h % 2
            acc = next_acc()
            accs[h] = acc
            done = 0
            for (k0, nb) in key_chunks:
                s = cnt["uid"] % 2
                cnt["uid"] += 1

                def pre(s=s, k0=k0, nb=nb, h=h, i=i):
                    S.dma("sp", lambda e: e.dma_start(out=mck[s][0:64, :nb * 128],
                                                      in_=khp_d[i, 64 * (h % 2):64 * (h % 2) + 64, k0:k0 + nb * 128]),
                          "mck%d" % s, reads=KV_KEYS, writes=["mck%d" % s])
                    S.dma("sp", lambda e: e.dma_start(out=mck[s][64:96, :nb * 128], in_=kr_d[:, k0:k0 + nb * 128]),
                          "mck%d" % s, reads=KV_KEYS, writes=["mck%d" % s])
                    S.dma("sp", lambda e: e.dma_start(
                        out=mv[s][:, :nb, :],
                        in_=vh_d[k0:k0 + nb * 128, 192 * i:192 * i + 192].rearrange("(b p) c -> p b c", p=128)),
                        "mv%d" % s, reads=KV_KEYS, writes=["mv%d" % s])
                for kb in range(nb):
                    qk = [(mck[s][:, kb * 128:(kb + 1) * 128], Qh[:, h, :N], ["mck%d" % s, ("Qh", h)], 0, N)]
                    blk = dict(np_=128, c0=0, c1=N, qk=qk, scale=scale,
                               v_lhsT=mv[s][:, kb, 64 * par:64 * par + 128], vkeys=["mv%d" % s],
                               acc=acc, first=(done == 0), last=(done == nblk_tot - 1),
                               pre=(pre if kb == 0 else None))
                    done += 1
                    blocks.append(blk)
            if par == 1:
                def epi(i=i, accA=accs[h - 1], accB=acc):
                    def fin(rec, rkey):
                        for (p0, a_) in ((0, accA), (64, accB)):
                            S.op("dve", lambda e, p0=p0, a_=a_: e.tensor_tensor(
                                out=cat[p0:p0 + 64, i, :N], in0=psb[a_][p0:p0 + 64, :N], in1=rec[p0:p0 + 64, :],
                                op=ALU.mult), reads=[pkey(a_), rkey], writes=[(catkey, i)])
                    return pair_epilogue(accA, accB, N, fin)
                blocks[-1]["epi"] = epi
        run_attn(blocks)

    NA_KEYS = [("nakT_d", t_ * T, i_) for t_ in range(NT_NAKV) for i_ in range(4)] + \
              [("nav_d", t_ * T, tb_) for t_ in range(NT_NAKV) for tb_ in range(4)]

    def na_heads_tile(j, N, cat, catkey, latent=True):
        scale = 0.125
        blocks = []
        for i in range(4):
            pre = None
            if latent:
                nqr = N // 64
                w0 = max(0, 8 * j - 4)
                w1 = min(NT_NAKV * 8, 8 * j + nqr + 4)
                nr = w1 - w0
                s = i % 2

                def pre(s=s, i=i, w0=w0, nr=nr):
                    S.dma("sp", lambda e: e.dma_start(out=nakw[:, s, :nr * 64], in_=nakT_d[:, i, w0 * 64:(w0 + nr) * 64]),
                          "nakw%d" % s, reads=NA_KEYS, writes=["nakw%d" % s])
                    S.dma("sp", lambda e: e.dma_start(
                        out=navw[:, s, :nr * 192].rearrange("p (r f) -> p r f", f=192),
                        in_=nav_d[w0 * 64:(w0 + nr) * 64, i * 192:(i + 1) * 192].rearrange("(r t) f -> t r f", t=64)),
                        "navw%d" % s, reads=NA_KEYS, writes=["navw%d" % s])
                    for pp in range(2):
                        S.dma("sp", lambda e, pp=pp: e.dma_start(
                            out=nastrip[64 * pp:64 * pp + 64, s, :], in_=nastr_d[2 * i + pp, 0]),
                            "nastrip%d" % s, reads=["nastr_d"], writes=["nastrip%d" % s])
                        if j == 0:
                            S.dma("sp", lambda e, pp=pp: e.dma_start(
                                out=naedge[64 * pp:64 * pp + 64, 0, :], in_=nastr_d[2 * i + pp, 1]),
                                "naedge0", reads=["nastr_d"], writes=["naedge0"])
            for par in range(2):
                h = 2 * i + par
                p0 = 64 * par
                acc = next_acc()
                if par == 0:
                    accA = acc
                vo = 192 * i + 64 * par
                hb = []
                for cb in range(2):
                    qk = [(nakc[p0:p0 + 64, i, cb * 128:(cb + 1) * 128], naq[p0:p0 + 64, i, :N],
                           [("nakc", i), ("naq", i)], 0, N)]
                    hb.append(dict(np_=128, c0=0, c1=N, qk=qk, scale=scale, v_lhsT=navc[:, cb, vo:vo + 128],
                                   vkeys=["navc"], acc=acc))
                if latent:
                    for kr in range(w0, w1):
                        if j == 0:
                            r0, r1 = (0, 7) if kr <= 7 else (kr - 4, 7)
                        else:
                            r0, r1 = max(8 * j, kr - 4), min(8 * j + nqr - 1, kr + 4)
                        if r0 > r1:
                            continue
                        a, b = r0 - 8 * j, r1 - 8 * j + 1
                        c0, c1 = a * 64, b * 64
                        lk = (kr - w0) * 64
                        segs = []
                        if j == 0 and a < 4:
                            be = min(b, 4)
                            segs.append((a, be, naedge[p0:p0 + 64, 0, :], "naedge0"))
                            if b > 4:
                                segs.append((4, b, nastrip[p0:p0 + 64, s, :], "nastrip%d" % s))
                        else:
                            segs.append((a, b, nastrip[p0:p0 + 64, s, :], "nastrip%d" % s))
                        qk = [(nakw[p0:p0 + 64, s, lk:lk + 64], naq[p0:p0 + 64, i, c0:c1],
                               ["nakw%d" % s, ("naq", i)], c0, c1)]
                        for (sa, sb_, strip, skey) in segs:
                            e0 = (7 - kr + 8 * j + sa) * 64
                            qk.append((ident_b[p0:p0 + 64, p0:p0 + 64], strip[:, e0:e0 + (sb_ - sa) * 64],
                                       [skey, "ident_b"], sa * 64, sb_ * 64))
                        vw = (kr - w0) * 192 + 64 * par
                        hb.append(dict(np_=64, c0=c0, c1=c1, qk=qk, scale=scale,
                                       v_lhsT=navw[:, s, vw:vw + 128], vkeys=["navw%d" % s], acc=acc))
                for bi, blk in enumerate(hb):
                    blk["first"] = (bi == 0)
                    blk["last"] = (bi == len(hb) - 1)
                if par == 0 and pre is not None:
                    hb[0]["pre"] = pre

                if par == 1:
                    def epi(i=i, accA=accA, accB=acc):
                        def fin(rec, rkey):
                            for (q0, a_) in ((0, accA), (64, accB)):
                                S.op("dve", lambda e, q0=q0, a_=a_: e.tensor_tensor(
                                    out=cat[q0:q0 + 64, 4 + i, :N], in0=psb[a_][q0:q0 + 64, :N], in1=rec[q0:q0 + 64, :],
                                    op=ALU.mult), reads=[pkey(a_), rkey], writes=[(catkey, 4 + i)])
                        return pair_epilogue(accA, accB, N, fin)
                    hb[-1]["epi"] = epi
                blocks.extend(hb)
        run_attn(blocks)

    def gated_residual(hdst, hkey, N, l, wg, col):
        def ev(i, ps, bk):
            ig = mvi(l, wg, i, col)
            S.op("dve", lambda e: e.scalar_tensor_tensor(
                out=hdst[:, i, :N], in0=ps, scalar=modv[:, ig:ig + 1], in1=hdst[:, i, :N], op0=ALU.mult, op1=ALU.add),
                reads=[bk, (hkey, i), ("modv", l)], writes=[(hkey, i)])
        return ev

    def ffn(hdst, hkey, N, l, col, ab, abkey):
        norm_mod(hdst, hkey, N, l, 3, 4, col, ab, abkey)
        wn = "wgu%d" % l
        for f0 in range(0, 22, 2):
            nf = min(2, 22 - f0)
            gview, gkey = wload(wn, 8, f0 * 128, nf * 128)
            uview, ukey = wload(wn, 8, DFF + f0 * 128, nf * 128)
            for ff in range(nf):
                fi = f0 + ff
                bg = mm_bank()
                for kc in range(8):
                    S.op("pe", lambda e, kc=kc, ff=ff, bg=bg, gview=gview: e.matmul(
                        psb[bg][:, :N], lhsT=gview[:, kc, ff * 128:(ff + 1) * 128], rhs=ab[:, kc, :N],
                        start=(kc == 0), stop=(kc == 7)), reads=[gkey, (abkey, kc)], writes=[pkey(bg)])
                bu = mm_bank()
                for kc in range(8):
                    S.op("pe", lambda e, kc=kc, ff=ff, bu=bu, uview=uview: e.matmul(
                        psb[bu][:, :N], lhsT=uview[:, kc, ff * 128:(ff + 1) * 128], rhs=ab[:, kc, :N],
                        start=(kc == 0), stop=(kc == 7)), reads=[ukey, (abkey, kc)], writes=[pkey(bu)])
                S.op("act", lambda e, fi=fi, bg=bg: e.activation(out=silt[:, fi % 2, :N], in_=psb[bg][:, :N],
                                                                 func=AF.Silu),
                     reads=[pkey(bg)], writes=[("silt", fi % 2)])
                S.op("dve", lambda e, fi=fi, bu=bu: e.tensor_tensor(out=actT[:, fi, :N], in0=psb[bu][:, :N],
                                                                   in1=silt[:, fi % 2, :N], op=ALU.mult),
                     reads=[pkey(bu), ("silt", fi % 2)], writes=[("actT", fi)])
        linear("wdn%d" % l, 22, lambda kc: actT[:, kc, :N], [("actT", kc) for kc in range(22)], N,
               [(c * 128, 128) for c in range(8)], gated_residual(hdst, hkey, N, l, 5, col), blockcols=128)

    def rope_pair(k, ps, bk, npp, N, dst_ap, dkey):
        ii = k // 2
        if k % 2 == 0:
            S.op("dve", lambda e: e.tensor_tensor(out=ropeT[0][:npp, :N], in0=ps, in1=ropeC[:npp, :N], op=ALU.mult),
                 reads=[bk, "ropeC"], writes=[("ropeT", 0)])
        else:
            S.op("dve", lambda e: e.tensor_tensor(out=ropeS[:npp, 1, :N], in0=ps, in1=ropeS[:npp, 0, :N], op=ALU.mult),
                 reads=[bk, "ropeS0"], writes=["ropeS1"])
            S.op("dve", lambda e: e.tensor_tensor(out=dst_ap, in0=ropeT[0][:npp, :N], in1=ropeS[:npp, 1, :N],
                                                  op=ALU.add),
                 reads=[("ropeT", 0), "ropeS1"], writes=[dkey])

    def l0_q_side(ab, abkey, N, tok0, rotate):
        def ev(i, ps, bk):
            if i < 2:
                S.op("dve", lambda e: e.tensor_copy(out=cq[:, i, :N], in_=ps), reads=[bk], writes=[("tmpf", i)])
            else:
                S.op("act", lambda e: e.activation(out=naq[:, i - 2, :N], in_=ps, func=AF.Copy),
                     reads=[bk], writes=[("naq", i - 2)])
        linear("w0q", 8, lambda kc: ab[:, kc, :N], [(abkey, kc) for kc in range(8)], N,
               [(c * 128, 128) for c in range(6)], ev)
        rms_stats(lambda kc: cq[:, kc, :N], [("tmpf", kc) for kc in range(2)], 2, N, 256)
        for kc in range(2):
            S.op("dve", lambda e, kc=kc: e.scalar_tensor_tensor(
                out=cqn[:, kc, :N], in0=cq[:, kc, :N], scalar=qnorm[:, kc:kc + 1], in1=rstd[:, :N],
                op0=ALU.mult, op1=ALU.mult), reads=[("tmpf", kc), "rstd", "qnorm"], writes=[("cqn", kc)])
        if rotate:
            mch = [(h * 192 + 96 * v, 96) for h in range(8) for v in range(2)]
            S.dma("sp", lambda e: e.dma_start(out=ropeC[:96, :N], in_=ropeMc[:, tok0:tok0 + N]), "ropeC",
                  writes=["ropeC"])
            S.dma("sp", lambda e: e.dma_start(out=ropeS[:96, 0, :N], in_=ropeMs[:, tok0:tok0 + N]), "ropeS",
                  writes=["ropeS0"])
        else:
            mch = [(h * 192, 96) for h in range(8)]

        def ev2(i, ps, bk):
            if not rotate:
                h = i
                if h % 2 == 0:
                    S.op("dve", lambda e: e.tensor_copy(out=Qh[:, h, :N], in_=ps), reads=[bk], writes=[("Qh", h)])
                else:
                    S.op("act", lambda e: e.activation(out=Qh[:, h, :N], in_=ps, func=AF.Copy),
                         reads=[bk], writes=[("Qh", h)])
                return
            h, v = i // 2, i % 2
            if v == 0:
                S.op("act", lambda e: e.activation(out=Qh[0:64, h, :N], in_=ps[0:64, :], func=AF.Copy),
                     reads=[bk], writes=[("Qh", h)])
                S.op("dve", lambda e: e.tensor_tensor(out=ropeT[0][64:96, :N], in0=ps[64:96, :], in1=ropeC[64:96, :N],
                                                      op=ALU.mult),
                     reads=[bk, "ropeC"], writes=[("ropeT", 0)])
            else:
                S.op("dve", lambda e: e.tensor_tensor(out=ropeS[64:96, 1, :N], in0=ps[64:96, :], in1=ropeS[64:96, 0, :N],
                                                      op=ALU.mult),
                     reads=[bk, "ropeS0"], writes=["ropeS1"])
                S.op("dve", lambda e: e.tensor_tensor(out=Qh[64:96, h, :N], in0=ropeT[0][64:96, :N],
                                                      in1=ropeS[64:96, 1, :N], op=ALU.add),
                     reads=[("ropeT", 0), "ropeS1"], writes=[("Qh", h)])
        linear("wqup", 2, lambda kc: cqn[:, kc, :N], [("cqn", kc) for kc in range(2)], N, mch, ev2, blockcols=1536)

    def l0_kv(ab, abkey, N, key0, tok0, rotate):
        if rotate:
            S.dma("sp", lambda e: e.dma_start(out=ropeC[:32, :N], in_=ropeMc[0:32, tok0:tok0 + N]), "ropeC",
                  writes=["ropeC"])
            S.dma("sp", lambda e: e.dma_start(out=ropeS[:32, 0, :N], in_=ropeMs[0:32, tok0:tok0 + N]), "ropeS",
                  writes=["ropeS0"])

        def ev(i, ps, bk):
            if i == 0:
                S.op("dve", lambda e: e.tensor_copy(out=ckvf[:, :N], in_=ps), reads=[bk], writes=[("recb", 0)])
            elif not rotate:
                S.op("act", lambda e: e.activation(out=pT[1][:32, :N], in_=ps, func=AF.Copy),
                     reads=[bk], writes=["pT1"])
            else:
                rope_pair(i - 1, ps, bk, 32, N, pT[1][:32, :N], "pT1")
        mch = [(0, 128), (128, 32)] + ([(160, 32)] if rotate else [])
        linear("w0kv", 8, lambda kc: ab[:, kc, :N], [(abkey, kc) for kc in range(8)], N, mch, ev)
        rms_stats(lambda kc: ckvf[:, :N], [("recb", 0)], 1, N, 128)
        S.op("dve", lambda e: e.scalar_tensor_tensor(out=pT[0][:, :N], in0=ckvf[:, :N], scalar=kvnorm[:, 0:1],
                                                     in1=rstd[:, :N], op0=ALU.mult, op1=ALU.mult),
             reads=[("recb", 0), "rstd", "kvnorm"], writes=["pT0"])
        S.dma("pool", lambda e: e.dma_start(out=kr_d[:, key0:key0 + N], in_=pT[1][:32, :N]),
              "kvo1", reads=["pT1"], writes=[("kr_d", key0)])
        ukv, ukkey = wload("wukP", 1, 0, 512)
        for i in range(4):
            bank = mm_bank()
            S.op("pe", lambda e, i=i, bank=bank: e.matmul(psb[bank][:, :N], lhsT=ukv[:, 0, i * 128:(i + 1) * 128],
                                                          rhs=pT[0][:, :N], start=True, stop=True),
                 reads=[ukkey, "pT0"], writes=[pkey(bank)])
            st_ = pT[2]
            sk = "pT2"
            if i % 2 == 0:
                S.op("dve", lambda e, st_=st_, bank=bank: e.tensor_copy(out=st_[:, :N], in_=psb[bank][:, :N]),
                     reads=[pkey(bank)], writes=[sk])
            else:
                S.op("act", lambda e, st_=st_, bank=bank: e.activation(out=st_[:, :N], in_=psb[bank][:, :N],
                                                                       func=AF.Copy),
                     reads=[pkey(bank)], writes=[sk])
            S.dma("pool", lambda e, i=i, st_=st_: e.dma_start(out=khp_d[i, :, key0:key0 + N], in_=st_[:, :N]),
                  "kvo0", reads=[sk], writes=[("khp_d", key0, i)])
        uvv, uvkey = wload("wuv", 1, 0, 512)
        for tb in range(N // 128):
            bank = mm_bank()
            S.op("pe", lambda e, tb=tb, bank=bank: e.matmul(psb[bank][:, :512], lhsT=pT[0][:, tb * 128:(tb + 1) * 128],
                                                            rhs=uvv[:, 0, :], start=True, stop=True),
                 reads=[uvkey, "pT0"], writes=[pkey(bank)])
            v_evac(psb[bank][:, :512], pkey(bank), vstage[:, :], "vstage")
            S.dma("pool", lambda e, tb=tb: e.dma_start(out=vh_d[key0 + tb * 128:key0 + (tb + 1) * 128, :], in_=vstage[:, :]),
                  "kvo2", reads=["vstage"], writes=[("vh_d", key0, tb)])

    def v_evac(ps, bk, dst2d, dkey):
        src = ps.rearrange("p (i t c) -> p i t c", t=2, c=64)
        dst = dst2d.rearrange("p (i u) -> p i u", u=192)
        S.op("dve", lambda e: e.tensor_copy(out=dst[:, :, 0:64], in_=src[:, :, 0, :]), reads=[bk], writes=[dkey])
        S.op("act", lambda e: e.activation(out=dst[:, :, 128:192], in_=src[:, :, 1, :], func=AF.Copy),
             reads=[bk], writes=[dkey])

    def l0_nakv(ab, abkey, N, tok0, to_ctx):
        if to_ctx:
            linear("w0nak", 8, lambda kc: ab[:, kc, :N], [(abkey, kc) for kc in range(8)], N,
                   [(c * 128, 128) for c in range(4)],
                   copy_evac(lambda i: nakc[:, i, :N], lambda i: ("nakc", i)))
            linear_tm("w0nav", 8, lambda kc, tb: ab[:, kc, tb * 128:(tb + 1) * 128],
                      [(abkey, kc) for kc in range(8)], N // 128, 128, 512,
                      lambda tb, ps, bk: v_evac(ps, bk, navc[:, tb, :], "navc"))
            return

        def evk(i, ps, bk):
            S.op("act" if i % 2 else "dve",
                 (lambda e: e.activation(out=pT[i % 2][:, :N], in_=ps, func=AF.Copy)) if i % 2 else
                 (lambda e: e.tensor_copy(out=pT[i % 2][:, :N], in_=ps)),
                 reads=[bk], writes=["pT%d" % (i % 2)])
            S.dma("pool", lambda e: e.dma_start(out=nakT_d[:, i, tok0:tok0 + N], in_=pT[i % 2][:, :N]),
                  "nako%d" % (i % 2), reads=["pT%d" % (i % 2)], writes=[("nakT_d", tok0, i)])
        linear("w0nak", 8, lambda kc: ab[:, kc, :N], [(abkey, kc) for kc in range(8)], N,
               [(c * 128, 128) for c in range(4)], evk)

        def evv(tb, ps, bk):
            v_evac(ps, bk, vstage[:, :], "vstage")
            S.dma("pool", lambda e: e.dma_start(out=nav_d[tok0 + tb * 128:tok0 + (tb + 1) * 128, :], in_=vstage[:, :]),
                  "navo", reads=["vstage"], writes=[("nav_d", tok0, tb)])
        linear_tm("w0nav", 8, lambda kc, tb: ab[:, kc, tb * 128:(tb + 1) * 128],
                  [(abkey, kc) for kc in range(8)], N // 128, 128, 512, evv)

    def v1_evac(ps, bk, dst2d, dkey):
        src = ps.rearrange("p (g c) -> p g c", c=64)
        dst = dst2d.rearrange("p (g u) -> p g u", u=192)
        S.op("dve", lambda e: e.tensor_copy(out=dst[:, :, 0:64], in_=src), reads=[bk], writes=[dkey])
        S.op("act", lambda e: e.activation(out=dst[:, :, 128:192], in_=src, func=AF.Copy), reads=[bk], writes=[dkey])

    def l0_finish(hdst, hkey, N, col, cat, catkey, ab2, ab2key):
        linear("wout0", 8, lambda kc: cat[:, kc, :N], [(catkey, kc) for kc in range(8)], N,
               [(c * 128, 128) for c in range(8)], gated_residual(hdst, hkey, N, 0, 2, col))
        ffn(hdst, hkey, N, 0, col, ab2, ab2key)

    def dump(name, src_fn, nchunk, N):
        if name not in dbg_out:
            return
        for kc in range(nchunk):
            src, keys = src_fn(kc)
            S.op("dve", lambda e, src=src: e.tensor_copy(out=ostage[0][:, :N], in_=src), reads=keys, writes=["xstage0"])
            S.dma("pool", lambda e, kc=kc: e.dma_start(out=dbg_out[name][kc * 128:(kc + 1) * 128, :], in_=ostage[0][:, :N]),
                  "dbg", reads=["xstage0"], out=True)

    def prologue_A():
        load_T(ctxl, 2, hctx, "hbuf1")
        norm_mod(hctx, "hbuf1", CTX, 0, 0, 1, 1, abuf[0], "abuf0")
        l0_kv(abuf[0], "abuf0", CTX, S_FULL, 0, False)
        l0_nakv(abuf[0], "abuf0", CTX, 0, True)
        l0_q_side(abuf[0], "abuf0", CTX, 0, False)
        mla_heads_tile(CTX, [(S_FULL, 2)], abuf[1], "abuf1")
        na_heads_tile(0, CTX, abuf[1], "abuf1", latent=False)
        l0_finish(hctx, "hbuf1", CTX, 1, abuf[1], "abuf1", abuf[0], "abuf0")
        dump("hctx1", lambda kc: (hctx[:, kc, :CTX], [("hbuf1", kc)]), 8, CTX)
        norm_mod(hctx, "hbuf1", CTX, 1, 0, 1, 1, abuf[0], "abuf0")
        linear("w1k", 8, lambda kc: abuf[0][:, kc, :CTX], [("abuf0", kc) for kc in range(8)], CTX,
               [(0, 128), (256, 128)], copy_evac(lambda i: k1c[:, i, :], lambda i: ("k1c", i)))
        linear_tm("w1v", 8, lambda kc, tb: abuf[0][:, kc, tb * 128:(tb + 1) * 128], [("abuf0", kc) for kc in range(8)],
                  2, 128, 128, lambda tb, ps, bk: v1_evac(ps, bk, v1c[:, tb, :], "v1c"))


    if stage < 5:
        return _finish()
    NPRO = S_FULL // T

    def pro_head(i):
        for k_ in CAST_AT.get(i, ()):
            cast_weight(k_)
        load_T_big(xl[i * T:(i + 1) * T, :], hbuf[i % 2], "hbuf%d" % (i % 2), hbuf[(i + 1) % 2], "hbuf%d" % ((i + 1) % 2))
        norm_mod(hbuf[i % 2], "hbuf%d" % (i % 2), T, 0, 0, 1, 0, abuf[i % 2], "abuf%d" % (i % 2))
        build_strip(i // 2, i % 2)

    pro_head(0)
    for i in range(NPRO):
        ab = abuf[i % 2]
        ak = "abuf%d" % (i % 2)
        if i + 1 < NPRO:
            pro_head(i + 1)
        l0_kv(ab, ak, T, i * T, i * T, True)
        if i < NT_NAKV:
            l0_nakv(ab, ak, T, i * T, False)

    def l1_kv(j, hb, hk, N=T):
        ab, ak = abuf[0], "abuf0"
        norm_mod(hb, hk, N, 1, 0, 1, 0, ab, ak)
        slot = j % 3
        S.dma("sp", lambda e: e.dma_start(out=ropeC[:, :N], in_=ropeSc[:, j * T:j * T + N]), "ropeC", writes=["ropeC"])
        S.dma("sp", lambda e: e.dma_start(out=ropeS[:, 0, :N], in_=ropeSs[:, j * T:j * T + N]), "ropeS",
              writes=["ropeS0"])

        def ev(i, ps, bk):
            rope_pair(i, ps, bk, 128, N, k1[:, i // 2, slot * T:slot * T + N], ("k1", i // 2, slot))
        linear("w1k", 8, lambda kc: ab[:, kc, :N], [(ak, kc) for kc in range(8)], N,
               [(c * 128, 128) for c in range(4)], ev)
        linear_tm("w1v", 8, lambda kc, tb: ab[:, kc, tb * 128:(tb + 1) * 128], [(ak, kc) for kc in range(8)],
                  N // 128, 128, 128, lambda tb, ps, bk: v1_evac(ps, bk, v1[:, slot * 4 + tb, :], ("v1", slot)))

    def l1_main(t, hb, hk):
        ab, ak = abuf[1], "abuf1"
        norm_mod(hb, hk, T, 1, 0, 1, 0, ab, ak)
        S.dma("sp", lambda e: e.dma_start(out=ropeC[:, :], in_=ropeSc[:, t * T:(t + 1) * T]), "ropeC", writes=["ropeC"])
        S.dma("sp", lambda e: e.dma_start(out=ropeS[:, 0, :], in_=ropeSs[:, t * T:(t + 1) * T]), "ropeS",
              writes=["ropeS0"])

        def evq(i, ps, bk):
            rope_pair(i, ps, bk, 128, T, q1[:, i // 2, :], ("q1", i // 2))
        linear("w1q", 8, lambda kc: ab[:, kc, :], [(ak, kc) for kc in range(8)], T,
               [(c * 128, 128) for c in range(16)], evq)
        cat, catkey = abuf[0], "abuf0"
        blocks = []
        for qb in range(4):
            gq = 4 * t + qb
            for g in range(2):
                for par in range(2):
                    p0 = 64 * par
                    acc = next_acc()
                    if par == 0:
                        accA = acc
                    vo = 192 * g + 64 * par
                    rhs = q1[p0:p0 + 64, 4 * g:4 * g + 4, qb * 128:(qb + 1) * 128]
                    rk = [("q1", 4 * g + c) for c in range(4)]
                    hbl = []
                    for cb in range(2):
                        qk = [(k1c[p0:p0 + 64, g, cb * 128:(cb + 1) * 128], rhs, [("k1c", g)] + rk, 0, 512)]
                        hbl.append(dict(np_=128, c0=0, c1=512, qk=qk, scale=0.125,
                                       v_lhsT=v1c[:, cb, vo:vo + 128], vkeys=["v1c"], acc=acc))
                    for dk in (-1, 0, 1):
                        kb = gq + dk
                        if kb < 0:
                            continue
                        slot = (kb // 4) % 3
                        kcol = slot * T + (kb % 4) * 128
                        qk = [(k1[p0:p0 + 64, g, kcol:kcol + 128], rhs, [("k1", g, slot)] + rk, 0, 512)]
                        if dk != 0:
                            mi = 0 if dk < 0 else 1
                            qk.append((ident_b[:, :], swam[:, mi, :], ["ident_b", "swam"], 0, 512))
                        hbl.append(dict(np_=128, c0=0, c1=512, qk=qk, scale=0.125,
                                       v_lhsT=v1[:, slot * 4 + (kb % 4), vo:vo + 128], vkeys=[("v1", slot)],
                                       acc=acc))
                    for bi, blk in enumerate(hbl):
                        blk["first"] = (bi == 0)
                        blk["last"] = (bi == len(hbl) - 1)

                    if par == 1:
                        def epi(g=g, qb=qb, accA=accA, accB=acc):
                            def fin(rec, rkey):
                                for (q0, a_) in ((0, accA), (64, accB)):
                                    S.op("dve", lambda e, q0=q0, a_=a_: e.tensor_tensor(
                                        out=cat[q0:q0 + 64, 4 * g:4 * g + 4, qb * 128:(qb + 1) * 128],
                                        in0=psb[a_][q0:q0 + 64, :].rearrange("p (h t) -> p h t", h=4),
                                        in1=rec[q0:q0 + 64, :].rearrange("p (h t) -> p h t", h=4), op=ALU.mult),
                                        reads=[pkey(a_), rkey], writes=[(catkey, 4 * g + c) for c in range(4)])
                            sink = ([8 * g + 2 * hh for hh in range(4)], [8 * g + 2 * hh + 1 for hh in range(4)])
                            return pair_epilogue(accA, accB, T, fin, sink_heads=sink)
                        hbl[-1]["epi"] = epi
                    blocks.extend(hbl)
        run_attn(blocks)
        linear("wout1", 8, lambda kc: cat[:, kc, :], [(catkey, kc) for kc in range(8)], T,
               [(c * 128, 128) for c in range(8)], gated_residual(hb, hk, T, 1, 2, 0))
        ffn(hb, hk, T, 1, 0, abuf[1], "abuf1")
        rms_stats(lambda kc: hb[:, kc, :], [(hk, kc) for kc in range(8)], 8, T, D)
        for kc in range(8):
            S.op("dve", lambda e, kc=kc: e.scalar_tensor_tensor(
                out=hb[:, kc, :], in0=hb[:, kc, :], scalar=fnorm[:, kc:kc + 1], in1=rstd[:, :],
                op0=ALU.mult, op1=ALU.mult), reads=[(hk, kc), "rstd", "fnorm"], writes=[(hk, kc)])
        for tb in range(4):
            os_ = ostage[tb % 2]
            ok = "xstage%d" % (tb % 2)
            for g2 in range(2):
                bank = mm_bank()
                for q in range(4):
                    kc = 4 * g2 + q
                    S.op("pe", lambda e, kc=kc, q=q, tb=tb, bank=bank: e.transpose(
                        out=psb[bank][:, q * 128:(q + 1) * 128], in_=hb[:, kc, tb * 128:(tb + 1) * 128],
                        identity=ident_f[:]), reads=[(hk, kc), "ident_f"], writes=[pkey(bank)])
                if g2 == 0:
                    S.op("dve", lambda e, os_=os_, bank=bank: e.tensor_copy(out=os_[:, 0:512], in_=psb[bank][:, :]),
                         reads=[pkey(bank)], writes=[ok])
                else:
                    S.op("act", lambda e, os_=os_, bank=bank: e.activation(out=os_[:, 512:1024], in_=psb[bank][:, :],
                                                                           func=AF.Copy),
                         reads=[pkey(bank)], writes=[ok])
            r0 = t * T + tb * 128
            S.dma("pool", lambda e, os_=os_, r0=r0: e.dma_start(out=y[r0:r0 + 128, :], in_=os_[:, :]),
                  "yout%d" % (tb % 2), reads=[ok], out=True)

    mod_big(1, 7)
    mod_finish(1, 7)
    prologue_A()
    if stage < 6:
        return _finish()
    for j in range(n_main):
        if j == 0:
            for k_ in ("w1q", "wout1", "wgu1", "wdn1"):
                cast_weight(k_)
        hb = hbuf[j % 2]
        hk = "hbuf%d" % (j % 2)
        NJ = T if j < 8 else 128
        load_T(xl[j * T:j * T + NJ, :], NJ // 128, hb, hk)
        norm_mod(hb, hk, NJ, 0, 0, 1, 0, abuf[0], "abuf0")
        l0_q_side(abuf[0], "abuf0", NJ, j * T, True)
        if stage == 61:
            return _finish()
        chunks = [(c * 1024, 8) for c in range(8)] + [(S_FULL, 2)]
        mla_heads_tile(NJ, chunks, abuf[1], "abuf1")
        if stage == 62:
            if j == 0:
                dump("h1t0", lambda kc: (abuf[1][:, kc, :], [("abuf1", kc)]), 4, T)
            return _finish()
        na_heads_tile(j, NJ, abuf[1], "abuf1", latent=True)
        if stage in (63, 631):
            if j == 0:
                dump("h1t0", lambda kc: (abuf[1][:, kc, :], [("abuf1", kc)]), 8, T)
            return _finish()
        l0_finish(hb, hk, NJ, 0, abuf[1], "abuf1", abuf[0], "abuf0")
        if j == 0:
            dump("h1t0", lambda kc: (hb[:, kc, :], [(hk, kc)]), 8, T)
        if stage == 64:
            return _finish()
        l1_kv(j, hb, hk, NJ)
        if j >= 1:
            l1_main(j - 1, hbuf[(j - 1) % 2], "hbuf%d" % ((j - 1) % 2))

    S.emit(st)
    st.close()
    ninst = {e: len(S.streams[e]) for e in ENGS}
    print("instructions per engine:", ninst)
    return nc


_CACHE = {}


def kernel(**inputs):
    inp = {k: np.asarray(v) for k, v in inputs.items()}
    w, sh = _prep_shared(inp)
    in_maps = []
    for cid in range(8):
        m = _prep_core(inp, cid, sh)
        m.update(w)
        m.update(sh)
        in_maps.append(m)
    if "nc" not in _CACHE:
        _CACHE["nc"] = build_program()
    res = run_bass_kernel_spmd(_CACHE["nc"], in_maps, core_ids=list(range(8)))
    out = np.zeros((4, S_FULL, D), np.float32)
    for cid in range(8):
        b, half = cid // 2, cid % 2
        yl = np.asarray(res.results[cid]["y"], dtype=np.float32)
        if half == 0:
            out[b, :OWN] = yl
        else:
            out[b, OWN:] = yl[::-1]
    return out
```

```python
from contextlib import ExitStack
import numpy as np
import concourse.bass as bass
import concourse.mybir as mybir
from concourse.bass_utils import run_bass_kernel_spmd

F32 = mybir.dt.float32
BF16 = mybir.dt.bfloat16
AF = mybir.ActivationFunctionType
ALU = mybir.AluOpType

D = 1024
S_FULL = 8192
CTX = 256
T = 512
NT_MAIN = 9
NT_NAKV = 9
OWN = 4096
EPS = 1e-6
DFF = 2816
NEG = -1e30

ENGS = ("pe", "act", "dve", "pool", "sp")


class Sched:
    def __init__(self, nc):
        self.nc = nc
        self.streams = {e: [] for e in ENGS}
        self.writer = {}
        self.readers = {}
        self.known = {e: {} for e in ENGS}
        self.dma_count = {}
        self.sig = {e: set() for e in ENGS}
        self.snap = {}
        self.out_slots = set()

    def _knows(self, eng, ev):
        k = self.known[eng]
        if ev[0] == 'c':
            return k.get(ev[1], -1) >= ev[2]
        return k.get(('d', ev[1]), 0) >= ev[2]

    def _learn(self, eng, ev):
        k = self.known[eng]
        key = ev[1] if ev[0] == 'c' else ('d', ev[1])
        if k.get(key, -1) < ev[2]:
            k[key] = ev[2]
        s = self.snap.get(ev)
        if s:
            for kk, vv in s.items():
                if k.get(kk, -1) < vv:
                    k[kk] = vv

    def _deps(self, eng, reads, writes):
        evs = []
        for key in reads:
            w = self.writer.get(key)
            if w is not None:
                evs.append(w)
        for key in writes:
            w = self.writer.get(key)
            if w is not None:
                evs.append(w)
            evs.extend(self.readers.get(key, ()))
        waits = []
        for ev in evs:
            if ev[0] == 'c' and ev[1] == eng and eng == 'pe':
                continue
            if self._knows(eng, ev):
                continue
            waits.append(ev)
            if ev[0] == 'c':
                self.sig[ev[1]].add(ev[2])
            self._learn(eng, ev)
        return waits

    def _record(self, ev, reads, writes):
        for key in reads:
            self.readers.setdefault(key, []).append(ev)
        for key in writes:
            self.writer[key] = ev
            self.readers[key] = []

    def op(self, eng, fn, reads=(), writes=()):
        waits = self._deps(eng, reads, writes)
        idx = len(self.streams[eng])
        ev = ('c', eng, idx)
        self.snap[ev] = {k: v for k, v in self.known[eng].items() if not isinstance(k, tuple)}
        self.streams[eng].append(('op', fn, waits, idx))
        self._record(ev, reads, writes)
        return ev

    def dma(self, eng, fn, slot, reads=(), writes=(), out=False):
        waits = self._deps(eng, reads, writes)
        n = self.dma_count.get(slot, 0) + 1
        self.dma_count[slot] = n
        ev = ('d', slot, n)
        self.snap[ev] = {k: v for k, v in self.known[eng].items() if not isinstance(k, tuple)}
        self.streams[eng].append(('dma', fn, waits, slot))
        self._record(ev, reads, writes)
        if out:
            self.out_slots.add(slot)
        return ev

    def emit(self, stack):
        nc = self.nc
        csem = {e: stack.enter_context(nc.semaphore("c_" + e)) for e in ENGS if e != "sp"}
        dsem = {s: stack.enter_context(nc.semaphore("d_%s" % (s,))) for s in self.dma_count}
        rank = {}
        for e in ENGS:
            srt = sorted(self.sig[e])
            rank[e] = {i: r + 1 for r, i in enumerate(srt)}
        block = stack.enter_context(nc.Block())

        def run(ename, engobj):
            for kind, fn, waits, x in self.streams[ename]:
                for ev in waits:
                    if ev[0] == 'c':
                        engobj.wait_ge(csem[ev[1]], rank[ev[1]][ev[2]])
                    else:
                        engobj.wait_ge(dsem[ev[1]], 16 * ev[2])
                ins = fn(engobj)
                if kind == 'dma':
                    ins.then_inc(dsem[x], 16)
                elif x in rank[ename]:
                    ins.then_inc(csem[ename], 1)
            if ename == "sp":
                for s in sorted(self.out_slots, key=str):
                    engobj.wait_ge(dsem[s], 16 * self.dma_count[s])

        @block.tensor
        def _(e):
            run("pe", e)

        @block.scalar
        def _(e):
            run("act", e)

        @block.vector
        def _(e):
            run("dve", e)

        @block.gpsimd
        def _(e):
            run("pool", e)

        @block.sync
        def _(e):
            run("sp", e)


def _rope_tables(pos_r, pos_c, dim):
    half = dim // 2
    inv = (10000.0 ** (-np.arange(0, half, 2, dtype=np.float32) / half)).astype(np.float32)
    nf = half // 2
    C = np.zeros((dim, pos_r.shape[0]), np.float32)
    Sg = np.zeros((dim, pos_r.shape[0]), np.float32)
    for f in range(dim):
        pos = pos_r if f < half else pos_c
        i = f % half
        ang = pos.astype(np.float32) * inv[i % nf]
        C[f] = np.cos(ang)
        Sg[f] = -np.sin(ang) if i < nf else np.sin(ang)
    return C, Sg


def _swap_idx(dim):
    half = dim // 2
    nf = half // 2
    idx = np.zeros(dim, np.int64)
    for f in range(dim):
        base = 0 if f < half else half
        i = f % half
        idx[f] = base + (i + nf if i < nf else i - nf)
    return idx


def _na_tables(rel_bias, rev):
    kc = np.arange(64)[:, None]
    qc = np.arange(64)[None, :]
    if rev:
        kt, qt = 63 - kc, 63 - qc
    else:
        kt, qt = kc, qc
    q_start = np.clip(qt - 8, 0, 48)
    col_ok = (kt >= q_start) & (kt < q_start + 16)
    dc_idx = np.clip(kt - qt + 15, 0, 30)
    bias = np.zeros((8, 2, 64, 15, 64), np.float32)
    mask = np.zeros((2, 64, 15, 64), np.float32)
    for e in range(15):
        dr_l = 7 - e
        dr_t = -dr_l if rev else dr_l
        g = rel_bias[:, dr_t + 7][:, dc_idx]
        bias[:, 0, :, e, :] = g
        bias[:, 1, :, e, :] = g
        row_ok = (-4 <= dr_t <= 3)
        mask[0, :, e, :] = np.where(col_ok & row_ok, 0.0, NEG)
        mask[1, :, e, :] = np.where(col_ok, 0.0, NEG)
    return bias.reshape(8, 2, 64, 960), mask.reshape(2, 64, 960)


def _prep_shared(inp):
    f = np.float32
    w = {}
    wi = inp["even_w_in"][0]
    kr = wi[:, 384:416]
    sw32 = _swap_idx(32)
    w["w0kv"] = np.concatenate([wi[:, 256:384], kr, kr[:, sw32]], 1)
    w["w0nak"] = wi[:, 928:1440]
    w["w0nav"] = wi[:, 1440:1952]
    w["w0q"] = np.concatenate([wi[:, 0:256], wi[:, 416:928]], 1)
    qu = inp["mla_w_q_up"][0].reshape(256, 8, 96)
    nope = qu[:, :, :64].reshape(256, 512)
    rope = qu[:, :, 64:]
    rsw = rope[:, :, sw32]
    nope3 = qu[:, :, :64]
    parts = []
    for h in range(8):
        parts += [nope3[:, h], rope[:, h], nope3[:, h], rsw[:, h]]
    w["wqup"] = np.concatenate(parts, 1)
    w["wukP"] = inp["mla_w_uk"][0].transpose(1, 0, 2).reshape(128, 512)
    w["wuv"] = inp["mla_w_uv"][0].transpose(1, 0, 2).reshape(128, 512)
    w["wout0"] = inp["even_w_out"][0]
    for l in range(2):
        w["wgu%d" % l] = inp["ffn_w_gate_up"][l]
        w["wdn%d" % l] = inp["ffn_w_down"][l]
    wo = inp["odd_w_in"][0]
    sw64 = _swap_idx(64)
    q = wo[:, :1024].reshape(1024, 16, 64)
    qs = q[:, :, sw64]
    w["w1q"] = np.concatenate([np.concatenate([q[:, 2 * c:2 * c + 2].reshape(1024, 128),
                                               qs[:, 2 * c:2 * c + 2].reshape(1024, 128)], 1) for c in range(8)], 1)
    k = wo[:, 1024:1152].reshape(1024, 2, 64)
    ksw = k[:, :, sw64]
    w["w1k"] = np.concatenate([k[:, 0], k[:, 0], ksw[:, 0], ksw[:, 0], k[:, 1], k[:, 1], ksw[:, 1], ksw[:, 1]], 1)
    v = wo[:, 1152:1280].reshape(1024, 2, 64)
    w["w1v"] = np.concatenate([v[:, 0], v[:, 1]], 1)
    w["wout1"] = inp["odd_w_out"][0]
    w = {k_: np.ascontiguousarray(v_, dtype=f) for k_, v_ in w.items()}
    sh = {}
    sh["modw"] = np.ascontiguousarray(inp["mod_w"], dtype=f)
    mb = inp["mod_b"].reshape(2, 48, 128).transpose(2, 0, 1)
    sh["modb"] = np.ascontiguousarray(np.repeat(mb[:, :, :, None], 2, axis=3), dtype=f)
    sh["qnorm"] = np.ascontiguousarray(inp["mla_q_norm"][0].reshape(2, 128).T, dtype=f)
    sh["kvnorm"] = np.ascontiguousarray(inp["mla_kv_norm"][0].reshape(1, 128).T, dtype=f)
    sh["fnorm"] = np.ascontiguousarray(inp["final_norm"].reshape(8, 128).T, dtype=f)
    sh["sinks"] = np.ascontiguousarray(np.repeat(inp["swa_sinks"][0][None, :], 128, axis=0), dtype=f)
    kk = np.arange(128)[:, None]
    qq = np.arange(128)[None, :]
    mlo = np.where(qq <= kk, 0.0, NEG).astype(f)
    mhi = np.where(kk <= qq, 0.0, NEG).astype(f)
    sh["swamask"] = np.ascontiguousarray(np.stack([np.tile(mlo, (1, 4)), np.tile(mhi, (1, 4))]), dtype=f)
    return w, sh


WSHAPES = {"w0kv": (1024, 192), "w0nak": (1024, 512), "w0nav": (1024, 512), "w0q": (1024, 768),
           "wqup": (256, 1536), "wukP": (128, 512), "wuv": (128, 512), "wout0": (1024, 1024),
           "wgu0": (1024, 5632), "wdn0": (2816, 1024), "w1q": (1024, 2048), "w1k": (1024, 512),
           "w1v": (1024, 128), "wout1": (1024, 1024), "wgu1": (1024, 5632), "wdn1": (2816, 1024)}
WORDER = ["w0kv", "w0nak", "w0nav", "w0q", "wqup", "wukP", "wuv", "wout0", "wgu0", "wdn0",
          "w1k", "w1v", "w1q", "wout1", "wgu1", "wdn1"]


def _prep_core(inp, cid, sh):
    f = np.float32
    b, half = cid // 2, cid % 2
    perm = np.arange(S_FULL) if half == 0 else np.arange(S_FULL)[::-1]
    m = {}
    m["xl"] = np.ascontiguousarray(inp["x"][b][perm], dtype=f)
    m["ctxl"] = np.ascontiguousarray(inp["ctx"][b], dtype=f)
    cv = np.stack([inp["c"][b].reshape(8, 128).T, inp["c_ctx"].reshape(8, 128).T], axis=2)
    m["cvec"] = np.ascontiguousarray(cv, dtype=f)
    rows, cols = perm // 64, perm % 64
    C, Sg = _rope_tables(rows, cols, 32)
    m["ropeMc"] = np.ascontiguousarray(np.tile(C, (3, 1)), dtype=f)
    m["ropeMs"] = np.ascontiguousarray(np.tile(Sg, (3, 1)), dtype=f)
    C, Sg = _rope_tables(rows[:5120], cols[:5120], 64)
    m["ropeSc"] = np.ascontiguousarray(np.tile(C, (2, 1)), dtype=f)
    m["ropeSs"] = np.ascontiguousarray(np.tile(Sg, (2, 1)), dtype=f)
    nb, nm = _na_tables(inp["na_rel_bias"][0], half == 1)
    m["nabias"] = np.ascontiguousarray(nb, dtype=f)
    m["namask"] = np.ascontiguousarray(nm, dtype=f)
    return m


def build_program(n_main=NT_MAIN, dbg=None, stage=9):
    nc = bass.Bass("TRN2", target_bir_lowering=False)
    S = Sched(nc)
    st = ExitStack()

    def din(name, shape, dt=F32):
        return nc.dram_tensor(name, list(shape), dt, kind="ExternalInput").ap()

    xl = din("xl", [S_FULL, D])
    ctxl = din("ctxl", [CTX, D])
    cvec = din("cvec", [128, 8, 2])
    modw = din("modw", [2, D, 6 * D])
    modb = din("modb", [128, 2, 48, 2])
    qnorm_d = din("qnorm", [128, 2])
    kvnorm_d = din("kvnorm", [128, 1])
    fnorm_d = din("fnorm", [128, 8])
    sinks_d = din("sinks", [128, 16])
    swamask_d = din("swamask", [2, 128, 512])
    ropeMc = din("ropeMc", [96, S_FULL])
    ropeMs = din("ropeMs", [96, S_FULL])
    ropeSc = din("ropeSc", [128, 5120])
    ropeSs = din("ropeSs", [128, 5120])
    nabias = din("nabias", [8, 2, 64, 960])
    namask = din("namask", [2, 64, 960])
    wf = {k: din(k, WSHAPES[k]) for k in WORDER}
    y = nc.dram_tensor("y", [OWN, D], F32, kind="ExternalOutput").ap()
    dbg_out = {}
    if dbg:
        for k, shp in dbg.items():
            dbg_out[k] = nc.dram_tensor("dbg_" + k, list(shp), F32, kind="ExternalOutput").ap()

    wb = {k: nc.dram_tensor("wb_" + k, list(WSHAPES[k]), BF16).ap() for k in WORDER}
    NK = S_FULL + CTX
    khp_d = nc.dram_tensor("khp_d", [4, 128, NK], BF16).ap()
    kr_d = nc.dram_tensor("kr_d", [32, NK], BF16).ap()
    vh_d = nc.dram_tensor("vh_d", [NK, 768], BF16).ap()
    NAT = NT_NAKV * T
    nakT_d = nc.dram_tensor("nakT_d", [128, 4, NAT], BF16).ap()
    nav_d = nc.dram_tensor("nav_d", [NAT, 768], BF16).ap()
    nastr_d = nc.dram_tensor("nastr_d", [8, 2, 64, 960], BF16).ap()

    def sb(name, shape, dt):
        return st.enter_context(nc.sbuf_tensor(name, list(shape), dt))

    ident_f = sb("ident_f", [128, 128], F32)
    ident_b = sb("ident_b", [128, 128], BF16)
    ones_b = sb("ones_b", [128, 128], BF16)
    modv = sb("modv", [128, 2 * 6 * 8 * 2], F32)
    onep = sb("onep", [128, 2 * 6 * 8 * 2], F32)
    silc = sb("silc", [128, 16], F32)
    qnorm = sb("qnorm_s", [128, 2], F32)
    kvnorm = sb("kvnorm_s", [128, 1], F32)
    fnorm = sb("fnorm_s", [128, 8], F32)
    esink = sb("esink", [128, 16], F32)
    swam = sb("swam", [128, 2, 512], BF16)
    hbuf = [sb("hbuf%d" % i, [128, 8, T], F32) for i in range(2)]
    hctx = hbuf[1]
    xstage = [sb("xstage%d" % i, [128, D], F32) for i in range(2)]
    ostage = xstage
    abuf = [sb("abuf%d" % i, [128, 8, T], BF16) for i in range(2)]
    NWS = 3
    wslot = [sb("wslot%d" % i, [128, 4096], BF16) for i in range(NWS)]
    sqb = sb("sqb", [128, 2, T], BF16)
    tmpf = sb("tmpf", [128, 2, T], F32)
    rstd = sb("rstd", [128, T], F32)
    cq = tmpf
    cqn = sb("cqn", [128, 2, T], BF16)
    uni = sb("uni", [128, 8 * T], BF16)
    qnope = uni[:, 0:4 * T].rearrange("p (k t) -> p k t", k=4)
    Qh = sb("Qh", [96, 8, T], BF16)
    pswap = sb("pswap", [128, 128], F32)
    vstage = sb("vstage", [128, 768], BF16)
    recb = sb("recb", [128, 2, T], F32)
    ckvf = recb[:, 0, :]
    naq = uni[:, 4 * T:8 * T].rearrange("p (k t) -> p k t", k=4)
    nakw = sb("nakw", [128, 2, 1024], BF16)
    navw = sb("navw", [64, 2, 16 * 192], BF16)
    nastrip = sb("nastrip", [128, 2, 960], BF16)
    naedge = sb("naedge", [128, 1, 960], BF16)
    nakc = sb("nakc", [128, 4, CTX], BF16)
    navc = sb("navc", [128, 2, 768], BF16)
    actT = sb("actT", [128, 22, T], BF16)
    silt = sb("silt", [128, 2, T], F32)
    q1 = uni[:, :].rearrange("p (k t) -> p k t", k=8)
    k1 = sb("k1", [128, 2, 3 * T], BF16)
    v1 = sb("v1", [128, 12, 384], BF16)
    k1c = sb("k1c", [128, 2, CTX], BF16)
    v1c = sb("v1c", [128, 2, 384], BF16)
    pT = [sb("pT%d" % i, [128, T], BF16) for i in range(3)]
    ropeC = sb("ropeC", [128, T], F32)
    ropeS = sb("ropeS", [128, 2, T], F32)
    mck = [sb("mck%d" % i, [96, 1024], BF16) for i in range(2)]
    mv = [sb("mv%d" % i, [128, 8, 192], BF16) for i in range(2)]

    psb = [st.enter_context(nc.psum_tensor("psb%d" % i, [128, 512], F32)) for i in range(8)]
    print("sbuf bytes remaining/partition:", nc.sbuf_bytes_remaining)

    cnt = {"mm": 0, "ws": 0, "pt": 0, "uid": 0}

    def mm_bank():
        b = cnt["mm"] % 4
        cnt["mm"] += 1
        return b

    def pkey(b):
        return "ps%d" % b

    def next_pt():
        i = cnt["pt"] % 3
        cnt["pt"] += 1
        return i

    def mvi(l, w, kc, col):
        i = ((l * 6 + w) * 8 + kc) * 2 + col
        return i

    S.op("pool", lambda e: e.memset(ident_f[:], 1.0), writes=["ident_f"])
    S.op("pool", lambda e: e.affine_select(out=ident_f[:], in_=ident_f[:], pattern=[[-1, 128]],
                                           compare_op=ALU.is_equal, fill=0.0, base=0, channel_multiplier=1),
         reads=["ident_f"], writes=["ident_f"])
    S.op("dve", lambda e: e.tensor_copy(out=ident_b[:], in_=ident_f[:]), reads=["ident_f"], writes=["ident_b"])
    S.op("pool", lambda e: e.memset(ones_b[:], 1.0), writes=["ones_b"])
    S.op("pool", lambda e: e.memset(pswap[:], 1.0), writes=["pswap"])
    S.op("pool", lambda e: e.affine_select(out=pswap[:], in_=pswap[:], pattern=[[-1, 128]],
                                           compare_op=ALU.is_equal, fill=0.0, base=64, channel_multiplier=1),
         reads=["pswap"], writes=["pswap"])
    S.op("pool", lambda e: e.memset(xstage[0][:, 0:128], 1.0), writes=["xstage0"])
    S.op("pool", lambda e: e.affine_select(out=xstage[0][:, 0:128], in_=xstage[0][:, 0:128], pattern=[[-1, 128]],
                                           compare_op=ALU.is_equal, fill=0.0, base=-64, channel_multiplier=1),
         reads=["xstage0"], writes=["xstage0"])
    S.op("pool", lambda e: e.tensor_tensor(out=pswap[:], in0=pswap[:], in1=xstage[0][:, 0:128], op=ALU.add),
         reads=["pswap", "xstage0"], writes=["pswap"])
    S.op("pool", lambda e: e.memset(vstage[:, :].rearrange("p (i u) -> p i u", u=192)[:, :, 64:128], 1.0),
         writes=["vstage"])
    S.op("pool", lambda e: e.memset(navc[:, :, :].rearrange("p b (i u) -> p b i u", u=192)[:, :, :, 64:128], 1.0),
         writes=["navc"])
    S.op("pool", lambda e: e.memset(v1[:, :, :].rearrange("p b (i u) -> p b i u", u=192)[:, :, :, 64:128], 1.0),
         writes=[("v1", 0), ("v1", 1), ("v1", 2)])
    S.op("pool", lambda e: e.memset(v1c[:, :, :].rearrange("p b (i u) -> p b i u", u=192)[:, :, :, 64:128], 1.0),
         writes=["v1c"])

    def cast_weight(k):
        r, c = WSHAPES[k]
        per = r * c // 128
        src = wf[k].rearrange("r c -> (r c)").rearrange("(p n) -> p n", p=128)
        dst = wb[k].rearrange("r c -> (r c)").rearrange("(p n) -> p n", p=128)
        for c0 in range(0, per, 8192):
            c1 = min(per, c0 + 8192)
            S.dma("pool", lambda e, s_=src[:, c0:c1], d_=dst[:, c0:c1]: e.dma_start(out=d_, in_=s_),
                  "wc_" + k, writes=[("wb", k, c0)])

    for k in ("w0kv", "wukP", "wuv", "w0nak", "w0nav"):
        cast_weight(k)
    CAST_AT = {1: ["w0q", "wqup"], 2: ["wout0"], 3: ["wgu0"], 7: ["wdn0"], 10: ["w1k", "w1v"]}

    def ld(dst_ap, src_ap, key, slot):
        S.dma("sp", lambda e: e.dma_start(out=dst_ap, in_=src_ap), slot, writes=[key])

    ld(qnorm[:], qnorm_d, "qnorm", "c0_qnorm")
    ld(kvnorm[:], kvnorm_d, "kvnorm", "c0_kvnorm")
    ld(fnorm[:], fnorm_d, "fnorm", "c0_fnorm")
    ld(esink[:], sinks_d, "esink", "c0_esink")
    ld(silc[:], cvec.rearrange("p k c -> p (k c)"), "silc", "c0_silc")
    S.dma("sp", lambda e: e.dma_start(out=modv[:, 0:192], in_=modb.rearrange("p l m c -> p (l m c)")), "c0_modv",
          writes=[("modv", 0), ("modv", 1)])
    S.op("act", lambda e: e.activation(out=esink[:], in_=esink[:], func=AF.Exp), reads=["esink"], writes=["esink"])
    S.op("act", lambda e: e.activation(out=silc[:], in_=silc[:], func=AF.Silu), reads=["silc"], writes=["silc"])
    for i in range(2):
        S.dma("sp", lambda e, i=i: e.dma_start(out=tmpf[:, i, :], in_=swamask_d[i]), "c1", writes=["tmpf"])
    S.op("act", lambda e: e.activation(out=swam[:, :, :], in_=tmpf[:, :, :], func=AF.Copy),
         reads=["tmpf"], writes=["swam"])


    def _finish():
        S.emit(st)
        st.close()
        print("instructions per engine:", {e: len(S.streams[e]) for e in ENGS})
        return nc
    if stage < 2:
        return _finish()
    def mod_blocks(l, ms, bank):
        for m in ms:
            xs = xstage[m % 2]
            xk = "xstage%d" % (m % 2)
            S.dma("sp", lambda e, xs=xs, m=m: e.dma_start(
                out=xs[:, :].rearrange("p (k c) -> p k c", k=8),
                in_=modw[l].rearrange("(k p) c -> p k c", p=128)[:, :, m * 128:(m + 1) * 128]),
                "xst%d" % (m % 2), writes=[xk])
            for kc in range(8):
                S.op("pe", lambda e, xs=xs, kc=kc, m=m: e.matmul(
                    psb[bank][:, 2 * m:2 * m + 2], lhsT=xs[:, kc * 128:(kc + 1) * 128],
                    rhs=silc[:, 2 * kc:2 * kc + 2], start=(kc == 0), stop=(kc == 7)),
                    reads=[xk, "silc"], writes=[pkey(bank)])

    def mod_finish(l, bank):
        S.op("dve", lambda e: e.tensor_tensor(
            out=modv[:, 96 * l:96 * l + 96], in0=psb[bank][:, 0:96], in1=modv[:, 96 * l:96 * l + 96], op=ALU.add),
            reads=[pkey(bank), ("modv", l)], writes=[("modv", l)])
        S.op("dve", lambda e: e.tensor_scalar_add(out=onep[:, 96 * l:96 * l + 96], in0=modv[:, 96 * l:96 * l + 96],
                                                  scalar1=1.0), reads=[("modv", l)], writes=[("onep", l)])

    def mod_big(l, bank):
        for blk in range(12):
            hbk = hbuf[blk % 2]
            hkk = "hbuf%d" % (blk % 2)
            S.dma("sp", lambda e, hbk=hbk, blk=blk: e.dma_start(
                out=hbk[:, :, :], in_=modw[l].rearrange("(k p) c -> p k c", p=128)[:, :, blk * 512:(blk + 1) * 512]),
                "modld%d" % (blk % 2), writes=[(hkk, kc) for kc in range(8)])
            for mm_ in range(4):
                m = blk * 4 + mm_
                for kc in range(8):
                    S.op("pe", lambda e, hbk=hbk, kc=kc, m=m, mm_=mm_: e.matmul(
                        psb[bank][:, 2 * m:2 * m + 2], lhsT=hbk[:, kc, mm_ * 128:(mm_ + 1) * 128],
                        rhs=silc[:, 2 * kc:2 * kc + 2], start=(kc == 0), stop=(kc == 7)),
                        reads=[(hkk, kc), "silc"], writes=[pkey(bank)])

    bank0 = mm_bank()
    mod_big(0, bank0)
    mod_finish(0, bank0)

    if stage < 3:
        return _finish()
    def build_strip(h, v):
        S.dma("sp", lambda e: e.dma_start(out=xstage[0][0:64, 0:960], in_=nabias[h, v]),
              "xst0", writes=["xstage0"])
        S.dma("sp", lambda e: e.dma_start(out=xstage[1][0:64, 0:960], in_=namask[v]),
              "xst1", writes=["xstage1"])
        S.op("pool", lambda e: e.tensor_scalar_mul(out=xstage[0][0:64, 0:960], in0=xstage[0][0:64, 0:960], scalar1=8.0),
             reads=["xstage0"], writes=["xstage0"])
        S.op("pool", lambda e: e.tensor_tensor(out=naedge[0:64, 0, :], in0=xstage[0][0:64, 0:960],
                                               in1=xstage[1][0:64, 0:960], op=ALU.add),
             reads=["xstage0", "xstage1"], writes=["naedge0"])
        S.dma("pool", lambda e: e.dma_start(out=nastr_d[h, v], in_=naedge[0:64, 0, :]), "nb3",
              reads=["naedge0"], writes=["nastr_d"])

    def load_T(src_rows_ap, nblk, dst, dkey):
        for tb in range(nblk):
            xs = xstage[tb % 2]
            xk = "xstage%d" % (tb % 2)
            S.dma("sp", lambda e, xs=xs, tb=tb: e.dma_start(out=xs[:, :], in_=src_rows_ap[tb * 128:(tb + 1) * 128, :]),
                  "xst%d" % (tb % 2), writes=[xk])
            for g in range(2):
                bank = mm_bank()
                for q in range(4):
                    kc = 4 * g + q
                    S.op("pe", lambda e, xs=xs, kc=kc, q=q, bank=bank: e.transpose(
                        out=psb[bank][:, q * 128:(q + 1) * 128], in_=xs[:, kc * 128:(kc + 1) * 128],
                        identity=ident_f[:]), reads=[xk, "ident_f"], writes=[pkey(bank)])
                eng = "dve" if g == 0 else "act"
                if eng == "dve":
                    S.op("dve", lambda e, g=g, tb=tb, bank=bank: e.tensor_copy(
                        out=dst[:, 4 * g:4 * g + 4, tb * 128:(tb + 1) * 128],
                        in_=psb[bank][:, :].rearrange("p (q t) -> p q t", q=4)),
                        reads=[pkey(bank)], writes=[(dkey, 4 * g + q) for q in range(4)])
                else:
                    S.op("act", lambda e, g=g, tb=tb, bank=bank: e.activation(
                        out=dst[:, 4 * g:4 * g + 4, tb * 128:(tb + 1) * 128],
                        in_=psb[bank][:, :].rearrange("p (q t) -> p q t", q=4), func=AF.Copy),
                        reads=[pkey(bank)], writes=[(dkey, 4 * g + q) for q in range(4)])

    def load_T_big(src_rows_ap, dst, dkey, stg, stgkey):
        stgv = stg[:, :, :].rearrange("p k t -> p (k t)").rearrange("p (b d) -> p b d", b=4)
        skeys = [(stgkey, kc) for kc in range(8)]
        S.dma("sp", lambda e: e.dma_start(out=stgv, in_=src_rows_ap.rearrange("(b p) d -> p b d", p=128)),
              "xbig", writes=skeys)
        for tb in range(4):
            for g in range(2):
                bank = mm_bank()
                for q in range(4):
                    kc = 4 * g + q
                    S.op("pe", lambda e, kc=kc, q=q, bank=bank, tb=tb: e.transpose(
                        out=psb[bank][:, q * 128:(q + 1) * 128], in_=stgv[:, tb, kc * 128:(kc + 1) * 128],
                        identity=ident_f[:]), reads=skeys + ["ident_f"], writes=[pkey(bank)])
                if g == 0:
                    S.op("dve", lambda e, g=g, tb=tb, bank=bank: e.tensor_copy(
                        out=dst[:, 4 * g:4 * g + 4, tb * 128:(tb + 1) * 128],
                        in_=psb[bank][:, :].rearrange("p (q t) -> p q t", q=4)),
                        reads=[pkey(bank)], writes=[(dkey, 4 * g + q) for q in range(4)])
                else:
                    S.op("act", lambda e, g=g, tb=tb, bank=bank: e.activation(
                        out=dst[:, 4 * g:4 * g + 4, tb * 128:(tb + 1) * 128],
                        in_=psb[bank][:, :].rearrange("p (q t) -> p q t", q=4), func=AF.Copy),
                        reads=[pkey(bank)], writes=[(dkey, 4 * g + q) for q in range(4)])

    def rms_stats(src_fn, skeys, KC, N, nfeat):
        bank = mm_bank()
        for kc in range(KC):
            S.op("act", lambda e, kc=kc: e.activation(out=sqb[:, kc % 2, :N], in_=src_fn(kc), func=AF.Square),
                 reads=[skeys[kc]], writes=[("sqb", kc % 2)])
            S.op("pe", lambda e, kc=kc, bank=bank: e.matmul(
                psb[bank][:, :N], lhsT=ones_b[:, :], rhs=sqb[:, kc % 2, :N], start=(kc == 0), stop=(kc == KC - 1)),
                reads=[("sqb", kc % 2), "ones_b"], writes=[pkey(bank)])
        S.op("dve", lambda e, bank=bank: e.tensor_scalar(
            out=rstd[:, :N], in0=psb[bank][:, :N], scalar1=1.0 / nfeat, scalar2=EPS, op0=ALU.mult, op1=ALU.add),
            reads=[pkey(bank)], writes=["rstd"])
        S.op("act", lambda e: e.activation(out=rstd[:, :N], in_=rstd[:, :N], func=AF.Sqrt),
             reads=["rstd"], writes=["rstd"])
        S.op("dve", lambda e: e.reciprocal(out=rstd[:, :N], in_=rstd[:, :N]), reads=["rstd"], writes=["rstd"])

    def norm_mod(hsrc, hkey, N, l, wsh, wsc, col, dst, dkey, eng="dve"):
        rms_stats(lambda kc: hsrc[:, kc, :N], [(hkey, kc) for kc in range(8)], 8, N, D)
        for kc in range(8):
            i_sc = mvi(l, wsc, kc, col)
            i_sh = mvi(l, wsh, kc, col)
            if eng == "pool" and kc % 2 == 1:
                S.op("pool", lambda e, kc=kc, i_sc=i_sc: e.tensor_scalar_mul(
                    out=tmpf[:, kc % 2, :N], in0=hsrc[:, kc, :N], scalar1=onep[:, i_sc:i_sc + 1]),
                    reads=[(hkey, kc), ("onep", l)], writes=[("tmpf", kc % 2)])
                S.op("pool", lambda e, kc=kc: e.tensor_tensor(
                    out=tmpf[:, kc % 2, :N], in0=tmpf[:, kc % 2, :N], in1=rstd[:, :N], op=ALU.mult),
                    reads=[("tmpf", kc % 2), "rstd"], writes=[("tmpf", kc % 2)])
            else:
                S.op("dve", lambda e, kc=kc, i_sc=i_sc: e.scalar_tensor_tensor(
                    out=tmpf[:, kc % 2, :N], in0=hsrc[:, kc, :N], scalar=onep[:, i_sc:i_sc + 1], in1=rstd[:, :N],
                    op0=ALU.mult, op1=ALU.mult),
                    reads=[(hkey, kc), "rstd", ("onep", l)], writes=[("tmpf", kc % 2)])
            S.op("act", lambda e, kc=kc, i_sh=i_sh: e.activation(
                out=dst[:, kc, :N], in_=tmpf[:, kc % 2, :N], func=AF.Identity, bias=modv[:, i_sh:i_sh + 1], scale=1.0),
                reads=[("tmpf", kc % 2), ("modv", l)], writes=[(dkey, kc)])

    def wload(wname, KC, c0, ncols, k0=0):
        s = cnt["ws"] % NWS
        cnt["ws"] += 1
        view = wslot[s][:, 0:KC * ncols].rearrange("p (k c) -> p k c", k=KC)
        P = min(128, WSHAPES[wname][0])
        src = wb[wname].rearrange("(k p) c -> p k c", p=P)[:, k0:k0 + KC, c0:c0 + ncols]
        S.dma("sp", lambda e: e.dma_start(out=view[:P], in_=src), "ws%d" % s,
              reads=[("wb", wname, q0) for q0 in range(0, WSHAPES[wname][0] * WSHAPES[wname][1] // 128, 8192)],
              writes=["wslot%d" % s])
        return view, "wslot%d" % s

    def linear(wname, KC, rhs_fn, rkeys, N, mchunks, evac, blockcols=None, kparts=None):
        if blockcols is None:
            blockcols = max(128, (4096 // KC) // 128 * 128)
        i = 0
        while i < len(mchunks):
            c0 = mchunks[i][0]
            j = i
            while j < len(mchunks) and mchunks[j][0] + mchunks[j][1] - c0 <= blockcols:
                j += 1
            ncols = mchunks[j - 1][0] + mchunks[j - 1][1] - c0
            view, wkey = wload(wname, KC, c0, ncols)
            for ii in range(i, j):
                mc0, mn = mchunks[ii]
                bank = mm_bank()
                for kc in range(KC):
                    kp = 128 if kparts is None else kparts
                    S.op("pe", lambda e, view=view, kc=kc, mc0=mc0, mn=mn, c0=c0, bank=bank, kp=kp: e.matmul(
                        psb[bank][:mn, :N], lhsT=view[:kp, kc, mc0 - c0:mc0 - c0 + mn], rhs=rhs_fn(kc),
                        start=(kc == 0), stop=(kc == KC - 1)),
                        reads=[wkey, rkeys[kc]], writes=[pkey(bank)])
                evac(ii, psb[bank][:mn, :N], pkey(bank))
            i = j

    def linear_tm(wname, KC, lhs_fn, lkeys, ntb, tbsz, ncols, evac):
        view, wkey = wload(wname, KC, 0, ncols)
        for tb in range(ntb):
            bank = mm_bank()
            for kc in range(KC):
                S.op("pe", lambda e, kc=kc, tb=tb, bank=bank: e.matmul(
                    psb[bank][:tbsz, :ncols], lhsT=lhs_fn(kc, tb), rhs=view[:, kc, :ncols],
                    start=(kc == 0), stop=(kc == KC - 1)),
                    reads=[wkey, lkeys[kc]], writes=[pkey(bank)])
            evac(tb, psb[bank][:tbsz, :ncols], pkey(bank))

    def copy_evac(dst_fn, dkey_fn, alt=True):
        def ev(i, ps, bk):
            if alt and i % 2 == 1:
                S.op("act", lambda e: e.activation(out=dst_fn(i), in_=ps, func=AF.Copy),
                     reads=[bk], writes=[dkey_fn(i)])
            else:
                S.op("dve", lambda e: e.tensor_copy(out=dst_fn(i), in_=ps), reads=[bk], writes=[dkey_fn(i)])
        return ev

    ropeT = [sb("ropeT%d" % i, [128, T], F32) for i in range(1)]

    NPT = len(pT)

    def run_attn(blocks, skew=2, defer=3):
        deferred = []
        nb = len(blocks)
        for k in range(nb + skew):
            if k < nb:
                b = blocks[k]
                if b.get("pre"):
                    b["pre"]()
                bank = mm_bank()
                np_, c0, c1 = b["np_"], b["c0"], b["c1"]
                n = len(b["qk"])
                for i, (lhsT, rhs, rk, oc0, oc1) in enumerate(b["qk"]):
                    o_ap = psb[bank][:np_, oc0:oc1]
                    if len(rhs.shape) == 3:
                        o_ap = o_ap.rearrange("p (h t) -> p h t", h=rhs.shape[1])
                    S.op("pe", lambda e, lhsT=lhsT, rhs=rhs, i=i, o_ap=o_ap, n=n: e.matmul(
                        o_ap, lhsT=lhsT, rhs=rhs, start=(i == 0), stop=(i == n - 1)),
                        reads=rk, writes=[pkey(bank)])
                pi = cnt["pt"] % NPT
                cnt["pt"] += 1
                b["pi"] = pi
                S.op("act", lambda e, bank=bank, pi=pi, np_=np_, c0=c0, c1=c1, sc=b["scale"]: e.activation(
                    out=pT[pi][:np_, c0:c1], in_=psb[bank][:np_, c0:c1], func=AF.Exp, scale=sc),
                    reads=[pkey(bank)], writes=["pT%d" % pi])
            still = []
            for (when, fn) in deferred:
                if when <= k:
                    fn()
                else:
                    still.append((when, fn))
            deferred = still
            kk = k - skew
            if kk >= 0:
                b = blocks[kk]
                pi, np_, c0, c1 = b["pi"], b["np_"], b["c0"], b["c1"]
                acc = b["acc"]
                S.op("pe", lambda e, b=b, pi=pi, np_=np_, c0=c0, c1=c1, acc=acc: e.matmul(
                    psb[acc][:, c0:c1], lhsT=b["v_lhsT"], rhs=pT[pi][:np_, c0:c1], start=b["first"], stop=b["last"]),
                    reads=["pT%d" % pi] + b["vkeys"], writes=[pkey(acc)])
                if b.get("epi"):
                    fn = b["epi"]()
                    if fn is not None:
                        deferred.append((k + defer, fn))
        for (_, fn) in deferred:
            fn()

    acc_rot = {"i": 0}

    def next_acc():
        i = acc_rot["i"] % 4
        acc_rot["i"] += 1
        return 4 + i

    def pair_epilogue(accA, accB, N, final_fn, sink_heads=None):
        for (acc, lo) in ((accA, 64), (accB, 0)):
            if sink_heads is None:
                S.op("dve", lambda e, acc=acc, lo=lo: e.reciprocal(out=recb[lo:lo + 64, 0, :N], in_=psb[acc][lo:lo + 64, :N]),
                     reads=[pkey(acc)], writes=[("recb", 0)])
            else:
                heads = sink_heads[0] if lo == 64 else sink_heads[1]
                for hh, head in enumerate(heads):
                    S.op("dve", lambda e, acc=acc, lo=lo, hh=hh, head=head: e.tensor_scalar_add(
                        out=recb[lo:lo + 64, 0, hh * 128:(hh + 1) * 128], in0=psb[acc][lo:lo + 64, hh * 128:(hh + 1) * 128],
                        scalar1=esink[lo:lo + 64, head:head + 1]),
                        reads=[pkey(acc), "esink"], writes=[("recb", 0)])
                S.op("dve", lambda e, lo=lo: e.reciprocal(out=recb[lo:lo + 64, 0, :N], in_=recb[lo:lo + 64, 0, :N]),
                     reads=[("recb", 0)], writes=[("recb", 0)])

        def pe_part():
            bank = mm_bank()
            S.op("pe", lambda e: e.matmul(psb[bank][:, :N], lhsT=pswap[:, :], rhs=recb[:, 0, :N], start=True, stop=True),
                 reads=[("recb", 0), "pswap"], writes=[pkey(bank)])
            S.op("act", lambda e: e.activation(out=recb[:, 1, :N], in_=psb[bank][:, :N], func=AF.Copy),
                 reads=[pkey(bank)], writes=[("recb", 1)])
            final_fn(recb[:, 1, :N], ("recb", 1))
        return pe_part

    KV_KEYS = []
    for t_ in range(17):
        k0_ = t_ * T
        KV_KEYS.append(("kr_d", k0_))
        KV_KEYS += [("khp_d", k0_, i_) for i_ in range(4)]
        KV_KEYS += [("vh_d", k0_, tb_) for tb_ in range(4)]

    def mla_heads_tile(N, key_chunks, cat, catkey):
        scale = 96.0 ** -0.5
        blocks = []
        nblk_tot = sum(nb for (_, nb) in key_chunks)
        accs = {}
        for h in range(8):
            i, par = h // 2, h % 2
            acc = next_acc()
            accs[h] = acc
            done = 0
            for (k0, nb) in key_chunks:
                s = cnt["uid"] % 2
                cnt["uid"] += 1

                def pre(s=s, k0=k0, nb=nb, h=h, i=i):
                    S.dma("sp", lambda e: e.dma_start(out=mck[s][0:64, :nb * 128],
                                                      in_=khp_d[i, 64 * (h % 2):64 * (h % 2) + 64, k0:k0 + nb * 128]),
                          "mck%d" % s, reads=KV_KEYS, writes=["mck%d" % s])
                    S.dma("sp", lambda e: e.dma_start(out=mck[s][64:96, :nb * 128], in_=kr_d[:, k0:k0 + nb * 128]),
                          "mck%d" % s, reads=KV_KEYS, writes=["mck%d" % s])
                    S.dma("sp", lambda e: e.dma_start(
                        out=mv[s][:, :nb, :],
                        in_=vh_d[k0:k0 + nb * 128, 192 * i:192 * i + 192].rearrange("(b p) c -> p b c", p=128)),
                        "mv%d" % s, reads=KV_KEYS, writes=["mv%d" % s])
                for kb in range(nb):
                    qk = [(mck[s][:, kb * 128:(kb + 1) * 128], Qh[:, h, :N], ["mck%d" % s, ("Qh", h)], 0, N)]
                    blk = dict(np_=128, c0=0, c1=N, qk=qk, scale=scale,
                               v_lhsT=mv[s][:, kb, 64 * par:64 * par + 128], vkeys=["mv%d" % s],
                               acc=acc, first=(done == 0), last=(done == nblk_tot - 1),
                               pre=(pre if kb == 0 else None))
                    done += 1
                    blocks.append(blk)
            if par == 1:
                def epi(i=i, accA=accs[h - 1], accB=acc):
                    def fin(rec, rkey):
                        for (p0, a_) in ((0, accA), (64, accB)):
                            S.op("dve", lambda e, p0=p0, a_=a_: e.tensor_tensor(
                                out=cat[p0:p0 + 64, i, :N], in0=psb[a_][p0:p0 + 64, :N], in1=rec[p0:p0 + 64, :],
                                op=ALU.mult), reads=[pkey(a_), rkey], writes=[(catkey, i)])
                    return pair_epilogue(accA, accB, N, fin)
                blocks[-1]["epi"] = epi
        run_attn(blocks)

    NA_KEYS = [("nakT_d", t_ * T, i_) for t_ in range(NT_NAKV) for i_ in range(4)] + \
              [("nav_d", t_ * T, tb_) for t_ in range(NT_NAKV) for tb_ in range(4)]

    def na_heads_tile(j, N, cat, catkey, latent=True):
        scale = 0.125
        blocks = []
        for i in range(4):
            pre = None
            if latent:
                nqr = N // 64
                w0 = max(0, 8 * j - 4)
                w1 = min(NT_NAKV * 8, 8 * j + nqr + 4)
                nr = w1 - w0
                s = i % 2

                def pre(s=s, i=i, w0=w0, nr=nr):
                    S.dma("sp", lambda e: e.dma_start(out=nakw[:, s, :nr * 64], in_=nakT_d[:, i, w0 * 64:(w0 + nr) * 64]),
                          "nakw%d" % s, reads=NA_KEYS, writes=["nakw%d" % s])
                    S.dma("sp", lambda e: e.dma_start(
                        out=navw[:, s, :nr * 192].rearrange("p (r f) -> p r f", f=192),
                        in_=nav_d[w0 * 64:(w0 + nr) * 64, i * 192:(i + 1) * 192].rearrange("(r t) f -> t r f", t=64)),
                        "navw%d" % s, reads=NA_KEYS, writes=["navw%d" % s])
                    for pp in range(2):
                        S.dma("sp", lambda e, pp=pp: e.dma_start(
                            out=nastrip[64 * pp:64 * pp + 64, s, :], in_=nastr_d[2 * i + pp, 0]),
                            "nastrip%d" % s, reads=["nastr_d"], writes=["nastrip%d" % s])
                        if j == 0:
                            S.dma("sp", lambda e, pp=pp: e.dma_start(
                                out=naedge[64 * pp:64 * pp + 64, 0, :], in_=nastr_d[2 * i + pp, 1]),
                                "naedge0", reads=["nastr_d"], writes=["naedge0"])
            for par in range(2):
                h = 2 * i + par
                p0 = 64 * par
                acc = next_acc()
                if par == 0:
                    accA = acc
                vo = 192 * i + 64 * par
                hb = []
                for cb in range(2):
                    qk = [(nakc[p0:p0 + 64, i, cb * 128:(cb + 1) * 128], naq[p0:p0 + 64, i, :N],
                           [("nakc", i), ("naq", i)], 0, N)]
                    hb.append(dict(np_=128, c0=0, c1=N, qk=qk, scale=scale, v_lhsT=navc[:, cb, vo:vo + 128],
                                   vkeys=["navc"], acc=acc))
                if latent:
                    for kr in range(w0, w1):
                        if j == 0:
                            r0, r1 = (0, 7) if kr <= 7 else (kr - 4, 7)
                        else:
                            r0, r1 = max(8 * j, kr - 4), min(8 * j + nqr - 1, kr + 4)
                        if r0 > r1:
                            continue
                        a, b = r0 - 8 * j, r1 - 8 * j + 1
                        c0, c1 = a * 64, b * 64
                        lk = (kr - w0) * 64
                        segs = []
                        if j == 0 and a < 4:
                            be = min(b, 4)
                            segs.append((a, be, naedge[p0:p0 + 64, 0, :], "naedge0"))
                            if b > 4:
                                segs.append((4, b, nastrip[p0:p0 + 64, s, :], "nastrip%d" % s))
                        else:
                            segs.append((a, b, nastrip[p0:p0 + 64, s, :], "nastrip%d" % s))
                        qk = [(nakw[p0:p0 + 64, s, lk:lk + 64], naq[p0:p0 + 64, i, c0:c1],
                               ["nakw%d" % s, ("naq", i)], c0, c1)]
                        for (sa, sb_, strip, skey) in segs:
                            e0 = (7 - kr + 8 * j + sa) * 64
                            qk.append((ident_b[p0:p0 + 64, p0:p0 + 64], strip[:, e0:e0 + (sb_ - sa) * 64],
                                       [skey, "ident_b"], sa * 64, sb_ * 64))
                        vw = (kr - w0) * 192 + 64 * par
                        hb.append(dict(np_=64, c0=c0, c1=c1, qk=qk, scale=scale,
                                       v_lhsT=navw[:, s, vw:vw + 128], vkeys=["navw%d" % s], acc=acc))
                for bi, blk in enumerate(hb):
                    blk["first"] = (bi == 0)
                    blk["last"] = (bi == len(hb) - 1)
                if par == 0 and pre is not None:
                    hb[0]["pre"] = pre

                if par == 1:
                    def epi(i=i, accA=accA, accB=acc):
                        def fin(rec, rkey):
                            for (q0, a_) in ((0, accA), (64, accB)):
                                S.op("dve", lambda e, q0=q0, a_=a_: e.tensor_tensor(
                                    out=cat[q0:q0 + 64, 4 + i, :N], in0=psb[a_][q0:q0 + 64, :N], in1=rec[q0:q0 + 64, :],
                                    op=ALU.mult), reads=[pkey(a_), rkey], writes=[(catkey, 4 + i)])
                        return pair_epilogue(accA, accB, N, fin)
                    hb[-1]["epi"] = epi
                blocks.extend(hb)
        run_attn(blocks)

    def gated_residual(hdst, hkey, N, l, wg, col):
        def ev(i, ps, bk):
            ig = mvi(l, wg, i, col)
            S.op("dve", lambda e: e.scalar_tensor_tensor(
                out=hdst[:, i, :N], in0=ps, scalar=modv[:, ig:ig + 1], in1=hdst[:, i, :N], op0=ALU.mult, op1=ALU.add),
                reads=[bk, (hkey, i), ("modv", l)], writes=[(hkey, i)])
        return ev

    def ffn(hdst, hkey, N, l, col, ab, abkey):
        norm_mod(hdst, hkey, N, l, 3, 4, col, ab, abkey)
        wn = "wgu%d" % l
        for f0 in range(0, 22, 2):
            nf = min(2, 22 - f0)
            gview, gkey = wload(wn, 8, f0 * 128, nf * 128)
            uview, ukey = wload(wn, 8, DFF + f0 * 128, nf * 128)
            for ff in range(nf):
                fi = f0 + ff
                bg = mm_bank()
                for kc in range(8):
                    S.op("pe", lambda e, kc=kc, ff=ff, bg=bg, gview=gview: e.matmul(
                        psb[bg][:, :N], lhsT=gview[:, kc, ff * 128:(ff + 1) * 128], rhs=ab[:, kc, :N],
                        start=(kc == 0), stop=(kc == 7)), reads=[gkey, (abkey, kc)], writes=[pkey(bg)])
                bu = mm_bank()
                for kc in range(8):
                    S.op("pe", lambda e, kc=kc, ff=ff, bu=bu, uview=uview: e.matmul(
                        psb[bu][:, :N], lhsT=uview[:, kc, ff * 128:(ff + 1) * 128], rhs=ab[:, kc, :N],
                        start=(kc == 0), stop=(kc == 7)), reads=[ukey, (abkey, kc)], writes=[pkey(bu)])
                S.op("act", lambda e, fi=fi, bg=bg: e.activation(out=silt[:, fi % 2, :N], in_=psb[bg][:, :N],
                                                                 func=AF.Silu),
                     reads=[pkey(bg)], writes=[("silt", fi % 2)])
                S.op("dve", lambda e, fi=fi, bu=bu: e.tensor_tensor(out=actT[:, fi, :N], in0=psb[bu][:, :N],
                                                                   in1=silt[:, fi % 2, :N], op=ALU.mult),
                     reads=[pkey(bu), ("silt", fi % 2)], writes=[("actT", fi)])
        linear("wdn%d" % l, 22, lambda kc: actT[:, kc, :N], [("actT", kc) for kc in range(22)], N,
               [(c * 128, 128) for c in range(8)], gated_residual(hdst, hkey, N, l, 5, col), blockcols=128)

    def rope_pair(k, ps, bk, npp, N, dst_ap, dkey):
        ii = k // 2
        if k % 2 == 0:
            S.op("dve", lambda e: e.tensor_tensor(out=ropeT[0][:npp, :N], in0=ps, in1=ropeC[:npp, :N], op=ALU.mult),
                 reads=[bk, "ropeC"], writes=[("ropeT", 0)])
        else:
            S.op("dve", lambda e: e.tensor_tensor(out=ropeS[:npp, 1, :N], in0=ps, in1=ropeS[:npp, 0, :N], op=ALU.mult),
                 reads=[bk, "ropeS0"], writes=["ropeS1"])
            S.op("dve", lambda e: e.tensor_tensor(out=dst_ap, in0=ropeT[0][:npp, :N], in1=ropeS[:npp, 1, :N],
                                                  op=ALU.add),
                 reads=[("ropeT", 0), "ropeS1"], writes=[dkey])

    def l0_q_side(ab, abkey, N, tok0, rotate):
        def ev(i, ps, bk):
            if i < 2:
                S.op("dve", lambda e: e.tensor_copy(out=cq[:, i, :N], in_=ps), reads=[bk], writes=[("tmpf", i)])
            else:
                S.op("act", lambda e: e.activation(out=naq[:, i - 2, :N], in_=ps, func=AF.Copy),
                     reads=[bk], writes=[("naq", i - 2)])
        linear("w0q", 8, lambda kc: ab[:, kc, :N], [(abkey, kc) for kc in range(8)], N,
               [(c * 128, 128) for c in range(6)], ev)
        rms_stats(lambda kc: cq[:, kc, :N], [("tmpf", kc) for kc in range(2)], 2, N, 256)
        for kc in range(2):
            S.op("dve", lambda e, kc=kc: e.scalar_tensor_tensor(
                out=cqn[:, kc, :N], in0=cq[:, kc, :N], scalar=qnorm[:, kc:kc + 1], in1=rstd[:, :N],
                op0=ALU.mult, op1=ALU.mult), reads=[("tmpf", kc), "rstd", "qnorm"], writes=[("cqn", kc)])
        if rotate:
            mch = [(h * 192 + 96 * v, 96) for h in range(8) for v in range(2)]
            S.dma("sp", lambda e: e.dma_start(out=ropeC[:96, :N], in_=ropeMc[:, tok0:tok0 + N]), "ropeC",
                  writes=["ropeC"])
            S.dma("sp", lambda e: e.dma_start(out=ropeS[:96, 0, :N], in_=ropeMs[:, tok0:tok0 + N]), "ropeS",
                  writes=["ropeS0"])
        else:
            mch = [(h * 192, 96) for h in range(8)]

        def ev2(i, ps, bk):
            if not rotate:
                h = i
                if h % 2 == 0:
                    S.op("dve", lambda e: e.tensor_copy(out=Qh[:, h, :N], in_=ps), reads=[bk], writes=[("Qh", h)])
                else:
                    S.op("act", lambda e: e.activation(out=Qh[:, h, :N], in_=ps, func=AF.Copy),
                         reads=[bk], writes=[("Qh", h)])
                return
            h, v = i // 2, i % 2
            if v == 0:
                S.op("act", lambda e: e.activation(out=Qh[0:64, h, :N], in_=ps[0:64, :], func=AF.Copy),
                     reads=[bk], writes=[("Qh", h)])
                S.op("dve", lambda e: e.tensor_tensor(out=ropeT[0][64:96, :N], in0=ps[64:96, :], in1=ropeC[64:96, :N],
                                                      op=ALU.mult),
                     reads=[bk, "ropeC"], writes=[("ropeT", 0)])
            else:
                S.op("dve", lambda e: e.tensor_tensor(out=ropeS[64:96, 1, :N], in0=ps[64:96, :], in1=ropeS[64:96, 0, :N],
                                                      op=ALU.mult),
                     reads=[bk, "ropeS0"], writes=["ropeS1"])
                S.op("dve", lambda e: e.tensor_tensor(out=Qh[64:96, h, :N], in0=ropeT[0][64:96, :N],
                                                      in1=ropeS[64:96, 1, :N], op=ALU.add),
                     reads=[("ropeT", 0), "ropeS1"], writes=[("Qh", h)])
        linear("wqup", 2, lambda kc: cqn[:, kc, :N], [("cqn", kc) for kc in range(2)], N, mch, ev2, blockcols=1536)

    def l0_kv(ab, abkey, N, key0, tok0, rotate):
        if rotate:
            S.dma("sp", lambda e: e.dma_start(out=ropeC[:32, :N], in_=ropeMc[0:32, tok0:tok0 + N]), "ropeC",
                  writes=["ropeC"])
            S.dma("sp", lambda e: e.dma_start(out=ropeS[:32, 0, :N], in_=ropeMs[0:32, tok0:tok0 + N]), "ropeS",
                  writes=["ropeS0"])

        def ev(i, ps, bk):
            if i == 0:
                S.op("dve", lambda e: e.tensor_copy(out=ckvf[:, :N], in_=ps), reads=[bk], writes=[("recb", 0)])
            elif not rotate:
                S.op("act", lambda e: e.activation(out=pT[1][:32, :N], in_=ps, func=AF.Copy),
                     reads=[bk], writes=["pT1"])
            else:
                rope_pair(i - 1, ps, bk, 32, N, pT[1][:32, :N], "pT1")
        mch = [(0, 128), (128, 32)] + ([(160, 32)] if rotate else [])
        linear("w0kv", 8, lambda kc: ab[:, kc, :N], [(abkey, kc) for kc in range(8)], N, mch, ev)
        rms_stats(lambda kc: ckvf[:, :N], [("recb", 0)], 1, N, 128)
        S.op("dve", lambda e: e.scalar_tensor_tensor(out=pT[0][:, :N], in0=ckvf[:, :N], scalar=kvnorm[:, 0:1],
                                                     in1=rstd[:, :N], op0=ALU.mult, op1=ALU.mult),
             reads=[("recb", 0), "rstd", "kvnorm"], writes=["pT0"])
        S.dma("pool", lambda e: e.dma_start(out=kr_d[:, key0:key0 + N], in_=pT[1][:32, :N]),
              "kvo1", reads=["pT1"], writes=[("kr_d", key0)])
        ukv, ukkey = wload("wukP", 1, 0, 512)
        for i in range(4):
            bank = mm_bank()
            S.op("pe", lambda e, i=i, bank=bank: e.matmul(psb[bank][:, :N], lhsT=ukv[:, 0, i * 128:(i + 1) * 128],
                                                          rhs=pT[0][:, :N], start=True, stop=True),
                 reads=[ukkey, "pT0"], writes=[pkey(bank)])
            st_ = pT[2]
            sk = "pT2"
            if i % 2 == 0:
                S.op("dve", lambda e, st_=st_, bank=bank: e.tensor_copy(out=st_[:, :N], in_=psb[bank][:, :N]),
                     reads=[pkey(bank)], writes=[sk])
            else:
                S.op("act", lambda e, st_=st_, bank=bank: e.activation(out=st_[:, :N], in_=psb[bank][:, :N],
                                                                       func=AF.Copy),
                     reads=[pkey(bank)], writes=[sk])
            S.dma("pool", lambda e, i=i, st_=st_: e.dma_start(out=khp_d[i, :, key0:key0 + N], in_=st_[:, :N]),
                  "kvo0", reads=[sk], writes=[("khp_d", key0, i)])
        uvv, uvkey = wload("wuv", 1, 0, 512)
        for tb in range(N // 128):
            bank = mm_bank()
            S.op("pe", lambda e, tb=tb, bank=bank: e.matmul(psb[bank][:, :512], lhsT=pT[0][:, tb * 128:(tb + 1) * 128],
                                                            rhs=uvv[:, 0, :], start=True, stop=True),
                 reads=[uvkey, "pT0"], writes=[pkey(bank)])
            v_evac(psb[bank][:, :512], pkey(bank), vstage[:, :], "vstage")
            S.dma("pool", lambda e, tb=tb: e.dma_start(out=vh_d[key0 + tb * 128:key0 + (tb + 1) * 128, :], in_=vstage[:, :]),
                  "kvo2", reads=["vstage"], writes=[("vh_d", key0, tb)])

    def v_evac(ps, bk, dst2d, dkey):
        src = ps.rearrange("p (i t c) -> p i t c", t=2, c=64)
        dst = dst2d.rearrange("p (i u) -> p i u", u=192)
        S.op("dve", lambda e: e.tensor_copy(out=dst[:, :, 0:64], in_=src[:, :, 0, :]), reads=[bk], writes=[dkey])
        S.op("act", lambda e: e.activation(out=dst[:, :, 128:192], in_=src[:, :, 1, :], func=AF.Copy),
             reads=[bk], writes=[dkey])

    def l0_nakv(ab, abkey, N, tok0, to_ctx):
        if to_ctx:
            linear("w0nak", 8, lambda kc: ab[:, kc, :N], [(abkey, kc) for kc in range(8)], N,
                   [(c * 128, 128) for c in range(4)],
                   copy_evac(lambda i: nakc[:, i, :N], lambda i: ("nakc", i)))
            linear_tm("w0nav", 8, lambda kc, tb: ab[:, kc, tb * 128:(tb + 1) * 128],
                      [(abkey, kc) for kc in range(8)], N // 128, 128, 512,
                      lambda tb, ps, bk: v_evac(ps, bk, navc[:, tb, :], "navc"))
            return

        def evk(i, ps, bk):
            S.op("act" if i % 2 else "dve",
                 (lambda e: e.activation(out=pT[i % 2][:, :N], in_=ps, func=AF.Copy)) if i % 2 else
                 (lambda e: e.tensor_copy(out=pT[i % 2][:, :N], in_=ps)),
                 reads=[bk], writes=["pT%d" % (i % 2)])
            S.dma("pool", lambda e: e.dma_start(out=nakT_d[:, i, tok0:tok0 + N], in_=pT[i % 2][:, :N]),
                  "nako%d" % (i % 2), reads=["pT%d" % (i % 2)], writes=[("nakT_d", tok0, i)])
        linear("w0nak", 8, lambda kc: ab[:, kc, :N], [(abkey, kc) for kc in range(8)], N,
               [(c * 128, 128) for c in range(4)], evk)

        def evv(tb, ps, bk):
            v_evac(ps, bk, vstage[:, :], "vstage")
            S.dma("pool", lambda e: e.dma_start(out=nav_d[tok0 + tb * 128:tok0 + (tb + 1) * 128, :], in_=vstage[:, :]),
                  "navo", reads=["vstage"], writes=[("nav_d", tok0, tb)])
        linear_tm("w0nav", 8, lambda kc, tb: ab[:, kc, tb * 128:(tb + 1) * 128],
                  [(abkey, kc) for kc in range(8)], N // 128, 128, 512, evv)

    def v1_evac(ps, bk, dst2d, dkey):
        src = ps.rearrange("p (g c) -> p g c", c=64)
        dst = dst2d.rearrange("p (g u) -> p g u", u=192)
        S.op("dve", lambda e: e.tensor_copy(out=dst[:, :, 0:64], in_=src), reads=[bk], writes=[dkey])
        S.op("act", lambda e: e.activation(out=dst[:, :, 128:192], in_=src, func=AF.Copy), reads=[bk], writes=[dkey])

    def l0_finish(hdst, hkey, N, col, cat, catkey, ab2, ab2key):
        linear("wout0", 8, lambda kc: cat[:, kc, :N], [(catkey, kc) for kc in range(8)], N,
               [(c * 128, 128) for c in range(8)], gated_residual(hdst, hkey, N, 0, 2, col))
        ffn(hdst, hkey, N, 0, col, ab2, ab2key)

    def dump(name, src_fn, nchunk, N):
        if name not in dbg_out:
            return
        for kc in range(nchunk):
            src, keys = src_fn(kc)
            S.op("dve", lambda e, src=src: e.tensor_copy(out=ostage[0][:, :N], in_=src), reads=keys, writes=["xstage0"])
            S.dma("pool", lambda e, kc=kc: e.dma_start(out=dbg_out[name][kc * 128:(kc + 1) * 128, :], in_=ostage[0][:, :N]),
                  "dbg", reads=["xstage0"], out=True)

    def prologue_A():
        load_T(ctxl, 2, hctx, "hbuf1")
        norm_mod(hctx, "hbuf1", CTX, 0, 0, 1, 1, abuf[0], "abuf0")
        l0_kv(abuf[0], "abuf0", CTX, S_FULL, 0, False)
        l0_nakv(abuf[0], "abuf0", CTX, 0, True)
        l0_q_side(abuf[0], "abuf0", CTX, 0, False)
        mla_heads_tile(CTX, [(S_FULL, 2)], abuf[1], "abuf1")
        na_heads_tile(0, CTX, abuf[1], "abuf1", latent=False)
        l0_finish(hctx, "hbuf1", CTX, 1, abuf[1], "abuf1", abuf[0], "abuf0")
        dump("hctx1", lambda kc: (hctx[:, kc, :CTX], [("hbuf1", kc)]), 8, CTX)
        norm_mod(hctx, "hbuf1", CTX, 1, 0, 1, 1, abuf[0], "abuf0")
        linear("w1k", 8, lambda kc: abuf[0][:, kc, :CTX], [("abuf0", kc) for kc in range(8)], CTX,
               [(0, 128), (256, 128)], copy_evac(lambda i: k1c[:, i, :], lambda i: ("k1c", i)))
        linear_tm("w1v", 8, lambda kc, tb: abuf[0][:, kc, tb * 128:(tb + 1) * 128], [("abuf0", kc) for kc in range(8)],
                  2, 128, 128, lambda tb, ps, bk: v1_evac(ps, bk, v1c[:, tb, :], "v1c"))


    if stage < 5:
        return _finish()
    NPRO = S_FULL // T

    def pro_head(i):
        for k_ in CAST_AT.get(i, ()):
            cast_weight(k_)
        load_T_big(xl[i * T:(i + 1) * T, :], hbuf[i % 2], "hbuf%d" % (i % 2), hbuf[(i + 1) % 2], "hbuf%d" % ((i + 1) % 2))
        norm_mod(hbuf[i % 2], "hbuf%d" % (i % 2), T, 0, 0, 1, 0, abuf[i % 2], "abuf%d" % (i % 2))
        build_strip(i // 2, i % 2)

    pro_head(0)
    for i in range(NPRO):
        ab = abuf[i % 2]
        ak = "abuf%d" % (i % 2)
        if i + 1 < NPRO:
            pro_head(i + 1)
        l0_kv(ab, ak, T, i * T, i * T, True)
        if i < NT_NAKV:
            l0_nakv(ab, ak, T, i * T, False)

    def l1_kv(j, hb, hk, N=T):
        ab, ak = abuf[0], "abuf0"
        norm_mod(hb, hk, N, 1, 0, 1, 0, ab, ak)
        slot = j % 3
        S.dma("sp", lambda e: e.dma_start(out=ropeC[:, :N], in_=ropeSc[:, j * T:j * T + N]), "ropeC", writes=["ropeC"])
        S.dma("sp", lambda e: e.dma_start(out=ropeS[:, 0, :N], in_=ropeSs[:, j * T:j * T + N]), "ropeS",
              writes=["ropeS0"])

        def ev(i, ps, bk):
            rope_pair(i, ps, bk, 128, N, k1[:, i // 2, slot * T:slot * T + N], ("k1", i // 2, slot))
        linear("w1k", 8, lambda kc: ab[:, kc, :N], [(ak, kc) for kc in range(8)], N,
               [(c * 128, 128) for c in range(4)], ev)
        linear_tm("w1v", 8, lambda kc, tb: ab[:, kc, tb * 128:(tb + 1) * 128], [(ak, kc) for kc in range(8)],
                  N // 128, 128, 128, lambda tb, ps, bk: v1_evac(ps, bk, v1[:, slot * 4 + tb, :], ("v1", slot)))

    def l1_main(t, hb, hk):
        ab, ak = abuf[1], "abuf1"
        norm_mod(hb, hk, T, 1, 0, 1, 0, ab, ak)
        S.dma("sp", lambda e: e.dma_start(out=ropeC[:, :], in_=ropeSc[:, t * T:(t + 1) * T]), "ropeC", writes=["ropeC"])
        S.dma("sp", lambda e: e.dma_start(out=ropeS[:, 0, :], in_=ropeSs[:, t * T:(t + 1) * T]), "ropeS",
              writes=["ropeS0"])

        def evq(i, ps, bk):
            rope_pair(i, ps, bk, 128, T, q1[:, i // 2, :], ("q1", i // 2))
        linear("w1q", 8, lambda kc: ab[:, kc, :], [(ak, kc) for kc in range(8)], T,
               [(c * 128, 128) for c in range(16)], evq)
        cat, catkey = abuf[0], "abuf0"
        blocks = []
        for qb in range(4):
            gq = 4 * t + qb
            for g in range(2):
                for par in range(2):
                    p0 = 64 * par
                    acc = next_acc()
                    if par == 0:
                        accA = acc
                    vo = 192 * g + 64 * par
                    rhs = q1[p0:p0 + 64, 4 * g:4 * g + 4, qb * 128:(qb + 1) * 128]
                    rk = [("q1", 4 * g + c) for c in range(4)]
                    hbl = []
                    for cb in range(2):
                        qk = [(k1c[p0:p0 + 64, g, cb * 128:(cb + 1) * 128], rhs, [("k1c", g)] + rk, 0, 512)]
                        hbl.append(dict(np_=128, c0=0, c1=512, qk=qk, scale=0.125,
                                       v_lhsT=v1c[:, cb, vo:vo + 128], vkeys=["v1c"], acc=acc))
                    for dk in (-1, 0, 1):
                        kb = gq + dk
                        if kb < 0:
                            continue
                        slot = (kb // 4) % 3
                        kcol = slot * T + (kb % 4) * 128
                        qk = [(k1[p0:p0 + 64, g, kcol:kcol + 128], rhs, [("k1", g, slot)] + rk, 0, 512)]
                        if dk != 0:
                            mi = 0 if dk < 0 else 1
                            qk.append((ident_b[:, :], swam[:, mi, :], ["ident_b", "swam"], 0, 512))
                        hbl.append(dict(np_=128, c0=0, c1=512, qk=qk, scale=0.125,
                                       v_lhsT=v1[:, slot * 4 + (kb % 4), vo:vo + 128], vkeys=[("v1", slot)],
                                       acc=acc))
                    for bi, blk in enumerate(hbl):
                        blk["first"] = (bi == 0)
                        blk["last"] = (bi == len(hbl) - 1)

                    if par == 1:
                        def epi(g=g, qb=qb, accA=accA, accB=acc):
                            def fin(rec, rkey):
                                for (q0, a_) in ((0, accA), (64, accB)):
                                    S.op("dve", lambda e, q0=q0, a_=a_: e.tensor_tensor(
                                        out=cat[q0:q0 + 64, 4 * g:4 * g + 4, qb * 128:(qb + 1) * 128],
                                        in0=psb[a_][q0:q0 + 64, :].rearrange("p (h t) -> p h t", h=4),
                                        in1=rec[q0:q0 + 64, :].rearrange("p (h t) -> p h t", h=4), op=ALU.mult),
                                        reads=[pkey(a_), rkey], writes=[(catkey, 4 * g + c) for c in range(4)])
                            sink = ([8 * g + 2 * hh for hh in range(4)], [8 * g + 2 * hh + 1 for hh in range(4)])
                            return pair_epilogue(accA, accB, T, fin, sink_heads=sink)
                        hbl[-1]["epi"] = epi
                    blocks.extend(hbl)
        run_attn(blocks)
        linear("wout1", 8, lambda kc: cat[:, kc, :], [(catkey, kc) for kc in range(8)], T,
               [(c * 128, 128) for c in range(8)], gated_residual(hb, hk, T, 1, 2, 0))
        ffn(hb, hk, T, 1, 0, abuf[1], "abuf1")
        rms_stats(lambda kc: hb[:, kc, :], [(hk, kc) for kc in range(8)], 8, T, D)
        for kc in range(8):
            S.op("dve", lambda e, kc=kc: e.scalar_tensor_tensor(
                out=hb[:, kc, :], in0=hb[:, kc, :], scalar=fnorm[:, kc:kc + 1], in1=rstd[:, :],
                op0=ALU.mult, op1=ALU.mult), reads=[(hk, kc), "rstd", "fnorm"], writes=[(hk, kc)])
        for tb in range(4):
            os_ = ostage[tb % 2]
            ok = "xstage%d" % (tb % 2)
            for g2 in range(2):
                bank = mm_bank()
                for q in range(4):
                    kc = 4 * g2 + q
                    S.op("pe", lambda e, kc=kc, q=q, tb=tb, bank=bank: e.transpose(
                        out=psb[bank][:, q * 128:(q + 1) * 128], in_=hb[:, kc, tb * 128:(tb + 1) * 128],
                        identity=ident_f[:]), reads=[(hk, kc), "ident_f"], writes=[pkey(bank)])
                if g2 == 0:
                    S.op("dve", lambda e, os_=os_, bank=bank: e.tensor_copy(out=os_[:, 0:512], in_=psb[bank][:, :]),
                         reads=[pkey(bank)], writes=[ok])
                else:
                    S.op("act", lambda e, os_=os_, bank=bank: e.activation(out=os_[:, 512:1024], in_=psb[bank][:, :],
                                                                           func=AF.Copy),
                         reads=[pkey(bank)], writes=[ok])
            r0 = t * T + tb * 128
            S.dma("pool", lambda e, os_=os_, r0=r0: e.dma_start(out=y[r0:r0 + 128, :], in_=os_[:, :]),
                  "yout%d" % (tb % 2), reads=[ok], out=True)

    for k_ in ("w1q", "wout1", "wgu1", "wdn1"):
        cast_weight(k_)
    mod_big(1, 7)
    mod_finish(1, 7)
    prologue_A()
    if stage < 6:
        return _finish()
    for j in range(n_main):
        hb = hbuf[j % 2]
        hk = "hbuf%d" % (j % 2)
        NJ = T if j < 8 else 128
        load_T(xl[j * T:j * T + NJ, :], NJ // 128, hb, hk)
        norm_mod(hb, hk, NJ, 0, 0, 1, 0, abuf[0], "abuf0")
        l0_q_side(abuf[0], "abuf0", NJ, j * T, True)
        if stage == 61:
            return _finish()
        chunks = [(c * 1024, 8) for c in range(8)] + [(S_FULL, 2)]
        mla_heads_tile(NJ, chunks, abuf[1], "abuf1")
        if stage == 62:
            if j == 0:
                dump("h1t0", lambda kc: (abuf[1][:, kc, :], [("abuf1", kc)]), 4, T)
            return _finish()
        na_heads_tile(j, NJ, abuf[1], "abuf1", latent=True)
        if stage in (63, 631):
            if j == 0:
                dump("h1t0", lambda kc: (abuf[1][:, kc, :], [("abuf1", kc)]), 8, T)
            return _finish()
        l0_finish(hb, hk, NJ, 0, abuf[1], "abuf1", abuf[0], "abuf0")
        if j == 0:
            dump("h1t0", lambda kc: (hb[:, kc, :], [(hk, kc)]), 8, T)
        if stage == 64:
            return _finish()
        l1_kv(j, hb, hk, NJ)
        if j >= 1:
            l1_main(j - 1, hbuf[(j - 1) % 2], "hbuf%d" % ((j - 1) % 2))

    S.emit(st)
    st.close()
    ninst = {e: len(S.streams[e]) for e in ENGS}
    print("instructions per engine:", ninst)
    return nc


_CACHE = {}


def kernel(**inputs):
    inp = {k: np.asarray(v) for k, v in inputs.items()}
    w, sh = _prep_shared(inp)
    in_maps = []
    for cid in range(8):
        m = _prep_core(inp, cid, sh)
        m.update(w)
        m.update(sh)
        in_maps.append(m)
    if "nc" not in _CACHE:
        _CACHE["nc"] = build_program()
    res = run_bass_kernel_spmd(_CACHE["nc"], in_maps, core_ids=list(range(8)))
    out = np.zeros((4, S_FULL, D), np.float32)
    for cid in range(8):
        b, half = cid // 2, cid % 2
        yl = np.asarray(res.results[cid]["y"], dtype=np.float32)
        if half == 0:
            out[b, :OWN] = yl
        else:
            out[b, OWN:] = yl[::-1]
    return out
```
